# Optimizing a Trainium2 kernel written in Bass

```python
import jax, jax.numpy as jnp
from jax import lax
import numpy as np

D_MODEL = 1024
BATCH = 2
SEQ = 8192
DEPTH = 2

GRID_W = 64
NA_HEADS = 8
NA_HEAD_DIM = 64
NA_WIDTH = NA_HEADS * NA_HEAD_DIM
NA_ROWS = 8
NA_COLS = 16
Q_BLOCK_COLS = 16
KEY_BAND_COLS = Q_BLOCK_COLS + NA_COLS
N_COL_BLOCKS = GRID_W // Q_BLOCK_COLS
F_GROUPS = 4
F_GROUP_DIM = 128
F_WIDTH = F_GROUPS * F_GROUP_DIM
IN_COLS = 3 * NA_WIDTH + F_WIDTH + 2 * D_MODEL
D_FF = 2816
RMS_EPS = 1e-6

kernel_name = "hybrid_na_fnet_macaron_encoder"


def rms_norm(x, g):
    xf = x.astype(jnp.float32)
    y = xf * lax.rsqrt(jnp.mean(xf * xf, axis=-1, keepdims=True) + RMS_EPS)
    return (y * g.astype(jnp.float32)).astype(x.dtype)


def swiglu(x, w_in, w_out):
    gu = x @ w_in
    g, u = jnp.split(gu, 2, axis=-1)
    return (jax.nn.silu(g) * u) @ w_out


def neighbourhood_attention(q, k, v, rpb):
    B, T, H, Dh = q.shape
    rows = T // GRID_W
    kh = min(NA_ROWS, rows)
    r = jnp.arange(rows)
    row_start = jnp.clip(r - kh // 2, 0, rows - kh)
    row_idx = row_start[:, None] + jnp.arange(kh)
    j = jnp.arange(N_COL_BLOCKS)
    band_start = jnp.clip(j * Q_BLOCK_COLS - NA_COLS // 2, 0, GRID_W - KEY_BAND_COLS)
    key_col = band_start[:, None] + jnp.arange(KEY_BAND_COLS)
    q_col = j[:, None] * Q_BLOCK_COLS + jnp.arange(Q_BLOCK_COLS)
    win_start = jnp.clip(q_col - NA_COLS // 2, 0, GRID_W - NA_COLS)
    col_ok = (key_col[:, None, :] >= win_start[..., None]) & (
        key_col[:, None, :] < win_start[..., None] + NA_COLS)

    qg = q.reshape(B, rows, N_COL_BLOCKS, Q_BLOCK_COLS, H, Dh)
    kg = k.reshape(B, rows, GRID_W, H, Dh)
    vg = v.reshape(B, rows, GRID_W, H, Dh)
    gi_r = row_idx[:, None, :, None]
    gi_c = key_col[None, :, None, :]
    k_win = kg[:, gi_r, gi_c]
    v_win = vg[:, gi_r, gi_c]

    dr = row_idx - r[:, None] + (NA_ROWS - 1)
    dc = jnp.clip(key_col[:, None, :] - q_col[..., None], -(NA_COLS - 1), NA_COLS - 1) + (NA_COLS - 1)
    bias = rpb[:, dr[:, None, None, :, None], dc[None, :, :, None, :]]
    bias = jnp.transpose(bias, (1, 2, 0, 3, 4, 5)).astype(jnp.float32)

    scale = Dh ** -0.5
    s = jnp.einsum('brjqhd,brjuvhd->brjhquv', qg, k_win).astype(jnp.float32) * scale + bias[None]
    s = jnp.where(col_ok[None, None, :, None, :, None, :], s, -jnp.inf)
    sh = s.shape
    p = jax.nn.softmax(s.reshape(sh[:-2] + (kh * KEY_BAND_COLS,)), axis=-1).reshape(sh)
    o = jnp.einsum('brjhquv,brjuvhd->brjqhd', p.astype(v.dtype), v_win)
    return o.reshape(B, T, H * Dh)


def fourier_mix(u):
    B, T, _ = u.shape
    ug = u.astype(jnp.float32).reshape(B, T, F_GROUPS, F_GROUP_DIM)
    f = jnp.fft.fft2(ug, axes=(1, 3), norm="ortho").real
    return f.reshape(B, T, F_WIDTH).astype(u.dtype)


def mixer(h, w_in, gate_bias, rpb, w_na_out, w_f_out, w_o):
    B, T, _ = h.shape
    z = h @ w_in
    s1 = NA_WIDTH
    q = z[..., :s1]
    k = z[..., s1:2 * s1]
    v = z[..., 2 * s1:3 * s1]
    uf = z[..., 3 * s1:3 * s1 + F_WIDTH]
    ga = z[..., 3 * s1 + F_WIDTH:3 * s1 + F_WIDTH + D_MODEL]
    gf = z[..., 3 * s1 + F_WIDTH + D_MODEL:]
    shp = (B, T, NA_HEADS, NA_HEAD_DIM)
    y_na = neighbourhood_attention(q.reshape(shp), k.reshape(shp), v.reshape(shp), rpb) @ w_na_out
    y_f = fourier_mix(uf) @ w_f_out
    g_na = jax.nn.sigmoid(ga + gate_bias[0])
    g_f = jax.nn.sigmoid(gf + gate_bias[1])
    return (g_na * y_na + g_f * y_f) @ w_o


def setup_inputs(seed: int = 0) -> dict:
    key = jax.random.key(seed)
    ks = jax.random.split(key, 20)
    L, D, F = DEPTH, D_MODEL, D_FF

    def w(k, shape, fan_in, mult=1.0):
        return jax.random.normal(k, shape, jnp.float32) * (mult * fan_in ** -0.5)

    def gain(k, shape):
        return 1.0 + 0.05 * jax.random.normal(k, shape, jnp.float32)

    return {
        "x": jax.random.normal(ks[0], (BATCH, SEQ, D), jnp.float32),
        "ffn1_norm": gain(ks[1], (L, D)),
        "ffn1_w_in": w(ks[2], (L, D, 2 * F), D),
        "ffn1_w_out": w(ks[3], (L, F, D), F),
        "mix_norm": gain(ks[4], (L, D)),
        "mix_w_in": w(ks[5], (L, D, IN_COLS), D),
        "mix_gate_bias": 0.1 * jax.random.normal(ks[6], (L, 2, D), jnp.float32),
        "na_rpb": 0.5 * jax.random.normal(ks[7], (L, NA_HEADS, 2 * NA_ROWS - 1, 2 * NA_COLS - 1), jnp.float32),
        "na_w_out": w(ks[8], (L, NA_WIDTH, D), NA_WIDTH),
        "f_w_out": w(ks[9], (L, F_WIDTH, D), F_WIDTH),
        "mix_w_o": w(ks[10], (L, D, D), D),
        "ffn2_norm": gain(ks[11], (L, D)),
        "ffn2_w_in": w(ks[12], (L, D, 2 * F), D),
        "ffn2_w_out": w(ks[13], (L, F, D), F),
        "final_norm": gain(ks[14], (D,)),
    }


def reference(x, ffn1_norm, ffn1_w_in, ffn1_w_out, mix_norm, mix_w_in, mix_gate_bias,
              na_rpb, na_w_out, f_w_out, mix_w_o, ffn2_norm, ffn2_w_in, ffn2_w_out,
              final_norm):
    for l in range(DEPTH):
        x = x + 0.5 * swiglu(rms_norm(x, ffn1_norm[l]), ffn1_w_in[l], ffn1_w_out[l])
        x = x + mixer(rms_norm(x, mix_norm[l]), mix_w_in[l], mix_gate_bias[l], na_rpb[l],
                      na_w_out[l], f_w_out[l], mix_w_o[l])
        x = x + 0.5 * swiglu(rms_norm(x, ffn2_norm[l]), ffn2_w_in[l], ffn2_w_out[l])
    return rms_norm(x, final_norm)
```

```python
import numpy as np
from contextlib import ExitStack
import concourse.bass as bass
import concourse.mybir as mybir
from concourse.bass_utils import run_bass_kernel_spmd

F32 = mybir.dt.float32
BF16 = mybir.dt.bfloat16
AF = mybir.ActivationFunctionType
ALU = mybir.AluOpType

D = 1024
NTOK = 2048
DFF = 2816
NJ = 22
JG = 11
EPS = 1e-6
L = 2


class Eng:
    def __init__(self, name):
        self.name = name
        self.ops = []
        self.cnt = 0
        self.sem = None
        self.seen = {}
        self.last_sig = True


class DSem:
    def __init__(self, sem):
        self.sem = sem
        self.cnt = 0


class Prog:
    def __init__(self, nc):
        self.nc = nc
        self.stack = ExitStack()
        self.eng = {n: Eng(n) for n in ("pe", "act", "dve", "pool", "sp")}
        for n, e in self.eng.items():
            e.sem = self.stack.enter_context(nc.semaphore("s_" + n))
        self.nds = 0

    def sb(self, name, shape, dt):
        return self.stack.enter_context(self.nc.sbuf_tensor("sb_" + name, list(shape), dt))

    def ps(self, name, shape, dt=F32):
        return self.stack.enter_context(self.nc.psum_tensor("ps_" + name, list(shape), dt))

    def dsem(self):
        self.nds += 1
        return DSem(self.stack.enter_context(self.nc.semaphore("d%d" % self.nds)))

    def _waits(self, e, waits):
        for tok in waits:
            if tok is None:
                continue
            sem, val = tok
            if e.seen.get(id(sem), 0) >= val:
                continue
            e.seen[id(sem)] = val
            e.ops.append(("wait", sem, val))

    def op(self, eng, fn, waits=(), sig=True):
        e = self.eng[eng]
        self._waits(e, waits)
        e.last_sig = sig
        if sig:
            e.cnt += 1
            e.ops.append(("op", fn, e.sem, 1))
            return (e.sem, e.cnt)
        e.ops.append(("op", fn, None, 0))
        return None

    def dma(self, queue, out, in_, dsem, waits=()):
        e = self.eng[queue]
        self._waits(e, waits)
        dsem.cnt += 16
        e.ops.append(("op", lambda g: g.dma_start(out=out, in_=in_), dsem.sem, 16))
        return (dsem.sem, dsem.cnt)

    def wait(self, eng, waits):
        self._waits(self.eng[eng], waits)

    def barrier(self, extra=()):
        toks = list(extra)
        for n in ("pe", "act", "dve", "pool"):
            e = self.eng[n]
            if e.cnt:
                assert e.last_sig, "engine %s: last op before barrier must signal" % n
                toks.append((e.sem, e.cnt))
        for n in ("pe", "act", "dve", "pool", "sp"):
            self._waits(self.eng[n], toks)

    def emit(self):
        def replay(e, g):
            for it in e.ops:
                if it[0] == "wait":
                    g.wait_ge(it[1], it[2])
                else:
                    ins = it[1](g)
                    if it[2] is not None:
                        ins.then_inc(it[2], it[3])

        with self.nc.Block() as block:
            @block.tensor
            def _(g):
                replay(self.eng["pe"], g)

            @block.scalar
            def _(g):
                replay(self.eng["act"], g)

            @block.vector
            def _(g):
                replay(self.eng["dve"], g)

            @block.gpsimd
            def _(g):
                replay(self.eng["pool"], g)

            @block.sync
            def _(g):
                replay(self.eng["sp"], g)
        self.stack.close()


ARENA = 51500


class Arena:
    def __init__(self, P, n=ARENA):
        self.t = P.sb("arena", [128, n], BF16)
        self.n = n
        self.off = 0

    def reset(self):
        self.off = 0

    def take(self, shape, dt):
        n_el = 1
        for k in shape[1:]:
            n_el *= k
        units = n_el * (2 if dt == F32 else 1)
        units = (units + 15) // 16 * 16
        assert self.off + units <= self.n, ("arena overflow", self.off, units, self.n)
        v = self.t[0:shape[0], self.off:self.off + units]
        self.off += units
        if dt == F32:
            v = v.bitcast(F32)
        v = v[:, 0:n_el]
        if len(shape) == 3:
            v = v.rearrange("p (a b) -> p a b", a=shape[1])
        elif len(shape) == 4:
            v = v.rearrange("p (a b c) -> p a b c", a=shape[1], b=shape[2])
        return v


class Ctx:
    def __init__(self, P):
        self.P = P
        self.xT = P.sb("xT", [128, 8, NTOK], F32)
        self.hT = P.sb("hT", [128, 8, NTOK], BF16)
        self.psum = P.ps("psum", [128, 4096], F32)
        self.ones = P.sb("ones", [128, 128], BF16)
        self.gains = P.sb("gains", [128, 7, 8], F32)
        self.arena = Arena(P)
        self.sq = None
        self.rstd = [P.sb("rstd%d" % i, [128, 512], F32) for i in range(2)]
        self.bank_free = [None] * 8
        self.x_tok = None
        self.h_tok = None
        self.h_free = None
        self.sq_free = None
        self.rstd_free = [None, None]
        self.nnorm = 0
        self.fin_sem0 = P.dsem()
        self.fin_sem1 = P.dsem()
        self.tok_ones = P.op("pool", lambda g: g.memset(self.ones[:], 1.0))

    def bank(self, b, n=1):
        return self.psum[:, b * 512:(b + n) * 512]


def emit_norm(C, gi, x_ready, out_fn=None):
    P = C.P
    toks = []
    C.sq = C.arena.take([128, 8, 512], BF16)
    C.sq_free = None
    for t in range(4):
        ts = slice(t * 512, (t + 1) * 512)
        r = C.rstd[C.nnorm % 2]
        k = C.nnorm % 2
        C.nnorm += 1
        tsq = P.op("act", lambda g, ts=ts: g.activation(out=C.sq, in_=C.xT[:, :, ts], func=AF.Square),
                   waits=[x_ready, C.sq_free])
        b = 7 if (t % 2) else 3
        for c in range(8):
            tk = P.op("pe", lambda g, c=c, b=b: g.matmul(C.bank(b), lhsT=C.ones[:], rhs=C.sq[:, c, :],
                                                         start=(c == 0), stop=(c == 7)),
                      waits=[tsq, C.tok_ones, C.bank_free[b]] if c == 0 else (), sig=(c == 7))
        C.sq_free = tk
        t1 = P.op("dve", lambda g, r=r, b=b: g.tensor_scalar(out=r[:], in0=C.bank(b), scalar1=1.0 / D, scalar2=EPS,
                                                             op0=ALU.mult, op1=ALU.add),
                  waits=[tk, C.rstd_free[k]])
        C.bank_free[b] = t1
        t1b = P.op("act", lambda g, r=r: g.activation(out=r[:], in_=r[:], func=AF.Sqrt), waits=[t1])
        t2 = P.op("dve", lambda g, r=r: g.reciprocal(out=r[:], in_=r[:]), waits=[t1b])
        last = None
        for c in range(8):
            if out_fn is None:
                last = P.op("dve", lambda g, c=c, ts=ts, r=r: g.scalar_tensor_tensor(
                    out=C.hT[:, c, ts], in0=C.xT[:, c, ts], scalar=C.gains[:, gi, c:c + 1], in1=r[:],
                    op0=ALU.mult, op1=ALU.mult), waits=[t2, C.h_free] if c == 0 else (), sig=(c == 7))
            else:
                last = out_fn(c, t, ts, r, t2)
        C.rstd_free[k] = last
        toks.append(last)
    return toks[-1]


class FFNBufs:
    def __init__(self, P):
        self.wi_sem = [P.dsem() for _ in range(4)]
        self.wo_sem = [P.dsem() for _ in range(3)]
        self.nwi = 0
        self.nwo = 0
        self.nunit = 0

    def carve(self, A):
        self.aT = A.take([128, JG, NTOK], BF16)
        self.wi = [A.take([128, 8, 2, 128], BF16) for i in range(4)]
        self.wo = [A.take([128, JG, 256], BF16) for i in range(3)]
        self.sg = [A.take([128, 1024], F32) for i in range(2)]
        self.wi_free = [None] * 4
        self.wo_free = [None] * 3
        self.sg_free = [None] * 2
        self.aT_free = None


def emit_ffn(C, B, w_in, w_out, h_ready):
    P = C.P
    x_tok = None
    for grp in range(2):
        a_tok = None
        unit_tok = []
        for jl in range(JG):
            j = grp * JG + jl
            slot = B.nwi % 4
            B.nwi += 1
            src_g = w_in[:, j * 128:(j + 1) * 128].rearrange("(c p) f -> p c f", p=128)
            src_u = w_in[:, DFF + j * 128:DFF + (j + 1) * 128].rearrange("(c p) f -> p c f", p=128)
            P.dma("pool", B.wi[slot][:, :, 0, :], src_g, B.wi_sem[slot], waits=[B.wi_free[slot]])
            wtok = P.dma("pool", B.wi[slot][:, :, 1, :], src_u, B.wi_sem[slot])
            for half in range(2):
                pb = 4 * (B.nunit % 2)
                sgi = B.nunit % 2
                B.nunit += 1
                last = None
                for which in range(2):
                    for d in range(8):
                        for t2 in range(2):
                            bk = pb + which * 2 + t2
                            first = (which == 0 and d == 0 and t2 == 0)
                            fin = (which == 1 and d == 7 and t2 == 1)
                            tsl = slice(half * 1024 + t2 * 512, half * 1024 + (t2 + 1) * 512)
                            w = [wtok, h_ready, C.bank_free[pb], C.bank_free[pb + 1], C.bank_free[pb + 2],
                                 C.bank_free[pb + 3]] if first else ()
                            last = P.op("pe", lambda g, bk=bk, slot=slot, d=d, which=which, tsl=tsl: g.matmul(
                                C.bank(bk), lhsT=B.wi[slot][:, d, which, :], rhs=C.hT[:, d, tsl],
                                start=(d == 0), stop=(d == 7)), waits=w, sig=fin)
                if half == 1:
                    B.wi_free[slot] = last
                C.h_free = last
                ts = P.op("act", lambda g, pb=pb, sgi=sgi: g.activation(out=B.sg[sgi][:], in_=C.bank(pb, 2), func=AF.Silu),
                          waits=[last, B.sg_free[sgi]])
                hs = slice(half * 1024, (half + 1) * 1024)
                tm = P.op("dve", lambda g, pb=pb, sgi=sgi, jl=jl, hs=hs: g.tensor_tensor(
                    out=B.aT[:, jl, hs], in0=B.sg[sgi][:], in1=C.bank(pb + 2, 2), op=ALU.mult),
                    waits=[ts, last, B.aT_free])
                B.sg_free[sgi] = tm
                for k in range(4):
                    C.bank_free[pb + k] = tm
                a_tok = tm
        for dp in range(4):
            slot = B.nwo % 3
            B.nwo += 1
            src = w_out[grp * JG * 128:(grp + 1) * JG * 128, dp * 256:(dp + 1) * 256].rearrange("(j p) d -> p j d", p=128)
            wtok = P.dma("pool", B.wo[slot][:], src, B.wo_sem[slot], waits=[B.wo_free[slot]])
            for ds_ in range(2):
                dch = dp * 2 + ds_
                pb = 4 * (B.nunit % 2)
                B.nunit += 1
                last = None
                for jl in range(JG):
                    for t in range(4):
                        first = (jl == 0 and t == 0)
                        fin = (jl == JG - 1 and t == 3)
                        w = [wtok, a_tok, C.bank_free[pb], C.bank_free[pb + 1], C.bank_free[pb + 2],
                             C.bank_free[pb + 3]] if first else ()
                        last = P.op("pe", lambda g, pb=pb, t=t, slot=slot, jl=jl, ds_=ds_: g.matmul(
                            C.bank(pb + t), lhsT=B.wo[slot][:, jl, ds_ * 128:(ds_ + 1) * 128],
                            rhs=B.aT[:, jl, t * 512:(t + 1) * 512], start=(jl == 0), stop=(jl == JG - 1)),
                            waits=w, sig=fin)
                if ds_ == 1:
                    B.wo_free[slot] = last
                te = P.op("dve", lambda g, pb=pb, dch=dch: g.scalar_tensor_tensor(
                    out=C.xT[:, dch, :], in0=C.bank(pb, 4), scalar=0.5, in1=C.xT[:, dch, :],
                    op0=ALU.mult, op1=ALU.add), waits=[last])
                for k in range(4):
                    C.bank_free[pb + k] = te
                x_tok = te
                B.aT_free = last
    return x_tok


class InprojBufs:
    def __init__(self, P):
        self.w_sem = P.dsem()
        self.st_sem = [P.dsem() for _ in range(2)]
        self.nst = 0
        self.nunit = 0

    def carve(self, A):
        self.w = A.take([128, 8, 2048], BF16)
        self.st = [A.take([128, 2048], BF16) for i in range(2)]
        self.st_free = [None, None]


def emit_inproj(C, B, w_in, h_ready, qT_d, kT_d, v_d, uf_d, w_free=None):
    P = C.P
    wt = None
    for q in range(4):
        wt = P.dma("pool", B.w[:, :, q * 512:(q + 1) * 512],
                   w_in[:, q * 512:(q + 1) * 512].rearrange("(c p) f -> p c f", p=128), B.w_sem, waits=[w_free])
    out_toks = []
    for m in range(8):
        pb = 4 * (B.nunit % 2)
        B.nunit += 1
        last = None
        for d in range(8):
            for t in range(4):
                first = (d == 0 and t == 0)
                w = [wt, h_ready] + [C.bank_free[pb + k] for k in range(4)] if first else ()
                last = P.op("pe", lambda g, pb=pb, t=t, d=d, m=m: g.matmul(
                    C.bank(pb + t), lhsT=B.w[:, d, m * 128:(m + 1) * 128], rhs=C.hT[:, d, t * 512:(t + 1) * 512],
                    start=(d == 0), stop=(d == 7)), waits=w, sig=(d == 7 and t == 3))
        si = B.nst % 2
        B.nst += 1
        if m < 4:
            te = P.op("dve", lambda g, pb=pb, si=si: g.tensor_scalar(out=B.st[si], in0=C.bank(pb, 4), scalar1=0.125, scalar2=None, op0=ALU.mult),
                      waits=[last, B.st_free[si]])
        else:
            te = P.op("act", lambda g, pb=pb, si=si: g.activation(out=B.st[si], in_=C.bank(pb, 4), func=AF.Copy),
                      waits=[last, B.st_free[si]])
        for k in range(4):
            C.bank_free[pb + k] = te
        dst = (qT_d if m < 4 else kT_d)[(m % 4) * 128:(m % 4 + 1) * 128, :]
        B.st_free[si] = P.dma("sp", dst, B.st[si][:], B.st_sem[si], waits=[te])
        out_toks.append(B.st_free[si])
    for which in range(2):
        for tq in range(4):
            pb = 4 * (B.nunit % 2)
            B.nunit += 1
            last = None
            for ti in range(4):
                tt = tq * 4 + ti
                for d in range(8):
                    first = (d == 0 and ti == 0)
                    w = [wt, h_ready] + [C.bank_free[pb + k] for k in range(4)] if first else ()
                    last = P.op("pe", lambda g, pb=pb, ti=ti, d=d, tt=tt, which=which: g.matmul(
                        C.bank(pb + ti), lhsT=C.hT[:, d, tt * 128:(tt + 1) * 128],
                        rhs=B.w[:, d, 1024 + which * 512:1024 + (which + 1) * 512],
                        start=(d == 0), stop=(d == 7)), waits=w, sig=(d == 7 and ti == 3))
            si = B.nst % 2
            B.nst += 1
            if tq % 2 == 0:
                te = P.op("act", lambda g, pb=pb, si=si: g.activation(out=B.st[si][:], in_=C.bank(pb, 4), func=AF.Copy),
                          waits=[last, B.st_free[si]])
            else:
                te = P.op("dve", lambda g, pb=pb, si=si: g.tensor_copy(out=B.st[si][:], in_=C.bank(pb, 4)),
                          waits=[last, B.st_free[si]])
            for k in range(4):
                C.bank_free[pb + k] = te
            dst = (v_d if which == 0 else uf_d)[tq * 512:(tq + 1) * 512, :].rearrange("(i p) f -> p i f", p=128)
            B.st_free[si] = P.dma("sp", dst, B.st[si][:].rearrange("p (i f) -> p i f", i=4), B.st_sem[si], waits=[te])
            out_toks.append(B.st_free[si])
            C.h_free = last
    return out_toks


class NABufs:
    def __init__(self, P):
        self.ld = P.dsem()
        self.b_sem = [P.dsem() for _ in range(2)]
        self.o_sem = P.dsem()

    def carve(self, A):
        self.qT = A.take([128, 4, NTOK], BF16)
        self.kT = A.take([128, 4, 2560], BF16)
        self.v = A.take([128, 20, 512], BF16)
        self.bias = [A.take([128, 5, 640], F32) for i in range(2)]
        self.tmp = [A.take([128, 640], F32) for i in range(2)]
        self.pT = [A.take([128, 640], BF16) for i in range(2)]
        self.rec = [A.take([64, 128], F32) for i in range(2)]
        self.ost = A.take([64, NTOK], BF16)
        self.ones = A.take([128, 64], BF16)


def emit_na(C, B, qT_d, kTe_d, ve_d, bias_d, oT_d, in_ready=None):
    P = C.P
    t_ones = P.op("pool", lambda g: g.memset(B.ones[:], 1.0))
    for m in range(4):
        P.dma("sp", B.qT[:, m, :], qT_d[m * 128:(m + 1) * 128, :], B.ld, waits=[in_ready])
        P.dma("sp", B.kT[:, m, :], kTe_d[m * 128:(m + 1) * 128, :], B.ld)
    ld = P.dma("sp", B.v[:], ve_d.rearrange("(i p) f -> p i f", p=128), B.ld)
    b_free = [None, None]
    tmp_free = [None, None]
    pT_free = [None, None]
    rec_free = [None, None]
    o_free = [None]
    unit = 0
    out_toks = []
    for h in range(8):
        m, po = h // 2, (h % 2) * 64
        bs = h % 2
        btok = P.dma("sp", B.bias[bs][:].rearrange("p a b -> p (a b)"), bias_d[h], B.b_sem[bs], waits=[b_free[bs]])
        os_ = 0
        lastw = None
        for ml in range(16):
            typ = {0: 0, 1: 1, 14: 3, 15: 4}.get(ml, 2)
            u2 = unit % 2
            unit += 1
            sb_ = 2 * u2
            ob = 4 + u2
            lastS = None
            for kc in range(5):
                w = [ld, C.bank_free[sb_], C.bank_free[sb_ + 1]] if kc == 0 else ()
                lastS = P.op("pe", lambda g, sb_=sb_, kc=kc, m=m, po=po, ml=ml: g.matmul(
                    C.psum[:, sb_ * 512 + kc * 128: sb_ * 512 + (kc + 1) * 128],
                    lhsT=B.kT[po:po + 64, m, (ml + kc) * 128:(ml + kc + 1) * 128],
                    rhs=B.qT[po:po + 64, m, ml * 128:(ml + 1) * 128], start=True, stop=True),
                    waits=w, sig=(kc == 4))
            tt = P.op("dve", lambda g, sb_=sb_, u2=u2, bs=bs, typ=typ: g.tensor_tensor(
                out=B.tmp[u2][:], in0=C.psum[:, sb_ * 512: sb_ * 512 + 640], in1=B.bias[bs][:, typ, :], op=ALU.add),
                waits=[lastS, btok, tmp_free[u2]])
            C.bank_free[sb_] = tt
            C.bank_free[sb_ + 1] = tt
            te = P.op("act", lambda g, u2=u2: g.activation(out=B.pT[u2][:], in_=B.tmp[u2][:], func=AF.Exp),
                      waits=[tt, pT_free[u2]])
            tmp_free[u2] = te
            lastO = None
            for kc in range(5):
                w = [te, t_ones, C.bank_free[ob]] if kc == 0 else ()
                P.op("pe", lambda g, ob=ob, kc=kc, ml=ml, h=h, u2=u2: g.matmul(
                    C.psum[0:64, ob * 512: ob * 512 + 128], lhsT=B.v[:, ml + kc, h * 64:(h + 1) * 64],
                    rhs=B.pT[u2][:, kc * 128:(kc + 1) * 128], start=(kc == 0), stop=(kc == 4)), waits=w, sig=False)
            for kc in range(5):
                lastO = P.op("pe", lambda g, ob=ob, kc=kc, u2=u2: g.matmul(
                    C.psum[0:64, ob * 512 + 128: ob * 512 + 256], lhsT=B.ones[:],
                    rhs=B.pT[u2][:, kc * 128:(kc + 1) * 128], start=(kc == 0), stop=(kc == 4)), sig=(kc == 4))
            pT_free[u2] = lastO
            tr = P.op("dve", lambda g, ob=ob, u2=u2: g.reciprocal(out=B.rec[u2][:], in_=C.psum[0:64, ob * 512 + 128: ob * 512 + 256]),
                      waits=[lastO, rec_free[u2]])
            lastw = P.op("dve", lambda g, ob=ob, u2=u2, os_=os_, ml=ml: g.tensor_tensor(
                out=B.ost[:, ml * 128:(ml + 1) * 128], in0=C.psum[0:64, ob * 512: ob * 512 + 128],
                in1=B.rec[u2][:], op=ALU.mult), waits=[tr, o_free[os_]] if ml == 0 else [tr])
            rec_free[u2] = lastw
            C.bank_free[ob] = lastw
            b_free[bs] = tt
        o_free[os_] = P.dma("sp", oT_d[h * 64:(h + 1) * 64, :], B.ost, B.o_sem, waits=[lastw])
        out_toks.append(o_free[os_])
    return out_toks


class FNBufs:
    def __init__(self, P):
        self.ld = P.dsem()
        self.e_sem = [P.dsem() for _ in range(2)]
        self.st = P.dsem()

    def carve(self, A):
        self.u = A.take([128, 64, 128], BF16)
        self.E = [A.take([128, 8, 256], BF16) for i in range(2)]
        self.Bsb = A.take([128, 256, 64], BF16)
        self.R = A.take([128, 512], BF16)
        self.CS = A.take([128, 128], BF16)
        self.G = [A.take([128, 4, 256], BF16) for i in range(2)]
        self.YT = A.take([128, 64, 128], BF16)


def emit_fnet(C, B, u_d, E_d, R_d, CS_d, YT_d, in_ready=None):
    P = C.P
    P.dma("sp", B.u[:].rearrange("p a b -> p (a b)"), u_d, B.ld, waits=[in_ready])
    P.dma("sp", B.R[:], R_d, B.ld)
    ld = P.dma("sp", B.CS[:], CS_d, B.ld)
    e_free = [None, None]
    unit = 0
    s1_act = s1_dve = None
    for ec in range(8):
        es = ec % 2
        etok = P.dma("sp", B.E[es][:].rearrange("p a b -> p (a b)"), E_d[:, ec * 2048:(ec + 1) * 2048], B.e_sem[es],
                     waits=[e_free[es]])
        for half in range(2):
            pb = 2 * (unit % 4)
            unit += 1
            last = None
            for ci in range(4):
                cl = half * 4 + ci
                c = ec * 8 + cl
                w = [ld, etok, C.bank_free[pb], C.bank_free[pb + 1]] if ci == 0 else ()
                last = P.op("pe", lambda g, pb=pb, ci=ci, c=c, cl=cl, es=es: g.matmul(
                    C.psum[:, pb * 512 + ci * 256: pb * 512 + (ci + 1) * 256], lhsT=B.u[:, c, :], rhs=B.E[es][:, cl, :],
                    start=True, stop=True), waits=w, sig=(ci == 3))
            c0 = ec * 8 + half * 4
            if unit % 2 == 0:
                te = P.op("act", lambda g, pb=pb, c0=c0: g.activation(
                    out=B.Bsb[:, :, c0:c0 + 4].rearrange("p n c -> p c n"),
                    in_=C.bank(pb, 2).rearrange("p (c n) -> p c n", c=4), func=AF.Copy), waits=[last])
            else:
                te = P.op("dve", lambda g, pb=pb, c0=c0: g.tensor_copy(
                    out=B.Bsb[:, :, c0:c0 + 4].rearrange("p n c -> p c n"),
                    in_=C.bank(pb, 2).rearrange("p (c n) -> p c n", c=4)), waits=[last])
            C.bank_free[pb] = te
            C.bank_free[pb + 1] = te
            if unit % 2 == 0:
                s1_act = te
            else:
                s1_dve = te
        e_free[es] = last
    g_free = [None, None]
    y_tok = None
    import os
    nst = int(os.environ.get("FN_STAGES", "3"))
    for kq in range(16 if nst >= 2 else 0):
        gs = kq % 2
        pb = 2 * (unit % 4)
        unit += 1
        last = None
        for ki in range(4):
            kp = kq * 4 + ki
            for part in range(2):
                w = [s1_act, s1_dve, C.bank_free[pb], C.bank_free[pb + 1]] if (ki == 0 and part == 0) else ()
                lhs = B.Bsb[:, part * 128 + 2 * kp: part * 128 + 2 * kp + 2, :].rearrange("p k c -> p (k c)")
                last = P.op("pe", lambda g, pb=pb, ki=ki, part=part, lhs=lhs: g.matmul(
                    C.psum[:, pb * 512 + ki * 256: pb * 512 + (ki + 1) * 256], lhsT=lhs,
                    rhs=B.R[:, part * 256:(part + 1) * 256], start=(part == 0), stop=(part == 1)),
                    waits=w, sig=(ki == 3 and part == 1))
        if kq % 2 == 0:
            te = P.op("act", lambda g, pb=pb, gs=gs: g.activation(out=B.G[gs][:].rearrange("p a b -> p (a b)"), in_=C.bank(pb, 2), func=AF.Copy),
                      waits=[last, g_free[gs]])
        else:
            te = P.op("dve", lambda g, pb=pb, gs=gs: g.tensor_copy(out=B.G[gs][:].rearrange("p a b -> p (a b)"), in_=C.bank(pb, 2)),
                      waits=[last, g_free[gs]])
        C.bank_free[pb] = te
        C.bank_free[pb + 1] = te
        if nst < 3:
            y_tok = te
            continue
        yb = 2 * (unit % 4)
        unit += 1
        last3 = None
        for ki in range(4):
            for k2 in range(2):
                po = k2 * 64
                for part in range(2):
                    w = [te, C.bank_free[yb], C.bank_free[yb + 1]] if (ki == 0 and k2 == 0 and part == 0) else ()
                    last3 = P.op("pe", lambda g, yb=yb, gs=gs, ki=ki, k2=k2, po=po, part=part: g.matmul(
                        C.psum[:, (yb + k2) * 512 + ki * 64: (yb + k2) * 512 + (ki + 1) * 64],
                        lhsT=B.G[gs][po:po + 64, ki, part * 128:(part + 1) * 128],
                        rhs=B.CS[po:po + 64, part * 64:(part + 1) * 64], start=(part == 0), stop=(part == 1)),
                        waits=w, sig=(ki == 3 and k2 == 1 and part == 1))
        g_free[gs] = last3
        kr0 = kq * 8
        ty = P.op("dve", lambda g, yb=yb, kr0=kr0: g.tensor_copy(
            out=B.YT[:, :, kr0:kr0 + 8].rearrange("p x (a k) -> p x a k", k=2),
            in_=C.bank(yb, 2).rearrange("p (k a x) -> p k a x", k=2, a=8)[:, :, 0:4, :].rearrange("p k a x -> p x a k")),
            waits=[last3])
        C.bank_free[yb] = ty
        C.bank_free[yb + 1] = ty
        y_tok = ty
    out = P.dma("sp", YT_d, B.YT[:].rearrange("p a b -> p (a b)"), B.st, waits=[y_tok, s1_act, s1_dve])
    return [out]


class OutBufs:
    def __init__(self, P):
        self.ld = P.dsem()
        self.wld = P.dsem()
        self.in_sem = [P.dsem() for _ in range(2)]

    def carve(self, A):
        self.oT = [A.take([128, 4, 512], BF16) for i in range(2)]
        self.yfT = [A.take([128, 4, 512], BF16) for i in range(2)]
        self.wna = A.take([128, 4, D], BF16)
        self.wf = A.take([128, 4, D], BF16)
        self.wo = A.take([128, 8, D], BF16)
        self.wg = A.take([128, 8, 2 * D], BF16)
        self.gb = A.take([128, 2, 8], F32)
        self.mT = A.take([128, 8, 512], BF16)
        self.sga = [A.take([128, 512], F32) for i in range(2)]
        self.sgf = [A.take([128, 512], F32) for i in range(2)]


def emit_outproj(C, B, oT_d, yfT_d, w_in, w_na, w_f, w_o, gb_d, in_ready=None):
    P = C.P
    ld = P.dma("sp", B.gb.rearrange("p a b -> p (a b)"), gb_d, B.ld, waits=[in_ready])
    P.dma("pool", B.wna, w_na.rearrange("(m p) d -> p m d", p=128), B.wld)
    P.dma("pool", B.wf, w_f.rearrange("(g p) d -> p g d", p=128), B.wld)
    wl = None
    for q in range(4):
        wl = P.dma("pool", B.wg[:, :, q * 512:(q + 1) * 512],
                   w_in[:, 2048 + q * 512:2048 + (q + 1) * 512].rearrange("(c p) f -> p c f", p=128), B.wld)
    wl = P.dma("pool", B.wo, w_o.rearrange("(c p) d -> p c d", p=128), B.wld)
    in_free = [None, None]
    s_free = [None, None]
    m_free = None
    x_tok = None
    unit = 0

    def load_in(t):
        k = t % 2
        tsl = slice(t * 512, (t + 1) * 512)
        P.dma("sp", B.oT[k], oT_d[:, tsl].rearrange("(m p) t -> p m t", p=128), B.in_sem[k], waits=[in_ready, in_free[k]])
        return P.dma("sp", B.yfT[k], yfT_d[:, tsl].rearrange("(g p) t -> p g t", p=128), B.in_sem[k])

    in_tok = {0: load_in(0)}
    for t in range(4):
        ts = slice(t * 512, (t + 1) * 512)
        k = t % 2
        if t + 1 < 4:
            in_tok[t + 1] = load_in(t + 1)
        d3 = None
        for dch in range(8):
            pb = 4 * (unit % 2)
            k2 = unit % 2
            unit += 1
            dsl = slice(dch * 128, (dch + 1) * 128)
            w0 = [ld, wl, in_tok[t]] + [C.bank_free[pb + q] for q in range(4)]
            for mm in range(4):
                P.op("pe", lambda g, pb=pb, mm=mm, dsl=dsl, k=k: g.matmul(
                    C.bank(pb), lhsT=B.wna[:, mm, dsl], rhs=B.oT[k][:, mm, :], start=(mm == 0), stop=(mm == 3)),
                    waits=w0 if mm == 0 else (), sig=False)
            for d in range(8):
                tB = P.op("pe", lambda g, pb=pb, d=d, dsl=dsl, ts=ts: g.matmul(
                    C.bank(pb + 1), lhsT=B.wg[:, d, dsl], rhs=C.hT[:, d, ts], start=(d == 0), stop=(d == 7)), sig=(d == 7))
            for gg in range(4):
                P.op("pe", lambda g, pb=pb, gg=gg, dsl=dsl, k=k: g.matmul(
                    C.bank(pb + 2), lhsT=B.wf[:, gg, dsl], rhs=B.yfT[k][:, gg, :], start=(gg == 0), stop=(gg == 3)), sig=False)
            for d in range(8):
                tD = P.op("pe", lambda g, pb=pb, d=d, dch=dch, ts=ts: g.matmul(
                    C.bank(pb + 3), lhsT=B.wg[:, d, D + dch * 128:D + (dch + 1) * 128], rhs=C.hT[:, d, ts],
                    start=(d == 0), stop=(d == 7)), sig=(d == 7))
            sa = P.op("act", lambda g, pb=pb, k2=k2, dch=dch: g.activation(
                out=B.sga[k2], in_=C.bank(pb + 1), func=AF.Sigmoid, bias=B.gb[:, 0, dch:dch + 1], scale=1.0),
                waits=[tB, s_free[k2]])
            sf = P.op("act", lambda g, pb=pb, k2=k2, dch=dch: g.activation(
                out=B.sgf[k2], in_=C.bank(pb + 3), func=AF.Sigmoid, bias=B.gb[:, 1, dch:dch + 1], scale=1.0),
                waits=[tD])
            d1 = P.op("dve", lambda g, pb=pb, k2=k2: g.tensor_tensor(out=B.sga[k2], in0=C.bank(pb), in1=B.sga[k2], op=ALU.mult),
                      waits=[sa, tD])
            d2 = P.op("dve", lambda g, pb=pb, k2=k2: g.tensor_tensor(out=B.sgf[k2], in0=C.bank(pb + 2), in1=B.sgf[k2], op=ALU.mult),
                      waits=[sf, d1])
            for q in range(4):
                C.bank_free[pb + q] = d2
            d3 = P.op("pool", lambda g, k2=k2, dch=dch: g.tensor_tensor(out=B.mT[:, dch, :], in0=B.sga[k2], in1=B.sgf[k2], op=ALU.add),
                      waits=[d2, m_free] if dch == 0 else [d2])
            s_free[k2] = d3
        in_free[k] = tD
        C.h_free = tD
        for dq in range(2):
            pb = 4 * (unit % 2)
            unit += 1
            last = None
            for di in range(4):
                do = dq * 4 + di
                for dch in range(8):
                    w = [d3] + [C.bank_free[pb + q] for q in range(4)] if (di == 0 and dch == 0) else ()
                    last = P.op("pe", lambda g, pb=pb, di=di, dch=dch, do=do: g.matmul(
                        C.bank(pb + di), lhsT=B.wo[:, dch, do * 128:(do + 1) * 128], rhs=B.mT[:, dch, :],
                        start=(dch == 0), stop=(dch == 7)), waits=w, sig=(di == 3 and dch == 7))
            for di in range(4):
                do = dq * 4 + di
                x_tok = P.op("dve", lambda g, pb=pb, di=di, do=do, ts=ts: g.tensor_tensor(
                    out=C.xT[:, do, ts], in0=C.bank(pb + di), in1=C.xT[:, do, ts], op=ALU.add),
                    waits=[last])
                C.bank_free[pb + di] = x_tok
            m_free = last
    return x_tok


def emit_final_norm(C, gi, out_d):
    P = C.P
    A = C.arena
    st = [A.take([128, 8, 512], F32) for _ in range(2)]
    sem = [C.fin_sem0, C.fin_sem1]
    free = [None, None]
    toks = []

    def out_fn(c, t, ts, r, t2):
        k = t % 2
        tk = P.op("dve", lambda g, c=c, ts=ts, r=r, k=k: g.scalar_tensor_tensor(
            out=st[k][:, c, :], in0=C.xT[:, c, ts], scalar=C.gains[:, gi, c:c + 1], in1=r[:],
            op0=ALU.mult, op1=ALU.mult), waits=[t2, free[k]] if c == 0 else (), sig=(c == 7))
        if c == 7:
            free[k] = P.dma("sp", out_d[:, ts].rearrange("(c p) t -> p c t", p=128), st[k], sem[k], waits=[tk])
            toks.append(free[k])
        return tk

    emit_norm(C, gi, None, out_fn=out_fn)
    return toks


def _load_xh(P, C, x_d, h_d, g_d):
    ld = P.dsem()
    for c in range(8):
        P.dma("sp", C.xT[:, c, :], x_d[c * 128:(c + 1) * 128, :], ld)
    if h_d is not None:
        P.dma("sp", C.hT[:], h_d.rearrange("(c p) t -> p c t", p=128), ld)
    return P.dma("sp", C.gains[:].rearrange("p a b -> p (a b)"), g_d, ld)


def _save_xh(P, C, x_d, h_d, waits):
    st = P.dsem()
    tok = None
    for c in range(8):
        tok = P.dma("sp", x_d[c * 128:(c + 1) * 128, :], C.xT[:, c, :], st, waits=waits)
    tok = P.dma("sp", h_d.rearrange("(c p) t -> p c t", p=128), C.hT[:], st)
    return tok


def _dram(nc, name, shape, dt, kind):
    return nc.dram_tensor(name, list(shape), dt, kind=kind).ap()


def _stage_front(nc, P, C, gi0, with_ffn, pre):
    toks = []
    if with_ffn:
        w1_in = _dram(nc, "w1_in", [D, 2 * DFF], F32, "ExternalInput")
        w1_out = _dram(nc, "w1_out", [DFF, D], F32, "ExternalInput")
        P.barrier(pre)
        C.arena.reset()
        h = emit_norm(C, gi0, None)
        C.ffn.carve(C.arena)
        emit_ffn(C, C.ffn, w1_in, w1_out, h)
        pre = ()
    wmix = _dram(nc, "wmixb", [D, 4096], F32, "ExternalInput")
    qT = _dram(nc, "qT", [512, NTOK], BF16, "ExternalOutput")
    kT = _dram(nc, "kT", [512, NTOK], BF16, "ExternalOutput")
    v = _dram(nc, "v", [NTOK, 512], BF16, "ExternalOutput")
    uf = _dram(nc, "uf", [NTOK, 512], BF16, "ExternalOutput")
    xo = _dram(nc, "xT_out", [D, NTOK], F32, "ExternalOutput")
    ho = _dram(nc, "hT_out", [D, NTOK], BF16, "ExternalOutput")
    P.barrier(pre)
    C.arena.reset()
    h = emit_norm(C, gi0 + 1, None)
    C.inp.carve(C.arena)
    toks += emit_inproj(C, C.inp, wmix, h, qT, kT, v, uf)
    P.barrier(toks)
    toks = [_save_xh(P, C, xo, ho, ())]
    return toks


def build_A():
    nc = bass.Bass("TRN2", target_bir_lowering=False)
    x_d = _dram(nc, "xT_in", [D, NTOK], F32, "ExternalInput")
    g_d = _dram(nc, "gains", [128, 56], F32, "ExternalInput")
    P = Prog(nc)
    C = Ctx(P)
    C.ffn = FFNBufs(P)
    C.inp = InprojBufs(P)
    ld = _load_xh(P, C, x_d, None, g_d)
    toks = _stage_front(nc, P, C, 0, True, [ld])
    P.wait("sp", toks)
    P.emit()
    return nc


def build_NF():
    nc = bass.Bass("TRN2", target_bir_lowering=False)
    qT = _dram(nc, "qT", [512, NTOK], BF16, "ExternalInput")
    kTe = _dram(nc, "kTe", [512, 2560], BF16, "ExternalInput")
    ve = _dram(nc, "ve", [2560, 512], BF16, "ExternalInput")
    bias = _dram(nc, "bias", [8, 128, 3200], F32, "ExternalInput")
    u = _dram(nc, "u", [128, 8192], BF16, "ExternalInput")
    E = _dram(nc, "E", [128, 16384], BF16, "ExternalInput")
    R = _dram(nc, "R", [128, 512], BF16, "ExternalInput")
    CS = _dram(nc, "CS", [128, 128], BF16, "ExternalInput")
    oT = _dram(nc, "oT", [512, NTOK], BF16, "ExternalOutput")
    YT = _dram(nc, "YT", [128, 8192], BF16, "ExternalOutput")
    P = Prog(nc)
    C = Ctx(P)
    na = NABufs(P)
    fn = FNBufs(P)
    import os
    which = os.environ.get("NF_ONLY", "both")
    toks = []
    if which in ("both", "na"):
        C.arena.reset()
        na.carve(C.arena)
        toks = emit_na(C, na, qT, kTe, ve, bias, oT)
    if which in ("both", "fn"):
        P.barrier(toks)
        C.arena.reset()
        fn.carve(C.arena)
        toks = emit_fnet(C, fn, u, E, R, CS, YT)
    P.wait("sp", toks)
    P.emit()
    return nc


def build_E(final, gi0):
    nc = bass.Bass("TRN2", target_bir_lowering=False)
    x_d = _dram(nc, "xT_in", [D, NTOK], F32, "ExternalInput")
    h_d = _dram(nc, "hT_in", [D, NTOK], BF16, "ExternalInput")
    g_d = _dram(nc, "gains", [128, 56], F32, "ExternalInput")
    oT = _dram(nc, "oT", [512, NTOK], BF16, "ExternalInput")
    yfT = _dram(nc, "yfT", [512, NTOK], BF16, "ExternalInput")
    wmixa = _dram(nc, "wmixa", [D, 4096], F32, "ExternalInput")
    w_na = _dram(nc, "w_na", [512, D], F32, "ExternalInput")
    w_f = _dram(nc, "w_f", [512, D], F32, "ExternalInput")
    w_o = _dram(nc, "w_o", [D, D], F32, "ExternalInput")
    gb = _dram(nc, "gb", [128, 16], F32, "ExternalInput")
    w2_in = _dram(nc, "w2_in", [D, 2 * DFF], F32, "ExternalInput")
    w2_out = _dram(nc, "w2_out", [DFF, D], F32, "ExternalInput")
    P = Prog(nc)
    C = Ctx(P)
    C.ffn = FFNBufs(P)
    C.inp = InprojBufs(P)
    ob = OutBufs(P)
    ld = _load_xh(P, C, x_d, h_d, g_d)
    P.barrier([ld])
    C.arena.reset()
    ob.carve(C.arena)
    emit_outproj(C, ob, oT, yfT, wmixa, w_na, w_f, w_o, gb)
    P.barrier()
    C.arena.reset()
    h = emit_norm(C, gi0 + 2, None)
    C.ffn.carve(C.arena)
    emit_ffn(C, C.ffn, w2_in, w2_out, h)
    if final:
        out = _dram(nc, "outT", [D, NTOK], F32, "ExternalOutput")
        P.barrier()
        C.arena.reset()
        toks = emit_final_norm(C, 6, out)
    else:
        toks = _stage_front(nc, P, C, gi0 + 3, True, ())
    P.wait("sp", toks)
    P.emit()
    return nc


import ml_dtypes
BF = ml_dtypes.bfloat16
NEG = -30000.0


def _tables():
    r = np.arange(128, dtype=np.int64)[:, None, None]
    c = np.arange(64, dtype=np.int64)[None, :, None]
    kr = np.arange(128, dtype=np.int64)[None, None, :]
    th = 2 * np.pi * ((kr * (64 * r + c)) % 8192) / 8192.0
    E = np.concatenate([np.cos(th), -np.sin(th)], axis=2) * 2.0 ** -10
    ch = np.arange(128, dtype=np.int64)
    th2 = 2 * np.pi * ((ch[:, None] * ch[None, :]) % 128) / 128.0
    Cm, Sm = np.cos(th2), np.sin(th2)
    R = np.concatenate([Cm, -Sm, Sm, Cm], axis=1)
    cc = np.arange(64, dtype=np.int64)
    th3 = 2 * np.pi * ((cc[:, None] * cc[None, :]) % 64) / 64.0
    CS64 = np.concatenate([np.cos(th3), np.sin(th3)], axis=1)
    CS = np.concatenate([CS64, CS64], axis=0)
    return (np.ascontiguousarray(E.reshape(128, 64 * 256)).astype(BF), R.astype(BF), CS.astype(BF))


def _ext_chunk(jj, e):
    gc = jj * 16 + e - 2
    if gc < 0:
        return 3 if e == 0 else None
    if gc > 63:
        return 60 if e == 19 else None
    return gc


def _bias_tiles(rpb, jj):
    out = np.full((8, 128, 5, 640), NEG, np.float32)
    qi = np.arange(2)[None, None, :, None]
    qc = np.arange(64)[None, None, None, :]
    ki = np.arange(2)[:, None, None, None]
    kcl = np.arange(64)[None, :, None, None]
    ws = np.clip(qc - 8, 0, 48)
    col_ok = (kcl >= ws) & (kcl < ws + 16)
    dc = np.clip(kcl - qc, -15, 15) + 15
    for typ, ml in ((0, 0), (1, 1), (2, 5), (3, 14), (4, 15)):
        m = jj * 16 + ml
        r = 2 * m + qi
        rs = np.clip(r - 4, 0, 120)
        for kc in range(5):
            gc = _ext_chunk(jj, ml + kc)
            if gc is None:
                continue
            krow = 2 * gc + ki
            ok = (krow >= rs) & (krow < rs + 8) & col_ok
            dr = np.clip(krow - r + 7, 0, 14)
            drb = np.broadcast_to(dr, ok.shape)
            dcb = np.broadcast_to(dc, ok.shape)
            vals = rpb[:, drb, dcb]
            tile = np.where(ok[None], vals, np.float32(NEG)).reshape(8, 128, 128)
            out[:, :, typ, kc * 128:(kc + 1) * 128] = tile
    return np.ascontiguousarray(out.reshape(8, 128, 3200))


def _gain_layout(g):
    n = g.shape[0]
    return np.ascontiguousarray(g.reshape(n, 8, 128).transpose(2, 0, 1).reshape(128, n * 8)).astype(np.float32)


def _run(nc, in_maps):
    res = run_bass_kernel_spmd(nc, in_maps, core_ids=list(range(8)))
    return res.results


def _exchange_front(outs, rpb_l, tabs):
    E, R, CS = tabs
    ims = []
    for core in range(8):
        b, jj = divmod(core, 4)
        kT_all = np.concatenate([outs[b * 4 + q]["kT"] for q in range(4)], axis=1)
        v_all = np.concatenate([outs[b * 4 + q]["v"] for q in range(4)], axis=0)
        uf_all = np.concatenate([outs[b * 4 + q]["uf"] for q in range(4)], axis=0)
        kTe = np.zeros((512, 2560), BF)
        ve = np.zeros((2560, 512), BF)
        for e in range(20):
            gc = _ext_chunk(jj, e)
            if gc is None:
                continue
            kTe[:, e * 128:(e + 1) * 128] = kT_all[:, gc * 128:(gc + 1) * 128]
            ve[e * 128:(e + 1) * 128, :] = v_all[gc * 128:(gc + 1) * 128, :]
        g = jj
        u = np.ascontiguousarray(uf_all[:, g * 128:(g + 1) * 128]).reshape(128, 64 * 128)
        ims.append({"qT": outs[core]["qT"], "kTe": kTe, "ve": ve, "bias": _bias_tiles(rpb_l, jj),
                    "u": u, "E": E, "R": R, "CS": CS})
    return ims


def kernel(x, ffn1_norm, ffn1_w_in, ffn1_w_out, mix_norm, mix_w_in, mix_gate_bias, na_rpb, na_w_out,
           f_w_out, mix_w_o, ffn2_norm, ffn2_w_in, ffn2_w_out, final_norm):
    f32 = lambda a: np.ascontiguousarray(np.asarray(a, dtype=np.float32))
    x = f32(x)
    gains = _gain_layout(np.stack([f32(ffn1_norm)[0], f32(mix_norm)[0], f32(ffn2_norm)[0],
                                   f32(ffn1_norm)[1], f32(mix_norm)[1], f32(ffn2_norm)[1], f32(final_norm)]))
    tabs = _tables()
    rpb = f32(na_rpb)
    ims = []
    for core in range(8):
        b, jj = divmod(core, 4)
        ims.append({"xT_in": np.ascontiguousarray(x[b, jj * NTOK:(jj + 1) * NTOK, :].T), "gains": gains,
                    "w1_in": f32(ffn1_w_in[0]), "w1_out": f32(ffn1_w_out[0]), "wmixb": f32(mix_w_in[0])})
    front = _run(build_A(), ims)
    out = None
    for l in range(L):
        nf = _run(build_NF(), _exchange_front(front, rpb[l], tabs))
        ims = []
        gb = _gain_layout(f32(mix_gate_bias[l]))
        for core in range(8):
            b, jj = divmod(core, 4)
            yfT = np.concatenate([nf[b * 4 + g]["YT"][:, jj * NTOK:(jj + 1) * NTOK] for g in range(4)], axis=0)
            im = {"xT_in": front[core]["xT_out"], "hT_in": front[core]["hT_out"], "gains": gains,
                  "oT": nf[core]["oT"], "yfT": np.ascontiguousarray(yfT), "wmixa": f32(mix_w_in[l]),
                  "w_na": f32(na_w_out[l]), "w_f": f32(f_w_out[l]), "w_o": f32(mix_w_o[l]), "gb": gb,
                  "w2_in": f32(ffn2_w_in[l]), "w2_out": f32(ffn2_w_out[l])}
            if l + 1 < L:
                im.update({"w1_in": f32(ffn1_w_in[l + 1]), "w1_out": f32(ffn1_w_out[l + 1]),
                           "wmixb": f32(mix_w_in[l + 1])})
            ims.append(im)
        res = _run(build_E(final=(l + 1 == L), gi0=3 * l), ims)
        if l + 1 < L:
            front = res
        else:
            out = np.empty((2, 8192, D), np.float32)
            for core in range(8):
                b, jj = divmod(core, 4)
                out[b, jj * NTOK:(jj + 1) * NTOK, :] = res[core]["outT"].T
    return out
```

```python
import numpy as np
from contextlib import ExitStack
import concourse.bass as bass
import concourse.mybir as mybir
from concourse.bass_utils import run_bass_kernel_spmd

F32 = mybir.dt.float32
BF16 = mybir.dt.bfloat16
AF = mybir.ActivationFunctionType
ALU = mybir.AluOpType

D = 1024
NTOK = 2048
DFF = 2816
NJ = 22
JG = 11
EPS = 1e-6
L = 2


class Eng:
    def __init__(self, name):
        self.name = name
        self.ops = []
        self.cnt = 0
        self.sem = None
        self.seen = {}
        self.last_sig = True


class DSem:
    def __init__(self, sem):
        self.sem = sem
        self.cnt = 0


class Prog:
    def __init__(self, nc):
        self.nc = nc
        self.stack = ExitStack()
        self.eng = {n: Eng(n) for n in ("pe", "act", "dve", "pool", "sp")}
        for n, e in self.eng.items():
            e.sem = self.stack.enter_context(nc.semaphore("s_" + n))
        self.nds = 0

    def sb(self, name, shape, dt):
        return self.stack.enter_context(self.nc.sbuf_tensor("sb_" + name, list(shape), dt))

    def ps(self, name, shape, dt=F32):
        return self.stack.enter_context(self.nc.psum_tensor("ps_" + name, list(shape), dt))

    def dsem(self):
        self.nds += 1
        return DSem(self.stack.enter_context(self.nc.semaphore("d%d" % self.nds)))

    def _waits(self, e, waits):
        for tok in waits:
            if tok is None:
                continue
            sem, val = tok
            if e.seen.get(id(sem), 0) >= val:
                continue
            e.seen[id(sem)] = val
            e.ops.append(("wait", sem, val))

    def op(self, eng, fn, waits=(), sig=True):
        e = self.eng[eng]
        self._waits(e, waits)
        e.last_sig = sig
        if sig:
            e.cnt += 1
            e.ops.append(("op", fn, e.sem, 1))
            return (e.sem, e.cnt)
        e.ops.append(("op", fn, None, 0))
        return None

    def dma(self, queue, out, in_, dsem, waits=()):
        e = self.eng[queue]
        self._waits(e, waits)
        dsem.cnt += 16
        e.ops.append(("op", lambda g: g.dma_start(out=out, in_=in_), dsem.sem, 16))
        return (dsem.sem, dsem.cnt)

    def wait(self, eng, waits):
        self._waits(self.eng[eng], waits)

    def barrier(self, extra=()):
        toks = list(extra)
        for n in ("pe", "act", "dve", "pool"):
            e = self.eng[n]
            if e.cnt:
                assert e.last_sig, "engine %s: last op before barrier must signal" % n
                toks.append((e.sem, e.cnt))
        for n in ("pe", "act", "dve", "pool", "sp"):
            self._waits(self.eng[n], toks)

    def emit(self):
        def replay(e, g):
            for it in e.ops:
                if it[0] == "wait":
                    g.wait_ge(it[1], it[2])
                else:
                    ins = it[1](g)
                    if it[2] is not None:
                        ins.then_inc(it[2], it[3])

        with self.nc.Block() as block:
            @block.tensor
            def _(g):
                replay(self.eng["pe"], g)

            @block.scalar
            def _(g):
                replay(self.eng["act"], g)

            @block.vector
            def _(g):
                replay(self.eng["dve"], g)

            @block.gpsimd
            def _(g):
                replay(self.eng["pool"], g)

            @block.sync
            def _(g):
                replay(self.eng["sp"], g)
        self.stack.close()


ARENA = 51500


class Arena:
    def __init__(self, P, n=ARENA):
        self.t = P.sb("arena", [128, n], BF16)
        self.n = n
        self.off = 0

    def reset(self):
        self.off = 0

    def take(self, shape, dt):
        n_el = 1
        for k in shape[1:]:
            n_el *= k
        units = n_el * (2 if dt == F32 else 1)
        units = (units + 15) // 16 * 16
        assert self.off + units <= self.n, ("arena overflow", self.off, units, self.n)
        v = self.t[0:shape[0], self.off:self.off + units]
        self.off += units
        if dt == F32:
            v = v.bitcast(F32)
        v = v[:, 0:n_el]
        if len(shape) == 3:
            v = v.rearrange("p (a b) -> p a b", a=shape[1])
        elif len(shape) == 4:
            v = v.rearrange("p (a b c) -> p a b c", a=shape[1], b=shape[2])
        return v


class Ctx:
    def __init__(self, P):
        self.P = P
        self.xT = P.sb("xT", [128, 8, NTOK], F32)
        self.hT = P.sb("hT", [128, 8, NTOK], BF16)
        self.psum = P.ps("psum", [128, 4096], F32)
        self.ones = P.sb("ones", [128, 128], BF16)
        self.gains = P.sb("gains", [128, 7, 8], F32)
        self.arena = Arena(P)
        self.sq = None
        self.rstd = [P.sb("rstd%d" % i, [128, 512], F32) for i in range(2)]
        self.bank_free = [None] * 8
        self.x_tok = None
        self.h_tok = None
        self.h_free = None
        self.sq_free = None
        self.rstd_free = [None, None]
        self.nnorm = 0
        self.fin_sem0 = P.dsem()
        self.fin_sem1 = P.dsem()
        self.tok_ones = P.op("pool", lambda g: g.memset(self.ones[:], 1.0))

    def bank(self, b, n=1):
        return self.psum[:, b * 512:(b + n) * 512]


def emit_norm(C, gi, x_ready, out_fn=None):
    P = C.P
    toks = []
    C.sq = C.arena.take([128, 8, 512], BF16)
    C.sq_free = None
    for t in range(4):
        ts = slice(t * 512, (t + 1) * 512)
        r = C.rstd[C.nnorm % 2]
        k = C.nnorm % 2
        C.nnorm += 1
        tsq = P.op("act", lambda g, ts=ts: g.activation(out=C.sq, in_=C.xT[:, :, ts], func=AF.Square),
                   waits=[x_ready, C.sq_free])
        b = 7 if (t % 2) else 3
        for c in range(8):
            tk = P.op("pe", lambda g, c=c, b=b: g.matmul(C.bank(b), lhsT=C.ones[:], rhs=C.sq[:, c, :],
                                                         start=(c == 0), stop=(c == 7)),
                      waits=[tsq, C.tok_ones, C.bank_free[b]] if c == 0 else (), sig=(c == 7))
        C.sq_free = tk
        t1 = P.op("dve", lambda g, r=r, b=b: g.tensor_scalar(out=r[:], in0=C.bank(b), scalar1=1.0 / D, scalar2=EPS,
                                                             op0=ALU.mult, op1=ALU.add),
                  waits=[tk, C.rstd_free[k]])
        C.bank_free[b] = t1
        t1b = P.op("act", lambda g, r=r: g.activation(out=r[:], in_=r[:], func=AF.Sqrt), waits=[t1])
        t2 = P.op("dve", lambda g, r=r: g.reciprocal(out=r[:], in_=r[:]), waits=[t1b])
        last = None
        for c in range(8):
            if out_fn is None:
                last = P.op("dve", lambda g, c=c, ts=ts, r=r: g.scalar_tensor_tensor(
                    out=C.hT[:, c, ts], in0=C.xT[:, c, ts], scalar=C.gains[:, gi, c:c + 1], in1=r[:],
                    op0=ALU.mult, op1=ALU.mult), waits=[t2, C.h_free] if c == 0 else (), sig=(c == 7))
            else:
                last = out_fn(c, t, ts, r, t2)
        C.rstd_free[k] = last
        toks.append(last)
    return toks[-1]


class FFNBufs:
    def __init__(self, P):
        self.wi_sem = [P.dsem() for _ in range(4)]
        self.wo_sem = [P.dsem() for _ in range(3)]
        self.nwi = 0
        self.nwo = 0
        self.nunit = 0

    def carve(self, A):
        self.aT = A.take([128, JG, NTOK], BF16)
        self.wi = [A.take([128, 8, 2, 128], BF16) for i in range(4)]
        self.wo = [A.take([128, JG, 256], BF16) for i in range(3)]
        self.sg = [A.take([128, 1024], F32) for i in range(2)]
        self.wi_free = [None] * 4
        self.wo_free = [None] * 3
        self.sg_free = [None] * 2
        self.aT_free = None


def emit_ffn(C, B, w_in, w_out, h_ready):
    P = C.P
    x_tok = None
    for grp in range(2):
        a_tok = None
        unit_tok = []
        for jl in range(JG):
            j = grp * JG + jl
            slot = B.nwi % 4
            B.nwi += 1
            src_g = w_in[:, j * 128:(j + 1) * 128].rearrange("(c p) f -> p c f", p=128)
            src_u = w_in[:, DFF + j * 128:DFF + (j + 1) * 128].rearrange("(c p) f -> p c f", p=128)
            P.dma("pool", B.wi[slot][:, :, 0, :], src_g, B.wi_sem[slot], waits=[B.wi_free[slot]])
            wtok = P.dma("pool", B.wi[slot][:, :, 1, :], src_u, B.wi_sem[slot])
            for half in range(2):
                pb = 4 * (B.nunit % 2)
                sgi = B.nunit % 2
                B.nunit += 1
                last = None
                for which in range(2):
                    for d in range(8):
                        for t2 in range(2):
                            bk = pb + which * 2 + t2
                            first = (which == 0 and d == 0 and t2 == 0)
                            fin = (which == 1 and d == 7 and t2 == 1)
                            tsl = slice(half * 1024 + t2 * 512, half * 1024 + (t2 + 1) * 512)
                            w = [wtok, h_ready, C.bank_free[pb], C.bank_free[pb + 1], C.bank_free[pb + 2],
                                 C.bank_free[pb + 3]] if first else ()
                            last = P.op("pe", lambda g, bk=bk, slot=slot, d=d, which=which, tsl=tsl: g.matmul(
                                C.bank(bk), lhsT=B.wi[slot][:, d, which, :], rhs=C.hT[:, d, tsl],
                                start=(d == 0), stop=(d == 7)), waits=w, sig=fin)
                if half == 1:
                    B.wi_free[slot] = last
                C.h_free = last
                ts = P.op("act", lambda g, pb=pb, sgi=sgi: g.activation(out=B.sg[sgi][:], in_=C.bank(pb, 2), func=AF.Silu),
                          waits=[last, B.sg_free[sgi]])
                hs = slice(half * 1024, (half + 1) * 1024)
                tm = P.op("dve", lambda g, pb=pb, sgi=sgi, jl=jl, hs=hs: g.tensor_tensor(
                    out=B.aT[:, jl, hs], in0=B.sg[sgi][:], in1=C.bank(pb + 2, 2), op=ALU.mult),
                    waits=[ts, last, B.aT_free])
                B.sg_free[sgi] = tm
                for k in range(4):
                    C.bank_free[pb + k] = tm
                a_tok = tm
        for dp in range(4):
            slot = B.nwo % 3
            B.nwo += 1
            src = w_out[grp * JG * 128:(grp + 1) * JG * 128, dp * 256:(dp + 1) * 256].rearrange("(j p) d -> p j d", p=128)
            wtok = P.dma("pool", B.wo[slot][:], src, B.wo_sem[slot], waits=[B.wo_free[slot]])
            for ds_ in range(2):
                dch = dp * 2 + ds_
                pb = 4 * (B.nunit % 2)
                B.nunit += 1
                last = None
                for jl in range(JG):
                    for t in range(4):
                        first = (jl == 0 and t == 0)
                        fin = (jl == JG - 1 and t == 3)
                        w = [wtok, a_tok, C.bank_free[pb], C.bank_free[pb + 1], C.bank_free[pb + 2],
                             C.bank_free[pb + 3]] if first else ()
                        last = P.op("pe", lambda g, pb=pb, t=t, slot=slot, jl=jl, ds_=ds_: g.matmul(
                            C.bank(pb + t), lhsT=B.wo[slot][:, jl, ds_ * 128:(ds_ + 1) * 128],
                            rhs=B.aT[:, jl, t * 512:(t + 1) * 512], start=(jl == 0), stop=(jl == JG - 1)),
                            waits=w, sig=fin)
                if ds_ == 1:
                    B.wo_free[slot] = last
                te = P.op("dve", lambda g, pb=pb, dch=dch: g.scalar_tensor_tensor(
                    out=C.xT[:, dch, :], in0=C.bank(pb, 4), scalar=0.5, in1=C.xT[:, dch, :],
                    op0=ALU.mult, op1=ALU.add), waits=[last])
                for k in range(4):
                    C.bank_free[pb + k] = te
                x_tok = te
                B.aT_free = last
    return x_tok


class InprojBufs:
    def __init__(self, P):
        self.w_sem = P.dsem()
        self.st_sem = [P.dsem() for _ in range(2)]
        self.nst = 0
        self.nunit = 0

    def carve(self, A):
        self.w = A.take([128, 8, 2048], BF16)
        self.st = [A.take([128, 2048], BF16) for i in range(2)]
        self.st_free = [None, None]


def emit_inproj(C, B, w_in, h_ready, qT_d, kT_d, v_d, uf_d, w_free=None):
    P = C.P
    wt = None
    for q in range(4):
        wt = P.dma("pool", B.w[:, :, q * 512:(q + 1) * 512],
                   w_in[:, q * 512:(q + 1) * 512].rearrange("(c p) f -> p c f", p=128), B.w_sem, waits=[w_free])
    out_toks = []
    for m in range(8):
        pb = 4 * (B.nunit % 2)
        B.nunit += 1
        last = None
        for d in range(8):
            for t in range(4):
                first = (d == 0 and t == 0)
                w = [wt, h_ready] + [C.bank_free[pb + k] for k in range(4)] if first else ()
                last = P.op("pe", lambda g, pb=pb, t=t, d=d, m=m: g.matmul(
                    C.bank(pb + t), lhsT=B.w[:, d, m * 128:(m + 1) * 128], rhs=C.hT[:, d, t * 512:(t + 1) * 512],
                    start=(d == 0), stop=(d == 7)), waits=w, sig=(d == 7 and t == 3))
        si = B.nst % 2
        B.nst += 1
        if m < 4:
            te = P.op("dve", lambda g, pb=pb, si=si: g.tensor_scalar(out=B.st[si], in0=C.bank(pb, 4), scalar1=0.125, scalar2=None, op0=ALU.mult),
                      waits=[last, B.st_free[si]])
        else:
            te = P.op("act", lambda g, pb=pb, si=si: g.activation(out=B.st[si], in_=C.bank(pb, 4), func=AF.Copy),
                      waits=[last, B.st_free[si]])
        for k in range(4):
            C.bank_free[pb + k] = te
        dst = (qT_d if m < 4 else kT_d)[(m % 4) * 128:(m % 4 + 1) * 128, :]
        B.st_free[si] = P.dma("sp", dst, B.st[si][:], B.st_sem[si], waits=[te])
        out_toks.append(B.st_free[si])
    for which in range(2):
        for tq in range(4):
            pb = 4 * (B.nunit % 2)
            B.nunit += 1
            last = None
            for ti in range(4):
                tt = tq * 4 + ti
                for d in range(8):
                    first = (d == 0 and ti == 0)
                    w = [wt, h_ready] + [C.bank_free[pb + k] for k in range(4)] if first else ()
                    last = P.op("pe", lambda g, pb=pb, ti=ti, d=d, tt=tt, which=which: g.matmul(
                        C.bank(pb + ti), lhsT=C.hT[:, d, tt * 128:(tt + 1) * 128],
                        rhs=B.w[:, d, 1024 + which * 512:1024 + (which + 1) * 512],
                        start=(d == 0), stop=(d == 7)), waits=w, sig=(d == 7 and ti == 3))
            si = B.nst % 2
            B.nst += 1
            if tq % 2 == 0:
                te = P.op("act", lambda g, pb=pb, si=si: g.activation(out=B.st[si][:], in_=C.bank(pb, 4), func=AF.Copy),
                          waits=[last, B.st_free[si]])
            else:
                te = P.op("dve", lambda g, pb=pb, si=si: g.tensor_copy(out=B.st[si][:], in_=C.bank(pb, 4)),
                          waits=[last, B.st_free[si]])
            for k in range(4):
                C.bank_free[pb + k] = te
            dst = (v_d if which == 0 else uf_d)[tq * 512:(tq + 1) * 512, :].rearrange("(i p) f -> p i f", p=128)
            B.st_free[si] = P.dma("sp", dst, B.st[si][:].rearrange("p (i f) -> p i f", i=4), B.st_sem[si], waits=[te])
            out_toks.append(B.st_free[si])
            C.h_free = last
    return out_toks


class NABufs:
    def __init__(self, P):
        self.ld = P.dsem()
        self.b_sem = [P.dsem() for _ in range(2)]
        self.o_sem = [P.dsem() for _ in range(2)]

    def carve(self, A):
        self.qT = A.take([128, 4, NTOK], BF16)
        self.kT = A.take([128, 4, 2560], BF16)
        self.v = A.take([128, 20, 512], BF16)
        self.bias = [A.take([128, 5, 640], F32) for i in range(2)]
        self.tmp = [A.take([128, 640], F32) for i in range(2)]
        self.pT = [A.take([128, 640], BF16) for i in range(2)]
        self.rec = [A.take([64, 128], F32) for i in range(2)]
        self.ost = [A.take([64, NTOK], BF16) for i in range(2)]
        self.ones = A.take([128, 64], BF16)


def emit_na(C, B, qT_d, kTe_d, ve_d, bias_d, oT_d, in_ready=None):
    P = C.P
    t_ones = P.op("pool", lambda g: g.memset(B.ones[:], 1.0))
    for m in range(4):
        P.dma("sp", B.qT[:, m, :], qT_d[m * 128:(m + 1) * 128, :], B.ld, waits=[in_ready])
        P.dma("sp", B.kT[:, m, :], kTe_d[m * 128:(m + 1) * 128, :], B.ld)
    ld = P.dma("sp", B.v[:], ve_d.rearrange("(i p) f -> p i f", p=128), B.ld)
    b_free = [None, None]
    tmp_free = [None, None]
    pT_free = [None, None]
    rec_free = [None, None]
    o_free = [None, None]
    btok = {}
    out_toks = []
    units = [(h, ml) for h in range(8) for ml in range(16)]
    st = {}

    def load_bias(h):
        bs = h % 2
        btok[h] = P.dma("sp", B.bias[bs][:].rearrange("p a b -> p (a b)"), bias_d[h], B.b_sem[bs], waits=[b_free[bs]])

    def emit_S(i):
        h, ml = units[i]
        m, po = h // 2, (h % 2) * 64
        u2 = i % 2
        sb_ = 2 * u2
        if ml == 0 and h + 1 < 8:
            load_bias(h + 1)
        lastS = None
        for kc in range(5):
            w = [ld, C.bank_free[sb_], C.bank_free[sb_ + 1]] if kc == 0 else ()
            lastS = P.op("pe", lambda g, sb_=sb_, kc=kc, m=m, po=po, ml=ml: g.matmul(
                C.psum[:, sb_ * 512 + kc * 128: sb_ * 512 + (kc + 1) * 128],
                lhsT=B.kT[po:po + 64, m, (ml + kc) * 128:(ml + kc + 1) * 128],
                rhs=B.qT[po:po + 64, m, ml * 128:(ml + 1) * 128], start=True, stop=True),
                waits=w, sig=(kc == 4))
        bs = h % 2
        typ = {0: 0, 1: 1, 14: 3, 15: 4}.get(ml, 2)
        tt = P.op("dve", lambda g, sb_=sb_, u2=u2, bs=bs, typ=typ: g.tensor_tensor(
            out=B.tmp[u2][:], in0=C.psum[:, sb_ * 512: sb_ * 512 + 640], in1=B.bias[bs][:, typ, :], op=ALU.add),
            waits=[lastS, btok[h], tmp_free[u2]])
        C.bank_free[sb_] = tt
        C.bank_free[sb_ + 1] = tt
        b_free[bs] = tt
        te = P.op("act", lambda g, u2=u2: g.activation(out=B.pT[u2][:], in_=B.tmp[u2][:], func=AF.Exp),
                  waits=[tt, pT_free[u2]])
        tmp_free[u2] = te
        st[i] = te

    def emit_rest(i):
        h, ml = units[i]
        u2 = i % 2
        ob = 4 + u2
        os_ = h % 2
        te = st.pop(i)
        for kc in range(5):
            w = [te, t_ones, C.bank_free[ob]] if kc == 0 else ()
            P.op("pe", lambda g, ob=ob, kc=kc, ml=ml, h=h, u2=u2: g.matmul(
                C.psum[0:64, ob * 512: ob * 512 + 128], lhsT=B.v[:, ml + kc, h * 64:(h + 1) * 64],
                rhs=B.pT[u2][:, kc * 128:(kc + 1) * 128], start=(kc == 0), stop=(kc == 4)), waits=w, sig=False)
        lastO = None
        for kc in range(5):
            lastO = P.op("pe", lambda g, ob=ob, kc=kc, u2=u2: g.matmul(
                C.psum[0:64, ob * 512 + 128: ob * 512 + 256], lhsT=B.ones[:],
                rhs=B.pT[u2][:, kc * 128:(kc + 1) * 128], start=(kc == 0), stop=(kc == 4)), sig=(kc == 4))
        pT_free[u2] = lastO
        tr = P.op("dve", lambda g, ob=ob, u2=u2: g.reciprocal(out=B.rec[u2][:], in_=C.psum[0:64, ob * 512 + 128: ob * 512 + 256]),
                  waits=[lastO, rec_free[u2]])
        lastw = P.op("dve", lambda g, ob=ob, u2=u2, os_=os_, ml=ml: g.tensor_tensor(
            out=B.ost[os_][:, ml * 128:(ml + 1) * 128], in0=C.psum[0:64, ob * 512: ob * 512 + 128],
            in1=B.rec[u2][:], op=ALU.mult), waits=[tr, o_free[os_]] if ml == 0 else [tr])
        rec_free[u2] = lastw
        C.bank_free[ob] = lastw
        if ml == 15:
            o_free[os_] = P.dma("sp", oT_d[h * 64:(h + 1) * 64, :], B.ost[os_], B.o_sem[os_], waits=[lastw])
            out_toks.append(o_free[os_])

    load_bias(0)
    n = len(units)
    for i in range(n + 1):
        if i < n:
            emit_S(i)
        if i >= 1:
            emit_rest(i - 1)
    return out_toks


class FNBufs:
    def __init__(self, P):
        self.ld = P.dsem()
        self.e_sem = [P.dsem() for _ in range(2)]
        self.st = P.dsem()

    def carve(self, A):
        self.u = A.take([128, 64, 128], BF16)
        self.E = [A.take([128, 8, 256], BF16) for i in range(2)]
        self.Bsb = A.take([128, 256, 64], BF16)
        self.R = A.take([128, 512], BF16)
        self.CS = A.take([128, 128], BF16)
        self.G = [A.take([128, 4, 256], BF16) for i in range(2)]
        self.YT = A.take([128, 64, 128], BF16)


def emit_fnet(C, B, u_d, E_d, R_d, CS_d, YT_d, in_ready=None):
    P = C.P
    P.dma("sp", B.u[:].rearrange("p a b -> p (a b)"), u_d, B.ld, waits=[in_ready])
    P.dma("sp", B.R[:], R_d, B.ld)
    ld = P.dma("sp", B.CS[:], CS_d, B.ld)
    e_free = [None, None]
    unit = 0
    s1_act = s1_dve = None
    for ec in range(8):
        es = ec % 2
        etok = P.dma("sp", B.E[es][:].rearrange("p a b -> p (a b)"), E_d[:, ec * 2048:(ec + 1) * 2048], B.e_sem[es],
                     waits=[e_free[es]])
        for half in range(2):
            pb = 2 * (unit % 4)
            unit += 1
            last = None
            for ci in range(4):
                cl = half * 4 + ci
                c = ec * 8 + cl
                w = [ld, etok, C.bank_free[pb], C.bank_free[pb + 1]] if ci == 0 else ()
                last = P.op("pe", lambda g, pb=pb, ci=ci, c=c, cl=cl, es=es: g.matmul(
                    C.psum[:, pb * 512 + ci * 256: pb * 512 + (ci + 1) * 256], lhsT=B.u[:, c, :], rhs=B.E[es][:, cl, :],
                    start=True, stop=True), waits=w, sig=(ci == 3))
            c0 = ec * 8 + half * 4
            if unit % 2 == 0:
                te = P.op("act", lambda g, pb=pb, c0=c0: g.activation(
                    out=B.Bsb[:, :, c0:c0 + 4].rearrange("p n c -> p c n"),
                    in_=C.bank(pb, 2).rearrange("p (c n) -> p c n", c=4), func=AF.Copy), waits=[last])
            else:
                te = P.op("dve", lambda g, pb=pb, c0=c0: g.tensor_copy(
                    out=B.Bsb[:, :, c0:c0 + 4].rearrange("p n c -> p c n"),
                    in_=C.bank(pb, 2).rearrange("p (c n) -> p c n", c=4)), waits=[last])
            C.bank_free[pb] = te
            C.bank_free[pb + 1] = te
            if unit % 2 == 0:
                s1_act = te
            else:
                s1_dve = te
        e_free[es] = last
    g_free = [None, None]
    y_tok = None
    import os
    nst = int(os.environ.get("FN_STAGES", "3"))
    for kq in range(16 if nst >= 2 else 0):
        gs = kq % 2
        pb = 2 * (unit % 4)
        unit += 1
        last = None
        for ki in range(4):
            kp = kq * 4 + ki
            for part in range(2):
                w = [s1_act, s1_dve, C.bank_free[pb], C.bank_free[pb + 1]] if (ki == 0 and part == 0) else ()
                lhs = B.Bsb[:, part * 128 + 2 * kp: part * 128 + 2 * kp + 2, :].rearrange("p k c -> p (k c)")
                last = P.op("pe", lambda g, pb=pb, ki=ki, part=part, lhs=lhs: g.matmul(
                    C.psum[:, pb * 512 + ki * 256: pb * 512 + (ki + 1) * 256], lhsT=lhs,
                    rhs=B.R[:, part * 256:(part + 1) * 256], start=(part == 0), stop=(part == 1)),
                    waits=w, sig=(ki == 3 and part == 1))
        if kq % 2 == 0:
            te = P.op("act", lambda g, pb=pb, gs=gs: g.activation(out=B.G[gs][:].rearrange("p a b -> p (a b)"), in_=C.bank(pb, 2), func=AF.Copy),
                      waits=[last, g_free[gs]])
        else:
            te = P.op("dve", lambda g, pb=pb, gs=gs: g.tensor_copy(out=B.G[gs][:].rearrange("p a b -> p (a b)"), in_=C.bank(pb, 2)),
                      waits=[last, g_free[gs]])
        C.bank_free[pb] = te
        C.bank_free[pb + 1] = te
        if nst < 3:
            y_tok = te
            continue
        yb = 2 * (unit % 4)
        unit += 1
        last3 = None
        for ki in range(4):
            for k2 in range(2):
                po = k2 * 64
                for part in range(2):
                    w = [te, C.bank_free[yb], C.bank_free[yb + 1]] if (ki == 0 and k2 == 0 and part == 0) else ()
                    last3 = P.op("pe", lambda g, yb=yb, gs=gs, ki=ki, k2=k2, po=po, part=part: g.matmul(
                        C.psum[:, (yb + k2) * 512 + ki * 64: (yb + k2) * 512 + (ki + 1) * 64],
                        lhsT=B.G[gs][po:po + 64, ki, part * 128:(part + 1) * 128],
                        rhs=B.CS[po:po + 64, part * 64:(part + 1) * 64], start=(part == 0), stop=(part == 1)),
                        waits=w, sig=(ki == 3 and k2 == 1 and part == 1))
        g_free[gs] = last3
        kr0 = kq * 8
        ty = P.op("dve", lambda g, yb=yb, kr0=kr0: g.tensor_copy(
            out=B.YT[:, :, kr0:kr0 + 8].rearrange("p x (a k) -> p x a k", k=2),
            in_=C.bank(yb, 2).rearrange("p (k a x) -> p k a x", k=2, a=8)[:, :, 0:4, :].rearrange("p k a x -> p x a k")),
            waits=[last3])
        C.bank_free[yb] = ty
        C.bank_free[yb + 1] = ty
        y_tok = ty
    out = P.dma("sp", YT_d, B.YT[:].rearrange("p a b -> p (a b)"), B.st, waits=[y_tok, s1_act, s1_dve])
    return [out]


class OutBufs:
    def __init__(self, P):
        self.ld = P.dsem()
        self.wld = P.dsem()
        self.in_sem = [P.dsem() for _ in range(2)]

    def carve(self, A):
        self.oT = [A.take([128, 4, 512], BF16) for i in range(2)]
        self.yfT = [A.take([128, 4, 512], BF16) for i in range(2)]
        self.wna = A.take([128, 4, D], BF16)
        self.wf = A.take([128, 4, D], BF16)
        self.wo = A.take([128, 8, D], BF16)
        self.wg = A.take([128, 8, 2 * D], BF16)
        self.gb = A.take([128, 2, 8], F32)
        self.mT = A.take([128, 8, 512], BF16)
        self.sga = [A.take([128, 512], F32) for i in range(2)]
        self.sgf = [A.take([128, 512], F32) for i in range(2)]


def emit_outproj(C, B, oT_d, yfT_d, w_in, w_na, w_f, w_o, gb_d, in_ready=None):
    P = C.P
    ld = P.dma("sp", B.gb.rearrange("p a b -> p (a b)"), gb_d, B.ld, waits=[in_ready])
    P.dma("pool", B.wna, w_na.rearrange("(m p) d -> p m d", p=128), B.wld)
    P.dma("pool", B.wf, w_f.rearrange("(g p) d -> p g d", p=128), B.wld)
    wl = None
    for q in range(4):
        wl = P.dma("pool", B.wg[:, :, q * 512:(q + 1) * 512],
                   w_in[:, 2048 + q * 512:2048 + (q + 1) * 512].rearrange("(c p) f -> p c f", p=128), B.wld)
    wl = P.dma("pool", B.wo, w_o.rearrange("(c p) d -> p c d", p=128), B.wld)
    in_free = [None, None]
    s_free = [None, None]
    m_free = None
    x_tok = None
    unit = 0

    def load_in(t):
        k = t % 2
        tsl = slice(t * 512, (t + 1) * 512)
        P.dma("sp", B.oT[k], oT_d[:, tsl].rearrange("(m p) t -> p m t", p=128), B.in_sem[k], waits=[in_ready, in_free[k]])
        return P.dma("sp", B.yfT[k], yfT_d[:, tsl].rearrange("(g p) t -> p g t", p=128), B.in_sem[k])

    in_tok = {0: load_in(0)}
    for t in range(4):
        ts = slice(t * 512, (t + 1) * 512)
        k = t % 2
        if t + 1 < 4:
            in_tok[t + 1] = load_in(t + 1)
        d3 = None
        for dch in range(8):
            pb = 4 * (unit % 2)
            k2 = unit % 2
            unit += 1
            dsl = slice(dch * 128, (dch + 1) * 128)
            w0 = [ld, wl, in_tok[t]] + [C.bank_free[pb + q] for q in range(4)]
            for mm in range(4):
                P.op("pe", lambda g, pb=pb, mm=mm, dsl=dsl, k=k: g.matmul(
                    C.bank(pb), lhsT=B.wna[:, mm, dsl], rhs=B.oT[k][:, mm, :], start=(mm == 0), stop=(mm == 3)),
                    waits=w0 if mm == 0 else (), sig=False)
            for d in range(8):
                tB = P.op("pe", lambda g, pb=pb, d=d, dsl=dsl, ts=ts: g.matmul(
                    C.bank(pb + 1), lhsT=B.wg[:, d, dsl], rhs=C.hT[:, d, ts], start=(d == 0), stop=(d == 7)), sig=(d == 7))
            for gg in range(4):
                P.op("pe", lambda g, pb=pb, gg=gg, dsl=dsl, k=k: g.matmul(
                    C.bank(pb + 2), lhsT=B.wf[:, gg, dsl], rhs=B.yfT[k][:, gg, :], start=(gg == 0), stop=(gg == 3)), sig=False)
            for d in range(8):
                tD = P.op("pe", lambda g, pb=pb, d=d, dch=dch, ts=ts: g.matmul(
                    C.bank(pb + 3), lhsT=B.wg[:, d, D + dch * 128:D + (dch + 1) * 128], rhs=C.hT[:, d, ts],
                    start=(d == 0), stop=(d == 7)), sig=(d == 7))
            sa = P.op("act", lambda g, pb=pb, k2=k2, dch=dch: g.activation(
                out=B.sga[k2], in_=C.bank(pb + 1), func=AF.Sigmoid, bias=B.gb[:, 0, dch:dch + 1], scale=1.0),
                waits=[tB, s_free[k2]])
            sf = P.op("act", lambda g, pb=pb, k2=k2, dch=dch: g.activation(
                out=B.sgf[k2], in_=C.bank(pb + 3), func=AF.Sigmoid, bias=B.gb[:, 1, dch:dch + 1], scale=1.0),
                waits=[tD])
            d1 = P.op("dve", lambda g, pb=pb, k2=k2: g.tensor_tensor(out=B.sga[k2], in0=C.bank(pb), in1=B.sga[k2], op=ALU.mult),
                      waits=[sa, tD])
            d2 = P.op("dve", lambda g, pb=pb, k2=k2: g.tensor_tensor(out=B.sgf[k2], in0=C.bank(pb + 2), in1=B.sgf[k2], op=ALU.mult),
                      waits=[sf, d1])
            for q in range(4):
                C.bank_free[pb + q] = d2
            d3 = P.op("pool", lambda g, k2=k2, dch=dch: g.tensor_tensor(out=B.mT[:, dch, :], in0=B.sga[k2], in1=B.sgf[k2], op=ALU.add),
                      waits=[d2, m_free] if dch == 0 else [d2])
            s_free[k2] = d3
        in_free[k] = tD
        C.h_free = tD
        for dq in range(2):
            pb = 4 * (unit % 2)
            unit += 1
            last = None
            for di in range(4):
                do = dq * 4 + di
                for dch in range(8):
                    w = [d3] + [C.bank_free[pb + q] for q in range(4)] if (di == 0 and dch == 0) else ()
                    last = P.op("pe", lambda g, pb=pb, di=di, dch=dch, do=do: g.matmul(
                        C.bank(pb + di), lhsT=B.wo[:, dch, do * 128:(do + 1) * 128], rhs=B.mT[:, dch, :],
                        start=(dch == 0), stop=(dch == 7)), waits=w, sig=(di == 3 and dch == 7))
            for di in range(4):
                do = dq * 4 + di
                x_tok = P.op("dve", lambda g, pb=pb, di=di, do=do, ts=ts: g.tensor_tensor(
                    out=C.xT[:, do, ts], in0=C.bank(pb + di), in1=C.xT[:, do, ts], op=ALU.add),
                    waits=[last])
                C.bank_free[pb + di] = x_tok
            m_free = last
    return x_tok


def emit_final_norm(C, gi, out_d):
    P = C.P
    A = C.arena
    st = [A.take([128, 8, 512], F32) for _ in range(2)]
    sem = [C.fin_sem0, C.fin_sem1]
    free = [None, None]
    toks = []

    def out_fn(c, t, ts, r, t2):
        k = t % 2
        tk = P.op("dve", lambda g, c=c, ts=ts, r=r, k=k: g.scalar_tensor_tensor(
            out=st[k][:, c, :], in0=C.xT[:, c, ts], scalar=C.gains[:, gi, c:c + 1], in1=r[:],
            op0=ALU.mult, op1=ALU.mult), waits=[t2, free[k]] if c == 0 else (), sig=(c == 7))
        if c == 7:
            free[k] = P.dma("sp", out_d[:, ts].rearrange("(c p) t -> p c t", p=128), st[k], sem[k], waits=[tk])
            toks.append(free[k])
        return tk

    emit_norm(C, gi, None, out_fn=out_fn)
    return toks


def _load_xh(P, C, x_d, h_d, g_d):
    ld = P.dsem()
    for c in range(8):
        P.dma("sp", C.xT[:, c, :], x_d[c * 128:(c + 1) * 128, :], ld)
    if h_d is not None:
        P.dma("sp", C.hT[:], h_d.rearrange("(c p) t -> p c t", p=128), ld)
    return P.dma("sp", C.gains[:].rearrange("p a b -> p (a b)"), g_d, ld)


def _save_xh(P, C, x_d, h_d, waits):
    st = P.dsem()
    tok = None
    for c in range(8):
        tok = P.dma("sp", x_d[c * 128:(c + 1) * 128, :], C.xT[:, c, :], st, waits=waits)
    tok = P.dma("sp", h_d.rearrange("(c p) t -> p c t", p=128), C.hT[:], st)
    return tok


def _dram(nc, name, shape, dt, kind):
    return nc.dram_tensor(name, list(shape), dt, kind=kind).ap()


def _stage_front(nc, P, C, gi0, with_ffn, pre):
    toks = []
    if with_ffn:
        w1_in = _dram(nc, "w1_in", [D, 2 * DFF], F32, "ExternalInput")
        w1_out = _dram(nc, "w1_out", [DFF, D], F32, "ExternalInput")
        P.barrier(pre)
        C.arena.reset()
        h = emit_norm(C, gi0, None)
        C.ffn.carve(C.arena)
        emit_ffn(C, C.ffn, w1_in, w1_out, h)
        pre = ()
    wmix = _dram(nc, "wmixb", [D, 4096], F32, "ExternalInput")
    qT = _dram(nc, "qT", [512, NTOK], BF16, "ExternalOutput")
    kT = _dram(nc, "kT", [512, NTOK], BF16, "ExternalOutput")
    v = _dram(nc, "v", [NTOK, 512], BF16, "ExternalOutput")
    uf = _dram(nc, "uf", [NTOK, 512], BF16, "ExternalOutput")
    xo = _dram(nc, "xT_out", [D, NTOK], F32, "ExternalOutput")
    ho = _dram(nc, "hT_out", [D, NTOK], BF16, "ExternalOutput")
    P.barrier(pre)
    C.arena.reset()
    h = emit_norm(C, gi0 + 1, None)
    C.inp.carve(C.arena)
    toks += emit_inproj(C, C.inp, wmix, h, qT, kT, v, uf)
    P.barrier(toks)
    toks = [_save_xh(P, C, xo, ho, ())]
    return toks


def build_A():
    nc = bass.Bass("TRN2", target_bir_lowering=False)
    x_d = _dram(nc, "xT_in", [D, NTOK], F32, "ExternalInput")
    g_d = _dram(nc, "gains", [128, 56], F32, "ExternalInput")
    P = Prog(nc)
    C = Ctx(P)
    C.ffn = FFNBufs(P)
    C.inp = InprojBufs(P)
    ld = _load_xh(P, C, x_d, None, g_d)
    toks = _stage_front(nc, P, C, 0, True, [ld])
    P.wait("sp", toks)
    P.emit()
    return nc


def build_NF():
    nc = bass.Bass("TRN2", target_bir_lowering=False)
    qT = _dram(nc, "qT", [512, NTOK], BF16, "ExternalInput")
    kTe = _dram(nc, "kTe", [512, 2560], BF16, "ExternalInput")
    ve = _dram(nc, "ve", [2560, 512], BF16, "ExternalInput")
    bias = _dram(nc, "bias", [8, 128, 3200], F32, "ExternalInput")
    u = _dram(nc, "u", [128, 8192], BF16, "ExternalInput")
    E = _dram(nc, "E", [128, 16384], BF16, "ExternalInput")
    R = _dram(nc, "R", [128, 512], BF16, "ExternalInput")
    CS = _dram(nc, "CS", [128, 128], BF16, "ExternalInput")
    oT = _dram(nc, "oT", [512, NTOK], BF16, "ExternalOutput")
    YT = _dram(nc, "YT", [128, 8192], BF16, "ExternalOutput")
    P = Prog(nc)
    C = Ctx(P)
    na = NABufs(P)
    fn = FNBufs(P)
    import os
    which = os.environ.get("NF_ONLY", "both")
    toks = []
    if which in ("both", "na"):
        C.arena.reset()
        na.carve(C.arena)
        toks = emit_na(C, na, qT, kTe, ve, bias, oT)
    if which in ("both", "fn"):
        P.barrier(toks)
        C.arena.reset()
        fn.carve(C.arena)
        toks = emit_fnet(C, fn, u, E, R, CS, YT)
    P.wait("sp", toks)
    P.emit()
    return nc


def build_E(final, gi0):
    nc = bass.Bass("TRN2", target_bir_lowering=False)
    x_d = _dram(nc, "xT_in", [D, NTOK], F32, "ExternalInput")
    h_d = _dram(nc, "hT_in", [D, NTOK], BF16, "ExternalInput")
    g_d = _dram(nc, "gains", [128, 56], F32, "ExternalInput")
    oT = _dram(nc, "oT", [512, NTOK], BF16, "ExternalInput")
    yfT = _dram(nc, "yfT", [512, NTOK], BF16, "ExternalInput")
    wmixa = _dram(nc, "wmixa", [D, 4096], F32, "ExternalInput")
    w_na = _dram(nc, "w_na", [512, D], F32, "ExternalInput")
    w_f = _dram(nc, "w_f", [512, D], F32, "ExternalInput")
    w_o = _dram(nc, "w_o", [D, D], F32, "ExternalInput")
    gb = _dram(nc, "gb", [128, 16], F32, "ExternalInput")
    w2_in = _dram(nc, "w2_in", [D, 2 * DFF], F32, "ExternalInput")
    w2_out = _dram(nc, "w2_out", [DFF, D], F32, "ExternalInput")
    P = Prog(nc)
    C = Ctx(P)
    C.ffn = FFNBufs(P)
    C.inp = InprojBufs(P)
    ob = OutBufs(P)
    ld = _load_xh(P, C, x_d, h_d, g_d)
    P.barrier([ld])
    C.arena.reset()
    ob.carve(C.arena)
    emit_outproj(C, ob, oT, yfT, wmixa, w_na, w_f, w_o, gb)
    P.barrier()
    C.arena.reset()
    h = emit_norm(C, gi0 + 2, None)
    C.ffn.carve(C.arena)
    emit_ffn(C, C.ffn, w2_in, w2_out, h)
    if final:
        out = _dram(nc, "outT", [D, NTOK], F32, "ExternalOutput")
        P.barrier()
        C.arena.reset()
        toks = emit_final_norm(C, 6, out)
    else:
        toks = _stage_front(nc, P, C, gi0 + 3, True, ())
    P.wait("sp", toks)
    P.emit()
    return nc


import ml_dtypes
BF = ml_dtypes.bfloat16
NEG = -30000.0


def _tables():
    r = np.arange(128, dtype=np.int64)[:, None, None]
    c = np.arange(64, dtype=np.int64)[None, :, None]
    kr = np.arange(128, dtype=np.int64)[None, None, :]
    th = 2 * np.pi * ((kr * (64 * r + c)) % 8192) / 8192.0
    E = np.concatenate([np.cos(th), -np.sin(th)], axis=2) * 2.0 ** -10
    ch = np.arange(128, dtype=np.int64)
    th2 = 2 * np.pi * ((ch[:, None] * ch[None, :]) % 128) / 128.0
    Cm, Sm = np.cos(th2), np.sin(th2)
    R = np.concatenate([Cm, -Sm, Sm, Cm], axis=1)
    cc = np.arange(64, dtype=np.int64)
    th3 = 2 * np.pi * ((cc[:, None] * cc[None, :]) % 64) / 64.0
    CS64 = np.concatenate([np.cos(th3), np.sin(th3)], axis=1)
    CS = np.concatenate([CS64, CS64], axis=0)
    return (np.ascontiguousarray(E.reshape(128, 64 * 256)).astype(BF), R.astype(BF), CS.astype(BF))


def _ext_chunk(jj, e):
    gc = jj * 16 + e - 2
    if gc < 0:
        return 3 if e == 0 else None
    if gc > 63:
        return 60 if e == 19 else None
    return gc


def _bias_tiles(rpb, jj):
    out = np.full((8, 128, 5, 640), NEG, np.float32)
    qi = np.arange(2)[None, None, :, None]
    qc = np.arange(64)[None, None, None, :]
    ki = np.arange(2)[:, None, None, None]
    kcl = np.arange(64)[None, :, None, None]
    ws = np.clip(qc - 8, 0, 48)
    col_ok = (kcl >= ws) & (kcl < ws + 16)
    dc = np.clip(kcl - qc, -15, 15) + 15
    for typ, ml in ((0, 0), (1, 1), (2, 5), (3, 14), (4, 15)):
        m = jj * 16 + ml
        r = 2 * m + qi
        rs = np.clip(r - 4, 0, 120)
        for kc in range(5):
            gc = _ext_chunk(jj, ml + kc)
            if gc is None:
                continue
            krow = 2 * gc + ki
            ok = (krow >= rs) & (krow < rs + 8) & col_ok
            dr = np.clip(krow - r + 7, 0, 14)
            drb = np.broadcast_to(dr, ok.shape)
            dcb = np.broadcast_to(dc, ok.shape)
            vals = rpb[:, drb, dcb]
            tile = np.where(ok[None], vals, np.float32(NEG)).reshape(8, 128, 128)
            out[:, :, typ, kc * 128:(kc + 1) * 128] = tile
    return np.ascontiguousarray(out.reshape(8, 128, 3200))


def _gain_layout(g):
    n = g.shape[0]
    return np.ascontiguousarray(g.reshape(n, 8, 128).transpose(2, 0, 1).reshape(128, n * 8)).astype(np.float32)


def _run(nc, in_maps):
    res = run_bass_kernel_spmd(nc, in_maps, core_ids=list(range(8)))
    return res.results


def _exchange_front(outs, rpb_l, tabs):
    E, R, CS = tabs
    ims = []
    for core in range(8):
        b, jj = divmod(core, 4)
        kT_all = np.concatenate([outs[b * 4 + q]["kT"] for q in range(4)], axis=1)
        v_all = np.concatenate([outs[b * 4 + q]["v"] for q in range(4)], axis=0)
        uf_all = np.concatenate([outs[b * 4 + q]["uf"] for q in range(4)], axis=0)
        kTe = np.zeros((512, 2560), BF)
        ve = np.zeros((2560, 512), BF)
        for e in range(20):
            gc = _ext_chunk(jj, e)
            if gc is None:
                continue
            kTe[:, e * 128:(e + 1) * 128] = kT_all[:, gc * 128:(gc + 1) * 128]
            ve[e * 128:(e + 1) * 128, :] = v_all[gc * 128:(gc + 1) * 128, :]
        g = jj
        u = np.ascontiguousarray(uf_all[:, g * 128:(g + 1) * 128]).reshape(128, 64 * 128)
        ims.append({"qT": outs[core]["qT"], "kTe": kTe, "ve": ve, "bias": _bias_tiles(rpb_l, jj),
                    "u": u, "E": E, "R": R, "CS": CS})
    return ims


def kernel(x, ffn1_norm, ffn1_w_in, ffn1_w_out, mix_norm, mix_w_in, mix_gate_bias, na_rpb, na_w_out,
           f_w_out, mix_w_o, ffn2_norm, ffn2_w_in, ffn2_w_out, final_norm):
    f32 = lambda a: np.ascontiguousarray(np.asarray(a, dtype=np.float32))
    x = f32(x)
    gains = _gain_layout(np.stack([f32(ffn1_norm)[0], f32(mix_norm)[0], f32(ffn2_norm)[0],
                                   f32(ffn1_norm)[1], f32(mix_norm)[1], f32(ffn2_norm)[1], f32(final_norm)]))
    tabs = _tables()
    rpb = f32(na_rpb)
    ims = []
    for core in range(8):
        b, jj = divmod(core, 4)
        ims.append({"xT_in": np.ascontiguousarray(x[b, jj * NTOK:(jj + 1) * NTOK, :].T), "gains": gains,
                    "w1_in": f32(ffn1_w_in[0]), "w1_out": f32(ffn1_w_out[0]), "wmixb": f32(mix_w_in[0])})
    front = _run(build_A(), ims)
    out = None
    for l in range(L):
        nf = _run(build_NF(), _exchange_front(front, rpb[l], tabs))
        ims = []
        gb = _gain_layout(f32(mix_gate_bias[l]))
        for core in range(8):
            b, jj = divmod(core, 4)
            yfT = np.concatenate([nf[b * 4 + g]["YT"][:, jj * NTOK:(jj + 1) * NTOK] for g in range(4)], axis=0)
            im = {"xT_in": front[core]["xT_out"], "hT_in": front[core]["hT_out"], "gains": gains,
                  "oT": nf[core]["oT"], "yfT": np.ascontiguousarray(yfT), "wmixa": f32(mix_w_in[l]),
                  "w_na": f32(na_w_out[l]), "w_f": f32(f_w_out[l]), "w_o": f32(mix_w_o[l]), "gb": gb,
                  "w2_in": f32(ffn2_w_in[l]), "w2_out": f32(ffn2_w_out[l])}
            if l + 1 < L:
                im.update({"w1_in": f32(ffn1_w_in[l + 1]), "w1_out": f32(ffn1_w_out[l + 1]),
                           "wmixb": f32(mix_w_in[l + 1])})
            ims.append(im)
        res = _run(build_E(final=(l + 1 == L), gi0=3 * l), ims)
        if l + 1 < L:
            front = res
        else:
            out = np.empty((2, 8192, D), np.float32)
            for core in range(8):
                b, jj = divmod(core, 4)
                out[b, jj * NTOK:(jj + 1) * NTOK, :] = res[core]["outT"].T
    return out
```

```python
import numpy as np
from contextlib import ExitStack
import concourse.bass as bass
import concourse.mybir as mybir
from concourse.bass_utils import run_bass_kernel_spmd

F32 = mybir.dt.float32
BF16 = mybir.dt.bfloat16
AF = mybir.ActivationFunctionType
ALU = mybir.AluOpType

D = 1024
NTOK = 2048
DFF = 2816
NJ = 22
JG = 11
EPS = 1e-6
L = 2


class Eng:
    def __init__(self, name):
        self.name = name
        self.ops = []
        self.cnt = 0
        self.sem = None
        self.seen = {}
        self.last_sig = True


class DSem:
    def __init__(self, sem):
        self.sem = sem
        self.cnt = 0


class Prog:
    def __init__(self, nc):
        self.nc = nc
        self.stack = ExitStack()
        self.eng = {n: Eng(n) for n in ("pe", "act", "dve", "pool", "sp")}
        for n, e in self.eng.items():
            e.sem = self.stack.enter_context(nc.semaphore("s_" + n))
        self.nds = 0

    def sb(self, name, shape, dt):
        return self.stack.enter_context(self.nc.sbuf_tensor("sb_" + name, list(shape), dt))

    def ps(self, name, shape, dt=F32):
        return self.stack.enter_context(self.nc.psum_tensor("ps_" + name, list(shape), dt))

    def dsem(self):
        self.nds += 1
        return DSem(self.stack.enter_context(self.nc.semaphore("d%d" % self.nds)))

    def _waits(self, e, waits):
        for tok in waits:
            if tok is None:
                continue
            sem, val = tok
            if e.seen.get(id(sem), 0) >= val:
                continue
            e.seen[id(sem)] = val
            e.ops.append(("wait", sem, val))

    def op(self, eng, fn, waits=(), sig=True):
        e = self.eng[eng]
        self._waits(e, waits)
        e.last_sig = sig
        if sig:
            e.cnt += 1
            e.ops.append(("op", fn, e.sem, 1))
            return (e.sem, e.cnt)
        e.ops.append(("op", fn, None, 0))
        return None

    def dma(self, queue, out, in_, dsem, waits=()):
        e = self.eng[queue]
        self._waits(e, waits)
        dsem.cnt += 16
        e.ops.append(("op", lambda g: g.dma_start(out=out, in_=in_), dsem.sem, 16))
        return (dsem.sem, dsem.cnt)

    def wait(self, eng, waits):
        self._waits(self.eng[eng], waits)

    def barrier(self, extra=()):
        toks = list(extra)
        for n in ("pe", "act", "dve", "pool"):
            e = self.eng[n]
            if e.cnt:
                assert e.last_sig, "engine %s: last op before barrier must signal" % n
                toks.append((e.sem, e.cnt))
        for n in ("pe", "act", "dve", "pool", "sp"):
            self._waits(self.eng[n], toks)

    def emit(self):
        def replay(e, g):
            for it in e.ops:
                if it[0] == "wait":
                    g.wait_ge(it[1], it[2])
                else:
                    ins = it[1](g)
                    if it[2] is not None:
                        ins.then_inc(it[2], it[3])

        with self.nc.Block() as block:
            @block.tensor
            def _(g):
                replay(self.eng["pe"], g)

            @block.scalar
            def _(g):
                replay(self.eng["act"], g)

            @block.vector
            def _(g):
                replay(self.eng["dve"], g)

            @block.gpsimd
            def _(g):
                replay(self.eng["pool"], g)

            @block.sync
            def _(g):
                replay(self.eng["sp"], g)
        self.stack.close()


ARENA = 51500


class Arena:
    def __init__(self, P, n=ARENA):
        self.t = P.sb("arena", [128, n], BF16)
        self.n = n
        self.off = 0

    def reset(self):
        self.off = 0

    def take(self, shape, dt):
        n_el = 1
        for k in shape[1:]:
            n_el *= k
        units = n_el * (2 if dt == F32 else 1)
        units = (units + 15) // 16 * 16
        assert self.off + units <= self.n, ("arena overflow", self.off, units, self.n)
        v = self.t[0:shape[0], self.off:self.off + units]
        self.off += units
        if dt == F32:
            v = v.bitcast(F32)
        v = v[:, 0:n_el]
        if len(shape) == 3:
            v = v.rearrange("p (a b) -> p a b", a=shape[1])
        elif len(shape) == 4:
            v = v.rearrange("p (a b c) -> p a b c", a=shape[1], b=shape[2])
        return v


class Ctx:
    def __init__(self, P):
        self.P = P
        self.xT = P.sb("xT", [128, 8, NTOK], F32)
        self.hT = P.sb("hT", [128, 8, NTOK], BF16)
        self.psum = P.ps("psum", [128, 4096], F32)
        self.ones = P.sb("ones", [128, 128], BF16)
        self.gains = P.sb("gains", [128, 7, 8], F32)
        self.arena = Arena(P)
        self.sq = None
        self.rstd = [P.sb("rstd%d" % i, [128, 512], F32) for i in range(2)]
        self.bank_free = [None] * 8
        self.x_tok = None
        self.h_tok = None
        self.h_free = None
        self.sq_free = None
        self.rstd_free = [None, None]
        self.nnorm = 0
        self.gains_tok = None
        self.fin_sem0 = P.dsem()
        self.fin_sem1 = P.dsem()
        self.tok_ones = P.op("pool", lambda g: g.memset(self.ones[:], 1.0))

    def bank(self, b, n=1):
        return self.psum[:, b * 512:(b + n) * 512]


def emit_norm(C, gi, x_ready, out_fn=None, on_chunk=None):
    P = C.P
    toks = []
    C.sq = C.arena.take([128, 8, 512], BF16)
    C.sq_free = None
    for t in range(4):
        ts = slice(t * 512, (t + 1) * 512)
        xr = x_ready[t] if isinstance(x_ready, list) else x_ready
        r = C.rstd[C.nnorm % 2]
        k = C.nnorm % 2
        C.nnorm += 1
        tsq = P.op("act", lambda g, ts=ts: g.activation(out=C.sq, in_=C.xT[:, :, ts], func=AF.Square),
                   waits=[xr, C.sq_free])
        b = 7 if (t % 2) else 3
        for c in range(8):
            tk = P.op("pe", lambda g, c=c, b=b: g.matmul(C.bank(b), lhsT=C.ones[:], rhs=C.sq[:, c, :],
                                                         start=(c == 0), stop=(c == 7)),
                      waits=[tsq, C.tok_ones, C.bank_free[b]] if c == 0 else (), sig=(c == 7))
        C.sq_free = tk
        t1 = P.op("dve", lambda g, r=r, b=b: g.tensor_scalar(out=r[:], in0=C.bank(b), scalar1=1.0 / D, scalar2=EPS,
                                                             op0=ALU.mult, op1=ALU.add),
                  waits=[tk, C.rstd_free[k]])
        C.bank_free[b] = t1
        t1b = P.op("act", lambda g, r=r: g.activation(out=r[:], in_=r[:], func=AF.Sqrt), waits=[t1])
        t2 = P.op("dve", lambda g, r=r: g.reciprocal(out=r[:], in_=r[:]), waits=[t1b])
        last = None
        for c in range(8):
            if out_fn is None:
                last = P.op("dve", lambda g, c=c, ts=ts, r=r: g.scalar_tensor_tensor(
                    out=C.hT[:, c, ts], in0=C.xT[:, c, ts], scalar=C.gains[:, gi, c:c + 1], in1=r[:],
                    op0=ALU.mult, op1=ALU.mult), waits=[t2, C.h_free, C.gains_tok] if c == 0 else (), sig=(c == 7))
            else:
                last = out_fn(c, t, ts, r, t2)
        C.rstd_free[k] = last
        toks.append(last)
        if on_chunk is not None:
            on_chunk(t, ts, last)
    return toks


class FFNBufs:
    def __init__(self, P):
        self.wi_sem = [P.dsem() for _ in range(4)]
        self.wo_sem = [P.dsem() for _ in range(3)]
        self.nwi = 0
        self.nwo = 0
        self.nunit = 0

    def carve(self, A):
        self.aT = A.take([128, JG, NTOK], BF16)
        self.wi = [A.take([128, 8, 2, 128], BF16) for i in range(4)]
        self.wo = [A.take([128, JG, 256], BF16) for i in range(3)]
        self.sg = [A.take([128, 1024], F32) for i in range(2)]
        self.wi_free = [None] * 4
        self.wo_free = [None] * 3
        self.sg_free = [None] * 2
        self.aT_free = None


def emit_ffn(C, B, w_in, w_out, h_ready, on_x_final=None):
    P = C.P
    x_tok = None
    for grp in range(2):
        a_tok = None
        unit_tok = []
        for jl in range(JG):
            j = grp * JG + jl
            slot = B.nwi % 4
            B.nwi += 1
            src_g = w_in[:, j * 128:(j + 1) * 128].rearrange("(c p) f -> p c f", p=128)
            src_u = w_in[:, DFF + j * 128:DFF + (j + 1) * 128].rearrange("(c p) f -> p c f", p=128)
            P.dma("pool", B.wi[slot][:, :, 0, :], src_g, B.wi_sem[slot], waits=[B.wi_free[slot]])
            wtok = P.dma("pool", B.wi[slot][:, :, 1, :], src_u, B.wi_sem[slot])
            for half in range(2):
                pb = 4 * (B.nunit % 2)
                sgi = B.nunit % 2
                B.nunit += 1
                last = None
                for which in range(2):
                    for d in range(8):
                        for t2 in range(2):
                            bk = pb + which * 2 + t2
                            first = (which == 0 and d == 0 and t2 == 0)
                            fin = (which == 1 and d == 7 and t2 == 1)
                            tsl = slice(half * 1024 + t2 * 512, half * 1024 + (t2 + 1) * 512)
                            hr = h_ready[2 * half + 1] if isinstance(h_ready, list) else h_ready
                            w = [wtok, hr, C.bank_free[pb], C.bank_free[pb + 1], C.bank_free[pb + 2],
                                 C.bank_free[pb + 3]] if first else ()
                            last = P.op("pe", lambda g, bk=bk, slot=slot, d=d, which=which, tsl=tsl: g.matmul(
                                C.bank(bk), lhsT=B.wi[slot][:, d, which, :], rhs=C.hT[:, d, tsl],
                                start=(d == 0), stop=(d == 7)), waits=w, sig=fin)
                if half == 1:
                    B.wi_free[slot] = last
                C.h_free = last
                ts = P.op("act", lambda g, pb=pb, sgi=sgi: g.activation(out=B.sg[sgi][:], in_=C.bank(pb, 2), func=AF.Silu),
                          waits=[last, B.sg_free[sgi]])
                hs = slice(half * 1024, (half + 1) * 1024)
                tm = P.op("dve", lambda g, pb=pb, sgi=sgi, jl=jl, hs=hs: g.tensor_tensor(
                    out=B.aT[:, jl, hs], in0=B.sg[sgi][:], in1=C.bank(pb + 2, 2), op=ALU.mult),
                    waits=[ts, last, B.aT_free])
                B.sg_free[sgi] = tm
                for k in range(4):
                    C.bank_free[pb + k] = tm
                a_tok = tm
        for dp in range(4):
            slot = B.nwo % 3
            B.nwo += 1
            src = w_out[grp * JG * 128:(grp + 1) * JG * 128, dp * 256:(dp + 1) * 256].rearrange("(j p) d -> p j d", p=128)
            wtok = P.dma("pool", B.wo[slot][:], src, B.wo_sem[slot], waits=[B.wo_free[slot]])
            for ds_ in range(2):
                dch = dp * 2 + ds_
                pb = 4 * (B.nunit % 2)
                B.nunit += 1
                last = None
                for jl in range(JG):
                    for t in range(4):
                        first = (jl == 0 and t == 0)
                        fin = (jl == JG - 1 and t == 3)
                        w = [wtok, a_tok, C.bank_free[pb], C.bank_free[pb + 1], C.bank_free[pb + 2],
                             C.bank_free[pb + 3]] if first else ()
                        last = P.op("pe", lambda g, pb=pb, t=t, slot=slot, jl=jl, ds_=ds_: g.matmul(
                            C.bank(pb + t), lhsT=B.wo[slot][:, jl, ds_ * 128:(ds_ + 1) * 128],
                            rhs=B.aT[:, jl, t * 512:(t + 1) * 512], start=(jl == 0), stop=(jl == JG - 1)),
                            waits=w, sig=fin)
                if ds_ == 1:
                    B.wo_free[slot] = last
                te = P.op("dve", lambda g, pb=pb, dch=dch: g.scalar_tensor_tensor(
                    out=C.xT[:, dch, :], in0=C.bank(pb, 4), scalar=0.5, in1=C.xT[:, dch, :],
                    op0=ALU.mult, op1=ALU.add), waits=[last])
                for k in range(4):
                    C.bank_free[pb + k] = te
                x_tok = te
                B.aT_free = last
                if grp == 1 and on_x_final is not None:
                    on_x_final(dch, te)
    return x_tok


class InprojBufs:
    def __init__(self, P):
        self.w_sem = P.dsem()
        self.st_sem = [P.dsem() for _ in range(2)]
        self.nst = 0
        self.nunit = 0

    def carve(self, A):
        self.w = A.take([128, 8, 2048], BF16)
        self.st = [A.take([128, 2048], BF16) for i in range(2)]
        self.st_free = [None, None]


def emit_inproj(C, B, w_in, h_ready, qT_d, kT_d, v_d, uf_d, w_free=None):
    P = C.P
    wt = None
    for q in range(4):
        wt = P.dma("pool", B.w[:, :, q * 512:(q + 1) * 512],
                   w_in[:, q * 512:(q + 1) * 512].rearrange("(c p) f -> p c f", p=128), B.w_sem, waits=[w_free])
    out_toks = []
    for which in range(2):
        for tq in range(4):
            pb = 4 * (B.nunit % 2)
            B.nunit += 1
            last = None
            for ti in range(4):
                tt = tq * 4 + ti
                for d in range(8):
                    first = (d == 0 and ti == 0)
                    w = [wt, h_ready[tq] if isinstance(h_ready, list) else h_ready] + [C.bank_free[pb + k] for k in range(4)] if first else ()
                    last = P.op("pe", lambda g, pb=pb, ti=ti, d=d, tt=tt, which=which: g.matmul(
                        C.bank(pb + ti), lhsT=C.hT[:, d, tt * 128:(tt + 1) * 128],
                        rhs=B.w[:, d, 1024 + which * 512:1024 + (which + 1) * 512],
                        start=(d == 0), stop=(d == 7)), waits=w, sig=(d == 7 and ti == 3))
            si = B.nst % 2
            B.nst += 1
            if tq % 2 == 0:
                te = P.op("act", lambda g, pb=pb, si=si: g.activation(out=B.st[si][:], in_=C.bank(pb, 4), func=AF.Copy),
                          waits=[last, B.st_free[si]])
            else:
                te = P.op("dve", lambda g, pb=pb, si=si: g.tensor_copy(out=B.st[si][:], in_=C.bank(pb, 4)),
                          waits=[last, B.st_free[si]])
            for k in range(4):
                C.bank_free[pb + k] = te
            dst = (v_d if which == 0 else uf_d)[tq * 512:(tq + 1) * 512, :].rearrange("(i p) f -> p i f", p=128)
            B.st_free[si] = P.dma("sp", dst, B.st[si][:].rearrange("p (i f) -> p i f", i=4), B.st_sem[si], waits=[te])
            out_toks.append(B.st_free[si])
            C.h_free = last
    for m in range(8):
        pb = 4 * (B.nunit % 2)
        B.nunit += 1
        last = None
        for d in range(8):
            for t in range(4):
                first = (d == 0 and t == 0)
                w = [wt, h_ready[3] if isinstance(h_ready, list) else h_ready] + [C.bank_free[pb + k] for k in range(4)] if first else ()
                last = P.op("pe", lambda g, pb=pb, t=t, d=d, m=m: g.matmul(
                    C.bank(pb + t), lhsT=B.w[:, d, m * 128:(m + 1) * 128], rhs=C.hT[:, d, t * 512:(t + 1) * 512],
                    start=(d == 0), stop=(d == 7)), waits=w, sig=(d == 7 and t == 3))
        si = B.nst % 2
        B.nst += 1
        if m < 4:
            te = P.op("dve", lambda g, pb=pb, si=si: g.tensor_scalar(out=B.st[si], in0=C.bank(pb, 4), scalar1=0.125, scalar2=None, op0=ALU.mult),
                      waits=[last, B.st_free[si]])
        else:
            te = P.op("act", lambda g, pb=pb, si=si: g.activation(out=B.st[si], in_=C.bank(pb, 4), func=AF.Copy),
                      waits=[last, B.st_free[si]])
        for k in range(4):
            C.bank_free[pb + k] = te
        dst = (qT_d if m < 4 else kT_d)[(m % 4) * 128:(m % 4 + 1) * 128, :]
        B.st_free[si] = P.dma("sp", dst, B.st[si][:], B.st_sem[si], waits=[te])
        out_toks.append(B.st_free[si])
    C.h_free = last
    return out_toks


class NABufs:
    def __init__(self, P):
        self.ld = P.dsem()
        self.b_sem = [P.dsem() for _ in range(2)]
        self.o_sem = [P.dsem() for _ in range(2)]

    def carve(self, A):
        self.qT = A.take([128, 4, NTOK], BF16)
        self.kT = A.take([128, 4, 2560], BF16)
        self.v = A.take([128, 20, 512], BF16)
        self.bias = [A.take([128, 5, 640], F32) for i in range(2)]
        self.tmp = [A.take([128, 640], F32) for i in range(2)]
        self.pT = [A.take([128, 640], BF16) for i in range(2)]
        self.rec = [A.take([64, 128], F32) for i in range(2)]
        self.ost = [A.take([64, NTOK], BF16) for i in range(2)]
        self.ones = A.take([128, 64], BF16)


def emit_na(C, B, qT_d, kTe_d, ve_d, bias_d, oT_d, in_ready=None):
    P = C.P
    t_ones = P.op("pool", lambda g: g.memset(B.ones[:], 1.0))
    for m in range(4):
        P.dma("sp", B.qT[:, m, :], qT_d[m * 128:(m + 1) * 128, :], B.ld, waits=[in_ready])
        P.dma("sp", B.kT[:, m, :], kTe_d[m * 128:(m + 1) * 128, :], B.ld)
    ld = P.dma("sp", B.v[:], ve_d.rearrange("(i p) f -> p i f", p=128), B.ld)
    b_free = [None, None]
    tmp_free = [None, None]
    pT_free = [None, None]
    rec_free = [None, None]
    o_free = [None, None]
    btok = {}
    out_toks = []
    units = [(h, ml) for h in range(8) for ml in range(16)]
    st = {}

    def load_bias(h):
        bs = h % 2
        btok[h] = P.dma("sp", B.bias[bs][:].rearrange("p a b -> p (a b)"), bias_d[h], B.b_sem[bs], waits=[b_free[bs]])

    def emit_S(i):
        h, ml = units[i]
        m, po = h // 2, (h % 2) * 64
        u2 = i % 2
        sb_ = 2 * u2
        if ml == 0 and h + 1 < 8:
            load_bias(h + 1)
        lastS = None
        for kc in range(5):
            w = [ld, C.bank_free[sb_], C.bank_free[sb_ + 1]] if kc == 0 else ()
            lastS = P.op("pe", lambda g, sb_=sb_, kc=kc, m=m, po=po, ml=ml: g.matmul(
                C.psum[:, sb_ * 512 + kc * 128: sb_ * 512 + (kc + 1) * 128],
                lhsT=B.kT[po:po + 64, m, (ml + kc) * 128:(ml + kc + 1) * 128],
                rhs=B.qT[po:po + 64, m, ml * 128:(ml + 1) * 128], start=True, stop=True),
                waits=w, sig=(kc == 4))
        bs = h % 2
        typ = {0: 0, 1: 1, 14: 3, 15: 4}.get(ml, 2)
        tt = P.op("dve", lambda g, sb_=sb_, u2=u2, bs=bs, typ=typ: g.tensor_tensor(
            out=B.tmp[u2][:], in0=C.psum[:, sb_ * 512: sb_ * 512 + 640], in1=B.bias[bs][:, typ, :], op=ALU.add),
            waits=[lastS, btok[h], tmp_free[u2]])
        C.bank_free[sb_] = tt
        C.bank_free[sb_ + 1] = tt
        b_free[bs] = tt
        te = P.op("act", lambda g, u2=u2: g.activation(out=B.pT[u2][:], in_=B.tmp[u2][:], func=AF.Exp),
                  waits=[tt, pT_free[u2]])
        tmp_free[u2] = te
        st[i] = te

    def emit_rest(i):
        h, ml = units[i]
        u2 = i % 2
        ob = 4 + u2
        os_ = h % 2
        te = st.pop(i)
        for kc in range(5):
            w = [te, t_ones, C.bank_free[ob]] if kc == 0 else ()
            P.op("pe", lambda g, ob=ob, kc=kc, ml=ml, h=h, u2=u2: g.matmul(
                C.psum[0:64, ob * 512: ob * 512 + 128], lhsT=B.v[:, ml + kc, h * 64:(h + 1) * 64],
                rhs=B.pT[u2][:, kc * 128:(kc + 1) * 128], start=(kc == 0), stop=(kc == 4)), waits=w, sig=False)
        lastO = None
        for kc in range(5):
            lastO = P.op("pe", lambda g, ob=ob, kc=kc, u2=u2: g.matmul(
                C.psum[0:64, ob * 512 + 128: ob * 512 + 256], lhsT=B.ones[:],
                rhs=B.pT[u2][:, kc * 128:(kc + 1) * 128], start=(kc == 0), stop=(kc == 4)), sig=(kc == 4))
        pT_free[u2] = lastO
        tr = P.op("dve", lambda g, ob=ob, u2=u2: g.reciprocal(out=B.rec[u2][:], in_=C.psum[0:64, ob * 512 + 128: ob * 512 + 256]),
                  waits=[lastO, rec_free[u2]])
        lastw = P.op("dve", lambda g, ob=ob, u2=u2, os_=os_, ml=ml: g.tensor_tensor(
            out=B.ost[os_][:, ml * 128:(ml + 1) * 128], in0=C.psum[0:64, ob * 512: ob * 512 + 128],
            in1=B.rec[u2][:], op=ALU.mult), waits=[tr, o_free[os_]] if ml == 0 else [tr])
        rec_free[u2] = lastw
        C.bank_free[ob] = lastw
        if ml == 15:
            o_free[os_] = P.dma("sp", oT_d[h * 64:(h + 1) * 64, :], B.ost[os_], B.o_sem[os_], waits=[lastw])
            out_toks.append(o_free[os_])

    load_bias(0)
    n = len(units)
    for i in range(n + 1):
        if i < n:
            emit_S(i)
        if i >= 1:
            emit_rest(i - 1)
    return out_toks


class FNBufs:
    def __init__(self, P):
        self.ld = P.dsem()
        self.e_sem = [P.dsem() for _ in range(2)]
        self.st = P.dsem()

    def carve(self, A):
        self.u = A.take([128, 64, 128], BF16)
        self.E = [A.take([128, 8, 256], BF16) for i in range(2)]
        self.Bsb = A.take([128, 256, 64], BF16)
        self.R = A.take([128, 512], BF16)
        self.CS = A.take([128, 128], BF16)
        self.G = [A.take([128, 4, 256], BF16) for i in range(2)]
        self.YT = A.take([128, 64, 128], BF16)


def emit_fnet(C, B, u_d, E_d, R_d, CS_d, YT_d, in_ready=None):
    P = C.P
    P.dma("sp", B.u[:].rearrange("p a b -> p (a b)"), u_d, B.ld, waits=[in_ready])
    P.dma("sp", B.R[:], R_d, B.ld)
    ld = P.dma("sp", B.CS[:], CS_d, B.ld)
    e_free = [None, None]
    unit = 0
    s1_act = s1_dve = None
    for ec in range(8):
        es = ec % 2
        etok = P.dma("sp", B.E[es][:].rearrange("p a b -> p (a b)"), E_d[:, ec * 2048:(ec + 1) * 2048], B.e_sem[es],
                     waits=[e_free[es]])
        for half in range(2):
            pb = 2 * (unit % 4)
            unit += 1
            last = None
            for ci in range(4):
                cl = half * 4 + ci
                c = ec * 8 + cl
                w = [ld, etok, C.bank_free[pb], C.bank_free[pb + 1]] if ci == 0 else ()
                last = P.op("pe", lambda g, pb=pb, ci=ci, c=c, cl=cl, es=es: g.matmul(
                    C.psum[:, pb * 512 + ci * 256: pb * 512 + (ci + 1) * 256], lhsT=B.u[:, c, :], rhs=B.E[es][:, cl, :],
                    start=True, stop=True), waits=w, sig=(ci == 3))
            c0 = ec * 8 + half * 4
            if unit % 2 == 0:
                te = P.op("act", lambda g, pb=pb, c0=c0: g.activation(
                    out=B.Bsb[:, :, c0:c0 + 4].rearrange("p n c -> p c n"),
                    in_=C.bank(pb, 2).rearrange("p (c n) -> p c n", c=4), func=AF.Copy), waits=[last])
            else:
                te = P.op("dve", lambda g, pb=pb, c0=c0: g.tensor_copy(
                    out=B.Bsb[:, :, c0:c0 + 4].rearrange("p n c -> p c n"),
                    in_=C.bank(pb, 2).rearrange("p (c n) -> p c n", c=4)), waits=[last])
            C.bank_free[pb] = te
            C.bank_free[pb + 1] = te
            if unit % 2 == 0:
                s1_act = te
            else:
                s1_dve = te
        e_free[es] = last
    g_free = [None, None]
    y_tok = None
    import os
    nst = int(os.environ.get("FN_STAGES", "3"))
    for kq in range(16 if nst >= 2 else 0):
        gs = kq % 2
        pb = 2 * (unit % 4)
        unit += 1
        last = None
        for ki in range(4):
            kp = kq * 4 + ki
            for part in range(2):
                w = [s1_act, s1_dve, C.bank_free[pb], C.bank_free[pb + 1]] if (ki == 0 and part == 0) else ()
                lhs = B.Bsb[:, part * 128 + 2 * kp: part * 128 + 2 * kp + 2, :].rearrange("p k c -> p (k c)")
                last = P.op("pe", lambda g, pb=pb, ki=ki, part=part, lhs=lhs: g.matmul(
                    C.psum[:, pb * 512 + ki * 256: pb * 512 + (ki + 1) * 256], lhsT=lhs,
                    rhs=B.R[:, part * 256:(part + 1) * 256], start=(part == 0), stop=(part == 1)),
                    waits=w, sig=(ki == 3 and part == 1))
        if kq % 2 == 0:
            te = P.op("act", lambda g, pb=pb, gs=gs: g.activation(out=B.G[gs][:].rearrange("p a b -> p (a b)"), in_=C.bank(pb, 2), func=AF.Copy),
                      waits=[last, g_free[gs]])
        else:
            te = P.op("dve", lambda g, pb=pb, gs=gs: g.tensor_copy(out=B.G[gs][:].rearrange("p a b -> p (a b)"), in_=C.bank(pb, 2)),
                      waits=[last, g_free[gs]])
        C.bank_free[pb] = te
        C.bank_free[pb + 1] = te
        if nst < 3:
            y_tok = te
            continue
        yb = 2 * (unit % 4)
        unit += 1
        last3 = None
        for ki in range(4):
            for k2 in range(2):
                po = k2 * 64
                for part in range(2):
                    w = [te, C.bank_free[yb], C.bank_free[yb + 1]] if (ki == 0 and k2 == 0 and part == 0) else ()
                    last3 = P.op("pe", lambda g, yb=yb, gs=gs, ki=ki, k2=k2, po=po, part=part: g.matmul(
                        C.psum[:, (yb + k2) * 512 + ki * 64: (yb + k2) * 512 + (ki + 1) * 64],
                        lhsT=B.G[gs][po:po + 64, ki, part * 128:(part + 1) * 128],
                        rhs=B.CS[po:po + 64, part * 64:(part + 1) * 64], start=(part == 0), stop=(part == 1)),
                        waits=w, sig=(ki == 3 and k2 == 1 and part == 1))
        g_free[gs] = last3
        kr0 = kq * 8
        ty = P.op("dve", lambda g, yb=yb, kr0=kr0: g.tensor_copy(
            out=B.YT[:, :, kr0:kr0 + 8].rearrange("p x (a k) -> p x a k", k=2),
            in_=C.bank(yb, 2).rearrange("p (k a x) -> p k a x", k=2, a=8)[:, :, 0:4, :].rearrange("p k a x -> p x a k")),
            waits=[last3])
        C.bank_free[yb] = ty
        C.bank_free[yb + 1] = ty
        y_tok = ty
    out = P.dma("sp", YT_d, B.YT[:].rearrange("p a b -> p (a b)"), B.st, waits=[y_tok, s1_act, s1_dve])
    return [out]


class OutBufs:
    def __init__(self, P):
        self.ld = P.dsem()
        self.wld = P.dsem()
        self.in_sem = [P.dsem() for _ in range(2)]

    def carve(self, A):
        self.oT = [A.take([128, 4, 512], BF16) for i in range(2)]
        self.yfT = [A.take([128, 4, 512], BF16) for i in range(2)]
        self.wna = A.take([128, 4, D], BF16)
        self.wf = A.take([128, 4, D], BF16)
        self.wo = A.take([128, 8, D], BF16)
        self.wg = A.take([128, 8, 2 * D], BF16)
        self.gb = A.take([128, 2, 8], F32)
        self.mT = A.take([128, 8, 512], BF16)
        self.sga = [A.take([128, 512], F32) for i in range(2)]
        self.sgf = [A.take([128, 512], F32) for i in range(2)]


def emit_outproj(C, B, oT_d, yfT_d, w_in, w_na, w_f, w_o, gb_d, in_ready=None, h_ready=None, x_ready=None):
    P = C.P
    ld = P.dma("sp", B.gb.rearrange("p a b -> p (a b)"), gb_d, B.ld, waits=[in_ready])
    P.dma("pool", B.wna, w_na.rearrange("(m p) d -> p m d", p=128), B.wld)
    P.dma("pool", B.wf, w_f.rearrange("(g p) d -> p g d", p=128), B.wld)
    wl = None
    for q in range(4):
        wl = P.dma("pool", B.wg[:, :, q * 512:(q + 1) * 512],
                   w_in[:, 2048 + q * 512:2048 + (q + 1) * 512].rearrange("(c p) f -> p c f", p=128), B.wld)
    wl = P.dma("pool", B.wo, w_o.rearrange("(c p) d -> p c d", p=128), B.wld)
    in_free = [None, None]
    s_free = [None, None]
    m_free = None
    x_tok = None
    unit = 0

    def load_in(t):
        k = t % 2
        tsl = slice(t * 512, (t + 1) * 512)
        P.dma("sp", B.oT[k], oT_d[:, tsl].rearrange("(m p) t -> p m t", p=128), B.in_sem[k], waits=[in_ready, in_free[k]])
        return P.dma("sp", B.yfT[k], yfT_d[:, tsl].rearrange("(g p) t -> p g t", p=128), B.in_sem[k])

    in_tok = {0: load_in(0)}
    for t in range(4):
        ts = slice(t * 512, (t + 1) * 512)
        k = t % 2
        if t + 1 < 4:
            in_tok[t + 1] = load_in(t + 1)
        d3 = None
        for dch in range(8):
            pb = 4 * (unit % 2)
            k2 = unit % 2
            unit += 1
            dsl = slice(dch * 128, (dch + 1) * 128)
            w0 = [ld, wl, in_tok[t], h_ready[t] if isinstance(h_ready, list) else h_ready] + [C.bank_free[pb + q] for q in range(4)]
            for mm in range(4):
                P.op("pe", lambda g, pb=pb, mm=mm, dsl=dsl, k=k: g.matmul(
                    C.bank(pb), lhsT=B.wna[:, mm, dsl], rhs=B.oT[k][:, mm, :], start=(mm == 0), stop=(mm == 3)),
                    waits=w0 if mm == 0 else (), sig=False)
            for d in range(8):
                tB = P.op("pe", lambda g, pb=pb, d=d, dsl=dsl, ts=ts: g.matmul(
                    C.bank(pb + 1), lhsT=B.wg[:, d, dsl], rhs=C.hT[:, d, ts], start=(d == 0), stop=(d == 7)), sig=(d == 7))
            for gg in range(4):
                P.op("pe", lambda g, pb=pb, gg=gg, dsl=dsl, k=k: g.matmul(
                    C.bank(pb + 2), lhsT=B.wf[:, gg, dsl], rhs=B.yfT[k][:, gg, :], start=(gg == 0), stop=(gg == 3)), sig=False)
            for d in range(8):
                tD = P.op("pe", lambda g, pb=pb, d=d, dch=dch, ts=ts: g.matmul(
                    C.bank(pb + 3), lhsT=B.wg[:, d, D + dch * 128:D + (dch + 1) * 128], rhs=C.hT[:, d, ts],
                    start=(d == 0), stop=(d == 7)), sig=(d == 7))
            sa = P.op("act", lambda g, pb=pb, k2=k2, dch=dch: g.activation(
                out=B.sga[k2], in_=C.bank(pb + 1), func=AF.Sigmoid, bias=B.gb[:, 0, dch:dch + 1], scale=1.0),
                waits=[tB, s_free[k2]])
            sf = P.op("act", lambda g, pb=pb, k2=k2, dch=dch: g.activation(
                out=B.sgf[k2], in_=C.bank(pb + 3), func=AF.Sigmoid, bias=B.gb[:, 1, dch:dch + 1], scale=1.0),
                waits=[tD])
            d1 = P.op("dve", lambda g, pb=pb, k2=k2: g.tensor_tensor(out=B.sga[k2], in0=C.bank(pb), in1=B.sga[k2], op=ALU.mult),
                      waits=[sa, tD])
            d2 = P.op("dve", lambda g, pb=pb, k2=k2: g.tensor_tensor(out=B.sgf[k2], in0=C.bank(pb + 2), in1=B.sgf[k2], op=ALU.mult),
                      waits=[sf, d1])
            for q in range(4):
                C.bank_free[pb + q] = d2
            d3 = P.op("pool", lambda g, k2=k2, dch=dch: g.tensor_tensor(out=B.mT[:, dch, :], in0=B.sga[k2], in1=B.sgf[k2], op=ALU.add),
                      waits=[d2, m_free] if dch == 0 else [d2])
            s_free[k2] = d3
        in_free[k] = tD
        C.h_free = tD
        for dq in range(2):
            pb = 4 * (unit % 2)
            unit += 1
            last = None
            for di in range(4):
                do = dq * 4 + di
                for dch in range(8):
                    w = [d3] + [C.bank_free[pb + q] for q in range(4)] if (di == 0 and dch == 0) else ()
                    last = P.op("pe", lambda g, pb=pb, di=di, dch=dch, do=do: g.matmul(
                        C.bank(pb + di), lhsT=B.wo[:, dch, do * 128:(do + 1) * 128], rhs=B.mT[:, dch, :],
                        start=(dch == 0), stop=(dch == 7)), waits=w, sig=(di == 3 and dch == 7))
            for di in range(4):
                do = dq * 4 + di
                x_tok = P.op("dve", lambda g, pb=pb, di=di, do=do, ts=ts: g.tensor_tensor(
                    out=C.xT[:, do, ts], in0=C.bank(pb + di), in1=C.xT[:, do, ts], op=ALU.add),
                    waits=[last, x_ready[t] if isinstance(x_ready, list) else x_ready])
                C.bank_free[pb + di] = x_tok
            m_free = last
    return x_tok


def emit_final_norm(C, gi, out_d):
    P = C.P
    A = C.arena
    st = [A.take([128, 8, 512], F32) for _ in range(2)]
    sem = [C.fin_sem0, C.fin_sem1]
    free = [None, None]
    toks = []

    def out_fn(c, t, ts, r, t2):
        k = t % 2
        tk = P.op("dve", lambda g, c=c, ts=ts, r=r, k=k: g.scalar_tensor_tensor(
            out=st[k][:, c, :], in0=C.xT[:, c, ts], scalar=C.gains[:, gi, c:c + 1], in1=r[:],
            op0=ALU.mult, op1=ALU.mult), waits=[t2, free[k], C.gains_tok] if c == 0 else (), sig=(c == 7))
        if c == 7:
            free[k] = P.dma("sp", out_d[:, ts].rearrange("(c p) t -> p c t", p=128), st[k], sem[k], waits=[tk])
            toks.append(free[k])
        return tk

    emit_norm(C, gi, None, out_fn=out_fn)
    return toks


def _load_xh(P, C, x_d, h_d, g_d):
    C.gains_tok = P.dma("sp", C.gains[:].rearrange("p a b -> p (a b)"), g_d, P.dsem())
    h_toks = None
    if h_d is not None:
        h_toks = []
        for t in range(4):
            ts = slice(t * 512, (t + 1) * 512)
            h_toks.append(P.dma("sp", C.hT[:, :, ts], h_d[:, ts].rearrange("(c p) t -> p c t", p=128), P.dsem()))
    x_toks = []
    for t in range(4):
        ts = slice(t * 512, (t + 1) * 512)
        x_toks.append(P.dma("sp", C.xT[:, :, ts], x_d[:, ts].rearrange("(c p) t -> p c t", p=128), P.dsem()))
    return x_toks, h_toks


def _dram(nc, name, shape, dt, kind):
    return nc.dram_tensor(name, list(shape), dt, kind=kind).ap()


def _stage_front(nc, P, C, gi0, x_ready, first):
    toks = []
    w1_in = _dram(nc, "w1_in", [D, 2 * DFF], F32, "ExternalInput")
    w1_out = _dram(nc, "w1_out", [DFF, D], F32, "ExternalInput")
    wmix = _dram(nc, "wmixb", [D, 4096], F32, "ExternalInput")
    qT = _dram(nc, "qT", [512, NTOK], BF16, "ExternalOutput")
    kT = _dram(nc, "kT", [512, NTOK], BF16, "ExternalOutput")
    v = _dram(nc, "v", [NTOK, 512], BF16, "ExternalOutput")
    uf = _dram(nc, "uf", [NTOK, 512], BF16, "ExternalOutput")
    xo = _dram(nc, "xT_out", [D, NTOK], F32, "ExternalOutput")
    ho = _dram(nc, "hT_out", [D, NTOK], BF16, "ExternalOutput")
    xs_sem = P.dsem()
    hs_sem = P.dsem()
    if not first:
        P.barrier()
    C.arena.reset()
    h = emit_norm(C, gi0, x_ready)
    C.ffn.carve(C.arena)

    def save_x(dch, tok):
        toks.append(P.dma("sp", xo[dch * 128:(dch + 1) * 128, :], C.xT[:, dch, :], xs_sem, waits=[tok]))

    emit_ffn(C, C.ffn, w1_in, w1_out, h, on_x_final=save_x)
    P.barrier()
    C.arena.reset()

    def save_h(t, ts, tok):
        toks.append(P.dma("sp", ho[:, ts].rearrange("(c p) t -> p c t", p=128), C.hT[:, :, ts], hs_sem, waits=[tok]))

    h = emit_norm(C, gi0 + 1, None, on_chunk=save_h)
    C.inp.carve(C.arena)
    toks += emit_inproj(C, C.inp, wmix, h, qT, kT, v, uf)
    return toks


def build_A():
    nc = bass.Bass("TRN2", target_bir_lowering=False)
    x_d = _dram(nc, "xT_in", [D, NTOK], F32, "ExternalInput")
    g_d = _dram(nc, "gains", [128, 56], F32, "ExternalInput")
    P = Prog(nc)
    C = Ctx(P)
    C.ffn = FFNBufs(P)
    C.inp = InprojBufs(P)
    x_toks, _ = _load_xh(P, C, x_d, None, g_d)
    toks = _stage_front(nc, P, C, 0, x_toks, True)
    P.wait("sp", toks)
    P.emit()
    return nc


def build_NF():
    nc = bass.Bass("TRN2", target_bir_lowering=False)
    qT = _dram(nc, "qT", [512, NTOK], BF16, "ExternalInput")
    kTe = _dram(nc, "kTe", [512, 2560], BF16, "ExternalInput")
    ve = _dram(nc, "ve", [2560, 512], BF16, "ExternalInput")
    bias = _dram(nc, "bias", [8, 128, 3200], F32, "ExternalInput")
    u = _dram(nc, "u", [128, 8192], BF16, "ExternalInput")
    E = _dram(nc, "E", [128, 16384], BF16, "ExternalInput")
    R = _dram(nc, "R", [128, 512], BF16, "ExternalInput")
    CS = _dram(nc, "CS", [128, 128], BF16, "ExternalInput")
    oT = _dram(nc, "oT", [512, NTOK], BF16, "ExternalOutput")
    YT = _dram(nc, "YT", [128, 8192], BF16, "ExternalOutput")
    P = Prog(nc)
    C = Ctx(P)
    na = NABufs(P)
    fn = FNBufs(P)
    import os
    which = os.environ.get("NF_ONLY", "both")
    toks = []
    if which in ("both", "na"):
        C.arena.reset()
        na.carve(C.arena)
        toks = emit_na(C, na, qT, kTe, ve, bias, oT)
    if which in ("both", "fn"):
        P.barrier(toks)
        C.arena.reset()
        fn.carve(C.arena)
        toks = emit_fnet(C, fn, u, E, R, CS, YT)
    P.wait("sp", toks)
    P.emit()
    return nc


def build_E(final, gi0):
    nc = bass.Bass("TRN2", target_bir_lowering=False)
    x_d = _dram(nc, "xT_in", [D, NTOK], F32, "ExternalInput")
    h_d = _dram(nc, "hT_in", [D, NTOK], BF16, "ExternalInput")
    g_d = _dram(nc, "gains", [128, 56], F32, "ExternalInput")
    oT = _dram(nc, "oT", [512, NTOK], BF16, "ExternalInput")
    yfT = _dram(nc, "yfT", [512, NTOK], BF16, "ExternalInput")
    wmixa = _dram(nc, "wmixa", [D, 4096], F32, "ExternalInput")
    w_na = _dram(nc, "w_na", [512, D], F32, "ExternalInput")
    w_f = _dram(nc, "w_f", [512, D], F32, "ExternalInput")
    w_o = _dram(nc, "w_o", [D, D], F32, "ExternalInput")
    gb = _dram(nc, "gb", [128, 16], F32, "ExternalInput")
    w2_in = _dram(nc, "w2_in", [D, 2 * DFF], F32, "ExternalInput")
    w2_out = _dram(nc, "w2_out", [DFF, D], F32, "ExternalInput")
    P = Prog(nc)
    C = Ctx(P)
    C.ffn = FFNBufs(P)
    C.inp = InprojBufs(P)
    ob = OutBufs(P)
    x_toks, h_toks = _load_xh(P, C, x_d, h_d, g_d)
    C.arena.reset()
    ob.carve(C.arena)
    emit_outproj(C, ob, oT, yfT, wmixa, w_na, w_f, w_o, gb, h_ready=h_toks, x_ready=x_toks)
    P.barrier()
    C.arena.reset()
    h = emit_norm(C, gi0 + 2, None)
    C.ffn.carve(C.arena)
    emit_ffn(C, C.ffn, w2_in, w2_out, h)
    if final:
        out = _dram(nc, "outT", [D, NTOK], F32, "ExternalOutput")
        P.barrier()
        C.arena.reset()
        toks = emit_final_norm(C, 6, out)
    else:
        toks = _stage_front(nc, P, C, gi0 + 3, None, False)
    P.wait("sp", toks)
    P.emit()
    return nc


import ml_dtypes
BF = ml_dtypes.bfloat16
NEG = -30000.0


def _tables():
    r = np.arange(128, dtype=np.int64)[:, None, None]
    c = np.arange(64, dtype=np.int64)[None, :, None]
    kr = np.arange(128, dtype=np.int64)[None, None, :]
    th = 2 * np.pi * ((kr * (64 * r + c)) % 8192) / 8192.0
    E = np.concatenate([np.cos(th), -np.sin(th)], axis=2) * 2.0 ** -10
    ch = np.arange(128, dtype=np.int64)
    th2 = 2 * np.pi * ((ch[:, None] * ch[None, :]) % 128) / 128.0
    Cm, Sm = np.cos(th2), np.sin(th2)
    R = np.concatenate([Cm, -Sm, Sm, Cm], axis=1)
    cc = np.arange(64, dtype=np.int64)
    th3 = 2 * np.pi * ((cc[:, None] * cc[None, :]) % 64) / 64.0
    CS64 = np.concatenate([np.cos(th3), np.sin(th3)], axis=1)
    CS = np.concatenate([CS64, CS64], axis=0)
    return (np.ascontiguousarray(E.reshape(128, 64 * 256)).astype(BF), R.astype(BF), CS.astype(BF))


def _ext_chunk(jj, e):
    gc = jj * 16 + e - 2
    if gc < 0:
        return 3 if e == 0 else None
    if gc > 63:
        return 60 if e == 19 else None
    return gc


def _bias_tiles(rpb, jj):
    out = np.full((8, 128, 5, 640), NEG, np.float32)
    qi = np.arange(2)[None, None, :, None]
    qc = np.arange(64)[None, None, None, :]
    ki = np.arange(2)[:, None, None, None]
    kcl = np.arange(64)[None, :, None, None]
    ws = np.clip(qc - 8, 0, 48)
    col_ok = (kcl >= ws) & (kcl < ws + 16)
    dc = np.clip(kcl - qc, -15, 15) + 15
    for typ, ml in ((0, 0), (1, 1), (2, 5), (3, 14), (4, 15)):
        m = jj * 16 + ml
        r = 2 * m + qi
        rs = np.clip(r - 4, 0, 120)
        for kc in range(5):
            gc = _ext_chunk(jj, ml + kc)
            if gc is None:
                continue
            krow = 2 * gc + ki
            ok = (krow >= rs) & (krow < rs + 8) & col_ok
            dr = np.clip(krow - r + 7, 0, 14)
            drb = np.broadcast_to(dr, ok.shape)
            dcb = np.broadcast_to(dc, ok.shape)
            vals = rpb[:, drb, dcb]
            tile = np.where(ok[None], vals, np.float32(NEG)).reshape(8, 128, 128)
            out[:, :, typ, kc * 128:(kc + 1) * 128] = tile
    return np.ascontiguousarray(out.reshape(8, 128, 3200))


def _gain_layout(g):
    n = g.shape[0]
    return np.ascontiguousarray(g.reshape(n, 8, 128).transpose(2, 0, 1).reshape(128, n * 8)).astype(np.float32)


def _run(nc, in_maps):
    res = run_bass_kernel_spmd(nc, in_maps, core_ids=list(range(8)))
    return res.results


def _exchange_front(outs, rpb_l, tabs):
    E, R, CS = tabs
    ims = []
    for core in range(8):
        b, jj = divmod(core, 4)
        kT_all = np.concatenate([outs[b * 4 + q]["kT"] for q in range(4)], axis=1)
        v_all = np.concatenate([outs[b * 4 + q]["v"] for q in range(4)], axis=0)
        uf_all = np.concatenate([outs[b * 4 + q]["uf"] for q in range(4)], axis=0)
        kTe = np.zeros((512, 2560), BF)
        ve = np.zeros((2560, 512), BF)
        for e in range(20):
            gc = _ext_chunk(jj, e)
            if gc is None:
                continue
            kTe[:, e * 128:(e + 1) * 128] = kT_all[:, gc * 128:(gc + 1) * 128]
            ve[e * 128:(e + 1) * 128, :] = v_all[gc * 128:(gc + 1) * 128, :]
        g = jj
        u = np.ascontiguousarray(uf_all[:, g * 128:(g + 1) * 128]).reshape(128, 64 * 128)
        ims.append({"qT": outs[core]["qT"], "kTe": kTe, "ve": ve, "bias": _bias_tiles(rpb_l, jj),
                    "u": u, "E": E, "R": R, "CS": CS})
    return ims


def kernel(x, ffn1_norm, ffn1_w_in, ffn1_w_out, mix_norm, mix_w_in, mix_gate_bias, na_rpb, na_w_out,
           f_w_out, mix_w_o, ffn2_norm, ffn2_w_in, ffn2_w_out, final_norm):
    f32 = lambda a: np.ascontiguousarray(np.asarray(a, dtype=np.float32))
    x = f32(x)
    gains = _gain_layout(np.stack([f32(ffn1_norm)[0], f32(mix_norm)[0], f32(ffn2_norm)[0],
                                   f32(ffn1_norm)[1], f32(mix_norm)[1], f32(ffn2_norm)[1], f32(final_norm)]))
    tabs = _tables()
    rpb = f32(na_rpb)
    ims = []
    for core in range(8):
        b, jj = divmod(core, 4)
        ims.append({"xT_in": np.ascontiguousarray(x[b, jj * NTOK:(jj + 1) * NTOK, :].T), "gains": gains,
                    "w1_in": f32(ffn1_w_in[0]), "w1_out": f32(ffn1_w_out[0]), "wmixb": f32(mix_w_in[0])})
    front = _run(build_A(), ims)
    out = None
    for l in range(L):
        nf = _run(build_NF(), _exchange_front(front, rpb[l], tabs))
        ims = []
        gb = _gain_layout(f32(mix_gate_bias[l]))
        for core in range(8):
            b, jj = divmod(core, 4)
            yfT = np.concatenate([nf[b * 4 + g]["YT"][:, jj * NTOK:(jj + 1) * NTOK] for g in range(4)], axis=0)
            im = {"xT_in": front[core]["xT_out"], "hT_in": front[core]["hT_out"], "gains": gains,
                  "oT": nf[core]["oT"], "yfT": np.ascontiguousarray(yfT), "wmixa": f32(mix_w_in[l]),
                  "w_na": f32(na_w_out[l]), "w_f": f32(f_w_out[l]), "w_o": f32(mix_w_o[l]), "gb": gb,
                  "w2_in": f32(ffn2_w_in[l]), "w2_out": f32(ffn2_w_out[l])}
            if l + 1 < L:
                im.update({"w1_in": f32(ffn1_w_in[l + 1]), "w1_out": f32(ffn1_w_out[l + 1]),
                           "wmixb": f32(mix_w_in[l + 1])})
            ims.append(im)
        res = _run(build_E(final=(l + 1 == L), gi0=3 * l), ims)
        if l + 1 < L:
            front = res
        else:
            out = np.empty((2, 8192, D), np.float32)
            for core in range(8):
                b, jj = divmod(core, 4)
                out[b, jj * NTOK:(jj + 1) * NTOK, :] = res[core]["outT"].T
    return out
```

```python
import numpy as np
from contextlib import ExitStack
import concourse.bass as bass
import concourse.mybir as mybir
from concourse.bass_utils import run_bass_kernel_spmd

F32 = mybir.dt.float32
BF16 = mybir.dt.bfloat16
AF = mybir.ActivationFunctionType
ALU = mybir.AluOpType

D = 1024
NTOK = 2048
DFF = 2816
NJ = 22
JG = 11
EPS = 1e-6
L = 2


class Eng:
    def __init__(self, name):
        self.name = name
        self.ops = []
        self.cnt = 0
        self.sem = None
        self.seen = {}
        self.last_sig = True


class DSem:
    def __init__(self, sem):
        self.sem = sem
        self.cnt = 0


class Prog:
    def __init__(self, nc):
        self.nc = nc
        self.stack = ExitStack()
        self.eng = {n: Eng(n) for n in ("pe", "act", "dve", "pool", "sp")}
        for n, e in self.eng.items():
            e.sem = self.stack.enter_context(nc.semaphore("s_" + n))
        self.nds = 0

    def sb(self, name, shape, dt):
        return self.stack.enter_context(self.nc.sbuf_tensor("sb_" + name, list(shape), dt))

    def ps(self, name, shape, dt=F32):
        return self.stack.enter_context(self.nc.psum_tensor("ps_" + name, list(shape), dt))

    def dsem(self):
        self.nds += 1
        return DSem(self.stack.enter_context(self.nc.semaphore("d%d" % self.nds)))

    def _waits(self, e, waits):
        for tok in waits:
            if tok is None:
                continue
            sem, val = tok
            if e.seen.get(id(sem), 0) >= val:
                continue
            e.seen[id(sem)] = val
            e.ops.append(("wait", sem, val))

    def op(self, eng, fn, waits=(), sig=True):
        e = self.eng[eng]
        self._waits(e, waits)
        e.last_sig = sig
        if sig:
            e.cnt += 1
            e.ops.append(("op", fn, e.sem, 1))
            return (e.sem, e.cnt)
        e.ops.append(("op", fn, None, 0))
        return None

    def dma(self, queue, out, in_, dsem, waits=()):
        e = self.eng[queue]
        self._waits(e, waits)
        dsem.cnt += 16
        e.ops.append(("op", lambda g: g.dma_start(out=out, in_=in_), dsem.sem, 16))
        return (dsem.sem, dsem.cnt)

    def wait(self, eng, waits):
        self._waits(self.eng[eng], waits)

    def barrier(self, extra=()):
        toks = list(extra)
        for n in ("pe", "act", "dve", "pool"):
            e = self.eng[n]
            if e.cnt:
                assert e.last_sig, "engine %s: last op before barrier must signal" % n
                toks.append((e.sem, e.cnt))
        for n in ("pe", "act", "dve", "pool", "sp"):
            self._waits(self.eng[n], toks)

    def emit(self):
        def replay(e, g):
            for it in e.ops:
                if it[0] == "wait":
                    g.wait_ge(it[1], it[2])
                else:
                    ins = it[1](g)
                    if it[2] is not None:
                        ins.then_inc(it[2], it[3])

        with self.nc.Block() as block:
            @block.tensor
            def _(g):
                replay(self.eng["pe"], g)

            @block.scalar
            def _(g):
                replay(self.eng["act"], g)

            @block.vector
            def _(g):
                replay(self.eng["dve"], g)

            @block.gpsimd
            def _(g):
                replay(self.eng["pool"], g)

            @block.sync
            def _(g):
                replay(self.eng["sp"], g)
        self.stack.close()


ARENA = 51500


class Arena:
    def __init__(self, P, n=ARENA):
        self.t = P.sb("arena", [128, n], BF16)
        self.n = n
        self.off = 0

    def reset(self):
        self.off = 0

    def take(self, shape, dt):
        n_el = 1
        for k in shape[1:]:
            n_el *= k
        units = n_el * (2 if dt == F32 else 1)
        units = (units + 15) // 16 * 16
        assert self.off + units <= self.n, ("arena overflow", self.off, units, self.n)
        v = self.t[0:shape[0], self.off:self.off + units]
        self.off += units
        if dt == F32:
            v = v.bitcast(F32)
        v = v[:, 0:n_el]
        if len(shape) == 3:
            v = v.rearrange("p (a b) -> p a b", a=shape[1])
        elif len(shape) == 4:
            v = v.rearrange("p (a b c) -> p a b c", a=shape[1], b=shape[2])
        return v


class Ctx:
    def __init__(self, P, with_x=True, arena=ARENA):
        self.P = P
        if with_x:
            self.xT = P.sb("xT", [128, 8, NTOK], F32)
            self.hT = P.sb("hT", [128, 8, NTOK], BF16)
        self.psum = P.ps("psum", [128, 4096], F32)
        self.ones = P.sb("ones", [128, 128], BF16)
        self.gains = P.sb("gains", [128, 7, 8], F32)
        self.arena = Arena(P, arena)
        self.sq = None
        self.rstd = [P.sb("rstd%d" % i, [128, 512], F32) for i in range(2)]
        self.bank_free = [None] * 8
        self.x_tok = None
        self.h_tok = None
        self.h_free = None
        self.sq_free = None
        self.rstd_free = [None, None]
        self.nnorm = 0
        self.gains_tok = None
        self.fin_sem0 = P.dsem()
        self.fin_sem1 = P.dsem()
        self.tok_ones = P.op("pool", lambda g: g.memset(self.ones[:], 1.0))

    def bank(self, b, n=1):
        return self.psum[:, b * 512:(b + n) * 512]


def emit_norm(C, gi, x_ready, out_fn=None, on_chunk=None):
    P = C.P
    toks = []
    C.sq = C.arena.take([128, 8, 512], BF16)
    C.sq_free = None
    for t in range(4):
        ts = slice(t * 512, (t + 1) * 512)
        xr = x_ready[t] if isinstance(x_ready, list) else x_ready
        r = C.rstd[C.nnorm % 2]
        k = C.nnorm % 2
        C.nnorm += 1
        tsq = P.op("act", lambda g, ts=ts: g.activation(out=C.sq, in_=C.xT[:, :, ts], func=AF.Square),
                   waits=[xr, C.sq_free])
        b = 7 if (t % 2) else 3
        for c in range(8):
            tk = P.op("pe", lambda g, c=c, b=b: g.matmul(C.bank(b), lhsT=C.ones[:], rhs=C.sq[:, c, :],
                                                         start=(c == 0), stop=(c == 7)),
                      waits=[tsq, C.tok_ones, C.bank_free[b]] if c == 0 else (), sig=(c == 7))
        C.sq_free = tk
        t1 = P.op("dve", lambda g, r=r, b=b: g.tensor_scalar(out=r[:], in0=C.bank(b), scalar1=1.0 / D, scalar2=EPS,
                                                             op0=ALU.mult, op1=ALU.add),
                  waits=[tk, C.rstd_free[k]])
        C.bank_free[b] = t1
        t1b = P.op("act", lambda g, r=r: g.activation(out=r[:], in_=r[:], func=AF.Sqrt), waits=[t1])
        t2 = P.op("dve", lambda g, r=r: g.reciprocal(out=r[:], in_=r[:]), waits=[t1b])
        last = None
        for c in range(8):
            if out_fn is None:
                last = P.op("dve", lambda g, c=c, ts=ts, r=r: g.scalar_tensor_tensor(
                    out=C.hT[:, c, ts], in0=C.xT[:, c, ts], scalar=C.gains[:, gi, c:c + 1], in1=r[:],
                    op0=ALU.mult, op1=ALU.mult), waits=[t2, C.h_free, C.gains_tok] if c == 0 else (), sig=(c == 7))
            else:
                last = out_fn(c, t, ts, r, t2)
        C.rstd_free[k] = last
        toks.append(last)
        if on_chunk is not None:
            on_chunk(t, ts, last)
    return toks


class FFNBufs:
    def __init__(self, P):
        self.wi_sem = [P.dsem() for _ in range(4)]
        self.wo_sem = [P.dsem() for _ in range(3)]
        self.nwi = 0
        self.nwo = 0
        self.nunit = 0

    def carve(self, A):
        self.aT = A.take([128, JG, NTOK], BF16)
        self.wi = [A.take([128, 8, 2, 128], BF16) for i in range(4)]
        self.wo = [A.take([128, JG, 256], BF16) for i in range(3)]
        self.sg = [A.take([128, 1024], F32) for i in range(2)]
        self.wi_free = [None] * 4
        self.wo_free = [None] * 3
        self.sg_free = [None] * 2
        self.aT_free = None


def emit_ffn(C, B, w_in, w_out, h_ready, on_x_final=None):
    P = C.P
    x_tok = None
    for grp in range(2):
        a_tok = None
        unit_tok = []
        for jl in range(JG):
            j = grp * JG + jl
            slot = B.nwi % 4
            B.nwi += 1
            src_g = w_in[:, j * 128:(j + 1) * 128].rearrange("(c p) f -> p c f", p=128)
            src_u = w_in[:, DFF + j * 128:DFF + (j + 1) * 128].rearrange("(c p) f -> p c f", p=128)
            P.dma("pool", B.wi[slot][:, :, 0, :], src_g, B.wi_sem[slot], waits=[B.wi_free[slot]])
            wtok = P.dma("pool", B.wi[slot][:, :, 1, :], src_u, B.wi_sem[slot])
            for half in range(2):
                pb = 4 * (B.nunit % 2)
                sgi = B.nunit % 2
                B.nunit += 1
                last = None
                for which in range(2):
                    for d in range(8):
                        for t2 in range(2):
                            bk = pb + which * 2 + t2
                            first = (which == 0 and d == 0 and t2 == 0)
                            fin = (which == 1 and d == 7 and t2 == 1)
                            tsl = slice(half * 1024 + t2 * 512, half * 1024 + (t2 + 1) * 512)
                            hr = h_ready[2 * half + 1] if isinstance(h_ready, list) else h_ready
                            w = [wtok, hr, C.bank_free[pb], C.bank_free[pb + 1], C.bank_free[pb + 2],
                                 C.bank_free[pb + 3]] if first else ()
                            last = P.op("pe", lambda g, bk=bk, slot=slot, d=d, which=which, tsl=tsl: g.matmul(
                                C.bank(bk), lhsT=B.wi[slot][:, d, which, :], rhs=C.hT[:, d, tsl],
                                start=(d == 0), stop=(d == 7)), waits=w, sig=fin)
                if half == 1:
                    B.wi_free[slot] = last
                C.h_free = last
                ts = P.op("act", lambda g, pb=pb, sgi=sgi: g.activation(out=B.sg[sgi][:], in_=C.bank(pb, 2), func=AF.Silu),
                          waits=[last, B.sg_free[sgi]])
                hs = slice(half * 1024, (half + 1) * 1024)
                tm = P.op("dve", lambda g, pb=pb, sgi=sgi, jl=jl, hs=hs: g.tensor_tensor(
                    out=B.aT[:, jl, hs], in0=B.sg[sgi][:], in1=C.bank(pb + 2, 2), op=ALU.mult),
                    waits=[ts, last, B.aT_free])
                B.sg_free[sgi] = tm
                for k in range(4):
                    C.bank_free[pb + k] = tm
                a_tok = tm
        for dp in range(4):
            slot = B.nwo % 3
            B.nwo += 1
            src = w_out[grp * JG * 128:(grp + 1) * JG * 128, dp * 256:(dp + 1) * 256].rearrange("(j p) d -> p j d", p=128)
            wtok = P.dma("pool", B.wo[slot][:], src, B.wo_sem[slot], waits=[B.wo_free[slot]])
            for ds_ in range(2):
                dch = dp * 2 + ds_
                pb = 4 * (B.nunit % 2)
                B.nunit += 1
                last = None
                for jl in range(JG):
                    for t in range(4):
                        first = (jl == 0 and t == 0)
                        fin = (jl == JG - 1 and t == 3)
                        w = [wtok, a_tok, C.bank_free[pb], C.bank_free[pb + 1], C.bank_free[pb + 2],
                             C.bank_free[pb + 3]] if first else ()
                        last = P.op("pe", lambda g, pb=pb, t=t, slot=slot, jl=jl, ds_=ds_: g.matmul(
                            C.bank(pb + t), lhsT=B.wo[slot][:, jl, ds_ * 128:(ds_ + 1) * 128],
                            rhs=B.aT[:, jl, t * 512:(t + 1) * 512], start=(jl == 0), stop=(jl == JG - 1)),
                            waits=w, sig=fin)
                if ds_ == 1:
                    B.wo_free[slot] = last
                te = P.op("dve", lambda g, pb=pb, dch=dch: g.scalar_tensor_tensor(
                    out=C.xT[:, dch, :], in0=C.bank(pb, 4), scalar=0.5, in1=C.xT[:, dch, :],
                    op0=ALU.mult, op1=ALU.add), waits=[last])
                for k in range(4):
                    C.bank_free[pb + k] = te
                x_tok = te
                B.aT_free = last
                if grp == 1 and on_x_final is not None:
                    on_x_final(dch, te)
    return x_tok


class InprojBufs:
    def __init__(self, P):
        self.w_sem = P.dsem()
        self.st_sem = [P.dsem() for _ in range(2)]
        self.nst = 0
        self.nunit = 0

    def carve(self, A):
        self.w = A.take([128, 8, 2048], BF16)
        self.st = [A.take([128, 2048], BF16) for i in range(2)]
        self.st_free = [None, None]


def emit_inproj(C, B, w_in, h_ready, qT_d, kT_d, v_d, uf_d, w_free=None):
    P = C.P
    wt = None
    for q in range(4):
        wt = P.dma("pool", B.w[:, :, q * 512:(q + 1) * 512],
                   w_in[:, q * 512:(q + 1) * 512].rearrange("(c p) f -> p c f", p=128), B.w_sem, waits=[w_free])
    out_toks = []
    for which in range(2):
        for tq in range(4):
            pb = 4 * (B.nunit % 2)
            B.nunit += 1
            last = None
            for ti in range(4):
                tt = tq * 4 + ti
                for d in range(8):
                    first = (d == 0 and ti == 0)
                    w = [wt, h_ready[tq] if isinstance(h_ready, list) else h_ready] + [C.bank_free[pb + k] for k in range(4)] if first else ()
                    last = P.op("pe", lambda g, pb=pb, ti=ti, d=d, tt=tt, which=which: g.matmul(
                        C.bank(pb + ti), lhsT=C.hT[:, d, tt * 128:(tt + 1) * 128],
                        rhs=B.w[:, d, 1024 + which * 512:1024 + (which + 1) * 512],
                        start=(d == 0), stop=(d == 7)), waits=w, sig=(d == 7 and ti == 3))
            si = B.nst % 2
            B.nst += 1
            if tq % 2 == 0:
                te = P.op("act", lambda g, pb=pb, si=si: g.activation(out=B.st[si][:], in_=C.bank(pb, 4), func=AF.Copy),
                          waits=[last, B.st_free[si]])
            else:
                te = P.op("dve", lambda g, pb=pb, si=si: g.tensor_copy(out=B.st[si][:], in_=C.bank(pb, 4)),
                          waits=[last, B.st_free[si]])
            for k in range(4):
                C.bank_free[pb + k] = te
            dst = (v_d if which == 0 else uf_d)[tq * 512:(tq + 1) * 512, :].rearrange("(i p) f -> p i f", p=128)
            B.st_free[si] = P.dma("sp", dst, B.st[si][:].rearrange("p (i f) -> p i f", i=4), B.st_sem[si], waits=[te])
            out_toks.append(B.st_free[si])
            C.h_free = last
    for m in range(8):
        pb = 4 * (B.nunit % 2)
        B.nunit += 1
        last = None
        for d in range(8):
            for t in range(4):
                first = (d == 0 and t == 0)
                w = [wt, h_ready[3] if isinstance(h_ready, list) else h_ready] + [C.bank_free[pb + k] for k in range(4)] if first else ()
                last = P.op("pe", lambda g, pb=pb, t=t, d=d, m=m: g.matmul(
                    C.bank(pb + t), lhsT=B.w[:, d, m * 128:(m + 1) * 128], rhs=C.hT[:, d, t * 512:(t + 1) * 512],
                    start=(d == 0), stop=(d == 7)), waits=w, sig=(d == 7 and t == 3))
        si = B.nst % 2
        B.nst += 1
        if m < 4:
            te = P.op("dve", lambda g, pb=pb, si=si: g.tensor_scalar(out=B.st[si], in0=C.bank(pb, 4), scalar1=0.125, scalar2=None, op0=ALU.mult),
                      waits=[last, B.st_free[si]])
        else:
            te = P.op("act", lambda g, pb=pb, si=si: g.activation(out=B.st[si], in_=C.bank(pb, 4), func=AF.Copy),
                      waits=[last, B.st_free[si]])
        for k in range(4):
            C.bank_free[pb + k] = te
        dst = (qT_d if m < 4 else kT_d)[(m % 4) * 128:(m % 4 + 1) * 128, :]
        B.st_free[si] = P.dma("sp", dst, B.st[si][:], B.st_sem[si], waits=[te])
        out_toks.append(B.st_free[si])
    C.h_free = last
    return out_toks


class NABufs:
    def __init__(self, P):
        self.ld = P.dsem()
        self.b_sem = [P.dsem() for _ in range(2)]
        self.o_sem = [P.dsem() for _ in range(2)]

    def carve(self, A):
        self.qT = A.take([128, 4, NTOK], BF16)
        self.kT = A.take([128, 4, 2560], BF16)
        self.v = A.take([128, 20, 512], BF16)
        self.bias = [A.take([128, 5, 640], F32) for i in range(2)]
        self.tmp = [A.take([128, 640], F32) for i in range(3)]
        self.pT = [A.take([128, 640], BF16) for i in range(3)]
        self.rec = [A.take([64, 128], F32) for i in range(3)]
        self.ost = [A.take([64, NTOK], BF16) for i in range(2)]
        self.ones = A.take([128, 64], BF16)


def emit_na(C, B, qT_d, kTe_d, ve_d, bias_d, oT_d, in_ready=None, filler=None):
    P = C.P
    t_ones = P.op("pool", lambda g: g.memset(B.ones[:], 1.0))
    for m in range(4):
        P.dma("sp", B.qT[:, m, :], qT_d[m * 128:(m + 1) * 128, :], B.ld, waits=[in_ready])
        P.dma("sp", B.kT[:, m, :], kTe_d[m * 128:(m + 1) * 128, :], B.ld)
    ld = P.dma("sp", B.v[:], ve_d.rearrange("(i p) f -> p i f", p=128), B.ld)
    SKEW = 2
    NB = SKEW + 1
    b_free = [None, None]
    tmp_free = [None] * NB
    pT_free = [None] * NB
    rec_free = [None] * NB
    s_free = [None, None]
    ob_free = [None, None]
    o_free = [None, None]
    btok = {}
    out_toks = []
    units = [(h, ml) for h in range(8) for ml in range(16)]
    st = {}

    def load_bias(h):
        bs = h % 2
        btok[h] = P.dma("sp", B.bias[bs][:].rearrange("p a b -> p (a b)"), bias_d[h], B.b_sem[bs], waits=[b_free[bs]])

    def emit_S(i):
        h, ml = units[i]
        m, po = h // 2, (h % 2) * 64
        u2 = i % NB
        us = i % 2
        sc = us * 1024
        if ml == 0 and h + 1 < 8:
            load_bias(h + 1)
        lastS = None
        for kc in range(5):
            w = [ld, s_free[us]] if kc == 0 else ()
            lastS = P.op("pe", lambda g, sc=sc, kc=kc, m=m, po=po, ml=ml: g.matmul(
                C.psum[:, sc + kc * 128: sc + (kc + 1) * 128],
                lhsT=B.kT[po:po + 64, m, (ml + kc) * 128:(ml + kc + 1) * 128],
                rhs=B.qT[po:po + 64, m, ml * 128:(ml + 1) * 128], start=True, stop=True),
                waits=w, sig=(kc == 4))
        bs = h % 2
        typ = {0: 0, 1: 1, 14: 3, 15: 4}.get(ml, 2)
        tt = P.op("dve", lambda g, sc=sc, u2=u2, bs=bs, typ=typ: g.tensor_tensor(
            out=B.tmp[u2][:], in0=C.psum[:, sc: sc + 640], in1=B.bias[bs][:, typ, :], op=ALU.add),
            waits=[lastS, btok[h], tmp_free[u2]])
        s_free[us] = tt
        b_free[bs] = tt
        te = P.op("act", lambda g, u2=u2: g.activation(out=B.pT[u2][:], in_=B.tmp[u2][:], func=AF.Exp),
                  waits=[tt, pT_free[u2]])
        tmp_free[u2] = te
        st[i] = te

    def emit_rest(i):
        h, ml = units[i]
        u2 = i % NB
        us = i % 2
        oc = 2048 + us * 512
        os_ = h % 2
        te = st.pop(i)
        for kc in range(5):
            w = [te, t_ones, ob_free[us]] if kc == 0 else ()
            P.op("pe", lambda g, oc=oc, kc=kc, ml=ml, h=h, u2=u2: g.matmul(
                C.psum[0:64, oc: oc + 128], lhsT=B.v[:, ml + kc, h * 64:(h + 1) * 64],
                rhs=B.pT[u2][:, kc * 128:(kc + 1) * 128], start=(kc == 0), stop=(kc == 4)), waits=w, sig=False)
        lastO = None
        for kc in range(5):
            lastO = P.op("pe", lambda g, oc=oc, kc=kc, u2=u2: g.matmul(
                C.psum[0:64, oc + 128: oc + 256], lhsT=B.ones[:],
                rhs=B.pT[u2][:, kc * 128:(kc + 1) * 128], start=(kc == 0), stop=(kc == 4)), sig=(kc == 4))
        pT_free[u2] = lastO
        tr = P.op("dve", lambda g, oc=oc, u2=u2: g.reciprocal(out=B.rec[u2][:], in_=C.psum[0:64, oc + 128: oc + 256]),
                  waits=[lastO, rec_free[u2]])
        lastw = P.op("dve", lambda g, oc=oc, u2=u2, os_=os_, ml=ml: g.tensor_tensor(
            out=B.ost[os_][:, ml * 128:(ml + 1) * 128], in0=C.psum[0:64, oc: oc + 128],
            in1=B.rec[u2][:], op=ALU.mult), waits=[tr, o_free[os_]] if ml == 0 else [tr])
        rec_free[u2] = lastw
        ob_free[us] = lastw
        if ml == 15:
            o_free[os_] = P.dma("sp", oT_d[h * 64:(h + 1) * 64, :], B.ost[os_], B.o_sem[os_], waits=[lastw])
            out_toks.append(o_free[os_])

    load_bias(0)
    n = len(units)
    for i in range(n + SKEW):
        if i < n:
            emit_S(i)
        if filler is not None and i % 5 in (1, 3):
            next(filler, None)
        if i >= SKEW:
            emit_rest(i - SKEW)
    if filler is not None:
        for _ in filler:
            pass
    return out_toks


class FNBufs:
    def __init__(self, P):
        self.ld = P.dsem()
        self.e_sem = [P.dsem() for _ in range(2)]
        self.st = P.dsem()

    def carve(self, A):
        self.u = A.take([128, 64, 128], BF16)
        self.E = [A.take([128, 8, 256], BF16) for i in range(2)]
        self.Bsb = A.take([128, 256, 64], BF16)
        self.R = A.take([128, 512], BF16)
        self.CS = A.take([128, 128], BF16)
        self.G = [A.take([128, 4, 256], BF16) for i in range(2)]
        self.YT = A.take([128, 64, 128], BF16)


def emit_fnet_gen(C, B, u_d, E_d, R_d, CS_d, YT_d, in_ready=None, fixed_pair=None):
    P = C.P
    P.dma("sp", B.u[:].rearrange("p a b -> p (a b)"), u_d, B.ld, waits=[in_ready])
    P.dma("sp", B.R[:], R_d, B.ld)
    ld = P.dma("sp", B.CS[:], CS_d, B.ld)
    e_free = [None, None]
    unit = 0
    s1_act = s1_dve = None
    for ec in range(8):
        es = ec % 2
        etok = P.dma("sp", B.E[es][:].rearrange("p a b -> p (a b)"), E_d[:, ec * 2048:(ec + 1) * 2048], B.e_sem[es],
                     waits=[e_free[es]])
        for half in range(2):
            pb = fixed_pair if fixed_pair is not None else 2 * (unit % 4)
            unit += 1
            last = None
            for ci in range(4):
                cl = half * 4 + ci
                c = ec * 8 + cl
                w = [ld, etok, C.bank_free[pb], C.bank_free[pb + 1]] if ci == 0 else ()
                last = P.op("pe", lambda g, pb=pb, ci=ci, c=c, cl=cl, es=es: g.matmul(
                    C.psum[:, pb * 512 + ci * 256: pb * 512 + (ci + 1) * 256], lhsT=B.u[:, c, :], rhs=B.E[es][:, cl, :],
                    start=True, stop=True), waits=w, sig=(ci == 3))
            c0 = ec * 8 + half * 4
            if unit % 2 == 0:
                te = P.op("act", lambda g, pb=pb, c0=c0: g.activation(
                    out=B.Bsb[:, :, c0:c0 + 4].rearrange("p n c -> p c n"),
                    in_=C.bank(pb, 2).rearrange("p (c n) -> p c n", c=4), func=AF.Copy), waits=[last])
            else:
                te = P.op("dve", lambda g, pb=pb, c0=c0: g.tensor_copy(
                    out=B.Bsb[:, :, c0:c0 + 4].rearrange("p n c -> p c n"),
                    in_=C.bank(pb, 2).rearrange("p (c n) -> p c n", c=4)), waits=[last])
            C.bank_free[pb] = te
            C.bank_free[pb + 1] = te
            if unit % 2 == 0:
                s1_act = te
            else:
                s1_dve = te
            yield
        e_free[es] = last
    g_free = [None, None]
    y_tok = None
    import os
    nst = int(os.environ.get("FN_STAGES", "3"))
    for kq in range(16 if nst >= 2 else 0):
        gs = kq % 2
        pb = fixed_pair if fixed_pair is not None else 2 * (unit % 4)
        unit += 1
        last = None
        for ki in range(4):
            kp = kq * 4 + ki
            for part in range(2):
                w = [s1_act, s1_dve, C.bank_free[pb], C.bank_free[pb + 1]] if (ki == 0 and part == 0) else ()
                lhs = B.Bsb[:, part * 128 + 2 * kp: part * 128 + 2 * kp + 2, :].rearrange("p k c -> p (k c)")
                last = P.op("pe", lambda g, pb=pb, ki=ki, part=part, lhs=lhs: g.matmul(
                    C.psum[:, pb * 512 + ki * 256: pb * 512 + (ki + 1) * 256], lhsT=lhs,
                    rhs=B.R[:, part * 256:(part + 1) * 256], start=(part == 0), stop=(part == 1)),
                    waits=w, sig=(ki == 3 and part == 1))
        if kq % 2 == 0:
            te = P.op("act", lambda g, pb=pb, gs=gs: g.activation(out=B.G[gs][:].rearrange("p a b -> p (a b)"), in_=C.bank(pb, 2), func=AF.Copy),
                      waits=[last, g_free[gs]])
        else:
            te = P.op("dve", lambda g, pb=pb, gs=gs: g.tensor_copy(out=B.G[gs][:].rearrange("p a b -> p (a b)"), in_=C.bank(pb, 2)),
                      waits=[last, g_free[gs]])
        C.bank_free[pb] = te
        C.bank_free[pb + 1] = te
        if nst < 3:
            y_tok = te
            continue
        yield
        yb = fixed_pair if fixed_pair is not None else 2 * (unit % 4)
        unit += 1
        last3 = None
        for ki in range(4):
            for k2 in range(2):
                po = k2 * 64
                for part in range(2):
                    w = [te, C.bank_free[yb], C.bank_free[yb + 1]] if (ki == 0 and k2 == 0 and part == 0) else ()
                    last3 = P.op("pe", lambda g, yb=yb, gs=gs, ki=ki, k2=k2, po=po, part=part: g.matmul(
                        C.psum[:, (yb + k2) * 512 + ki * 64: (yb + k2) * 512 + (ki + 1) * 64],
                        lhsT=B.G[gs][po:po + 64, ki, part * 128:(part + 1) * 128],
                        rhs=B.CS[po:po + 64, part * 64:(part + 1) * 64], start=(part == 0), stop=(part == 1)),
                        waits=w, sig=(ki == 3 and k2 == 1 and part == 1))
        g_free[gs] = last3
        kr0 = kq * 8
        ty = P.op("dve", lambda g, yb=yb, kr0=kr0: g.tensor_copy(
            out=B.YT[:, :, kr0:kr0 + 8].rearrange("p x (a k) -> p x a k", k=2),
            in_=C.bank(yb, 2).rearrange("p (k a x) -> p k a x", k=2, a=8)[:, :, 0:4, :].rearrange("p k a x -> p x a k")),
            waits=[last3])
        C.bank_free[yb] = ty
        C.bank_free[yb + 1] = ty
        y_tok = ty
        yield
    out = P.dma("sp", YT_d, B.YT[:].rearrange("p a b -> p (a b)"), B.st, waits=[y_tok, s1_act, s1_dve])
    B.out_toks = [out]


def emit_fnet(C, B, *a, **k):
    for _ in emit_fnet_gen(C, B, *a, **k):
        pass
    return B.out_toks


class OutBufs:
    def __init__(self, P):
        self.ld = P.dsem()
        self.wld = P.dsem()
        self.in_sem = [P.dsem() for _ in range(2)]

    def carve(self, A):
        self.oT = [A.take([128, 4, 512], BF16) for i in range(2)]
        self.yfT = [A.take([128, 4, 512], BF16) for i in range(2)]
        self.wna = A.take([128, 4, D], BF16)
        self.wf = A.take([128, 4, D], BF16)
        self.wo = A.take([128, 8, D], BF16)
        self.wg = A.take([128, 8, 2 * D], BF16)
        self.gb = A.take([128, 2, 8], F32)
        self.mT = A.take([128, 8, 512], BF16)
        self.sga = [A.take([128, 512], F32) for i in range(2)]
        self.sgf = [A.take([128, 512], F32) for i in range(2)]


def emit_outproj(C, B, oT_d, yfT_d, w_in, w_na, w_f, w_o, gb_d, in_ready=None, h_ready=None, x_ready=None):
    P = C.P
    ld = P.dma("sp", B.gb.rearrange("p a b -> p (a b)"), gb_d, B.ld, waits=[in_ready])
    P.dma("pool", B.wna, w_na.rearrange("(m p) d -> p m d", p=128), B.wld)
    P.dma("pool", B.wf, w_f.rearrange("(g p) d -> p g d", p=128), B.wld)
    wl = None
    for q in range(4):
        wl = P.dma("pool", B.wg[:, :, q * 512:(q + 1) * 512],
                   w_in[:, 2048 + q * 512:2048 + (q + 1) * 512].rearrange("(c p) f -> p c f", p=128), B.wld)
    wl = P.dma("pool", B.wo, w_o.rearrange("(c p) d -> p c d", p=128), B.wld)
    in_free = [None, None]
    s_free = [None, None]
    m_free = None
    x_tok = None
    unit = 0

    def load_in(t):
        k = t % 2
        tsl = slice(t * 512, (t + 1) * 512)
        P.dma("sp", B.oT[k], oT_d[:, tsl].rearrange("(m p) t -> p m t", p=128), B.in_sem[k], waits=[in_ready, in_free[k]])
        return P.dma("sp", B.yfT[k], yfT_d[:, tsl].rearrange("(g p) t -> p g t", p=128), B.in_sem[k])

    in_tok = {0: load_in(0)}
    for t in range(4):
        ts = slice(t * 512, (t + 1) * 512)
        k = t % 2
        if t + 1 < 4:
            in_tok[t + 1] = load_in(t + 1)
        d3 = None
        for dch in range(8):
            pb = 4 * (unit % 2)
            k2 = unit % 2
            unit += 1
            dsl = slice(dch * 128, (dch + 1) * 128)
            w0 = [ld, wl, in_tok[t], h_ready[t] if isinstance(h_ready, list) else h_ready] + [C.bank_free[pb + q] for q in range(4)]
            for mm in range(4):
                P.op("pe", lambda g, pb=pb, mm=mm, dsl=dsl, k=k: g.matmul(
                    C.bank(pb), lhsT=B.wna[:, mm, dsl], rhs=B.oT[k][:, mm, :], start=(mm == 0), stop=(mm == 3)),
                    waits=w0 if mm == 0 else (), sig=False)
            for d in range(8):
                tB = P.op("pe", lambda g, pb=pb, d=d, dsl=dsl, ts=ts: g.matmul(
                    C.bank(pb + 1), lhsT=B.wg[:, d, dsl], rhs=C.hT[:, d, ts], start=(d == 0), stop=(d == 7)), sig=(d == 7))
            for gg in range(4):
                P.op("pe", lambda g, pb=pb, gg=gg, dsl=dsl, k=k: g.matmul(
                    C.bank(pb + 2), lhsT=B.wf[:, gg, dsl], rhs=B.yfT[k][:, gg, :], start=(gg == 0), stop=(gg == 3)), sig=False)
            for d in range(8):
                tD = P.op("pe", lambda g, pb=pb, d=d, dch=dch, ts=ts: g.matmul(
                    C.bank(pb + 3), lhsT=B.wg[:, d, D + dch * 128:D + (dch + 1) * 128], rhs=C.hT[:, d, ts],
                    start=(d == 0), stop=(d == 7)), sig=(d == 7))
            sa = P.op("act", lambda g, pb=pb, k2=k2, dch=dch: g.activation(
                out=B.sga[k2], in_=C.bank(pb + 1), func=AF.Sigmoid, bias=B.gb[:, 0, dch:dch + 1], scale=1.0),
                waits=[tB, s_free[k2]])
            sf = P.op("act", lambda g, pb=pb, k2=k2, dch=dch: g.activation(
                out=B.sgf[k2], in_=C.bank(pb + 3), func=AF.Sigmoid, bias=B.gb[:, 1, dch:dch + 1], scale=1.0),
                waits=[tD])
            d1 = P.op("dve", lambda g, pb=pb, k2=k2: g.tensor_tensor(out=B.sga[k2], in0=C.bank(pb), in1=B.sga[k2], op=ALU.mult),
                      waits=[sa, tD])
            d2 = P.op("dve", lambda g, pb=pb, k2=k2: g.tensor_tensor(out=B.sgf[k2], in0=C.bank(pb + 2), in1=B.sgf[k2], op=ALU.mult),
                      waits=[sf, d1])
            for q in range(4):
                C.bank_free[pb + q] = d2
            d3 = P.op("pool", lambda g, k2=k2, dch=dch: g.tensor_tensor(out=B.mT[:, dch, :], in0=B.sga[k2], in1=B.sgf[k2], op=ALU.add),
                      waits=[d2, m_free] if dch == 0 else [d2])
            s_free[k2] = d3
        in_free[k] = tD
        C.h_free = tD
        for dq in range(2):
            pb = 4 * (unit % 2)
            unit += 1
            last = None
            for di in range(4):
                do = dq * 4 + di
                for dch in range(8):
                    w = [d3] + [C.bank_free[pb + q] for q in range(4)] if (di == 0 and dch == 0) else ()
                    last = P.op("pe", lambda g, pb=pb, di=di, dch=dch, do=do: g.matmul(
                        C.bank(pb + di), lhsT=B.wo[:, dch, do * 128:(do + 1) * 128], rhs=B.mT[:, dch, :],
                        start=(dch == 0), stop=(dch == 7)), waits=w, sig=(di == 3 and dch == 7))
            for di in range(4):
                do = dq * 4 + di
                x_tok = P.op("dve", lambda g, pb=pb, di=di, do=do, ts=ts: g.tensor_tensor(
                    out=C.xT[:, do, ts], in0=C.bank(pb + di), in1=C.xT[:, do, ts], op=ALU.add),
                    waits=[last, x_ready[t] if isinstance(x_ready, list) else x_ready])
                C.bank_free[pb + di] = x_tok
            m_free = last
    return x_tok


def emit_final_norm(C, gi, out_d):
    P = C.P
    A = C.arena
    st = [A.take([128, 8, 512], F32) for _ in range(2)]
    sem = [C.fin_sem0, C.fin_sem1]
    free = [None, None]
    toks = []

    def out_fn(c, t, ts, r, t2):
        k = t % 2
        tk = P.op("dve", lambda g, c=c, ts=ts, r=r, k=k: g.scalar_tensor_tensor(
            out=st[k][:, c, :], in0=C.xT[:, c, ts], scalar=C.gains[:, gi, c:c + 1], in1=r[:],
            op0=ALU.mult, op1=ALU.mult), waits=[t2, free[k], C.gains_tok] if c == 0 else (), sig=(c == 7))
        if c == 7:
            free[k] = P.dma("sp", out_d[:, ts].rearrange("(c p) t -> p c t", p=128), st[k], sem[k], waits=[tk])
            toks.append(free[k])
        return tk

    emit_norm(C, gi, None, out_fn=out_fn)
    return toks


def _load_xh(P, C, x_d, h_d, g_d):
    C.gains_tok = P.dma("sp", C.gains[:].rearrange("p a b -> p (a b)"), g_d, P.dsem())
    h_toks = None
    if h_d is not None:
        h_toks = []
        for t in range(4):
            ts = slice(t * 512, (t + 1) * 512)
            h_toks.append(P.dma("sp", C.hT[:, :, ts], h_d[:, ts].rearrange("(c p) t -> p c t", p=128), P.dsem()))
    x_toks = []
    for t in range(4):
        ts = slice(t * 512, (t + 1) * 512)
        x_toks.append(P.dma("sp", C.xT[:, :, ts], x_d[:, ts].rearrange("(c p) t -> p c t", p=128), P.dsem()))
    return x_toks, h_toks


def _dram(nc, name, shape, dt, kind):
    return nc.dram_tensor(name, list(shape), dt, kind=kind).ap()


def _stage_front(nc, P, C, gi0, x_ready, first):
    toks = []
    w1_in = _dram(nc, "w1_in", [D, 2 * DFF], F32, "ExternalInput")
    w1_out = _dram(nc, "w1_out", [DFF, D], F32, "ExternalInput")
    wmix = _dram(nc, "wmixb", [D, 4096], F32, "ExternalInput")
    qT = _dram(nc, "qT", [512, NTOK], BF16, "ExternalOutput")
    kT = _dram(nc, "kT", [512, NTOK], BF16, "ExternalOutput")
    v = _dram(nc, "v", [NTOK, 512], BF16, "ExternalOutput")
    uf = _dram(nc, "uf", [NTOK, 512], BF16, "ExternalOutput")
    xo = _dram(nc, "xT_out", [D, NTOK], F32, "ExternalOutput")
    ho = _dram(nc, "hT_out", [D, NTOK], BF16, "ExternalOutput")
    xs_sem = P.dsem()
    hs_sem = P.dsem()
    if not first:
        P.barrier()
    C.arena.reset()
    h = emit_norm(C, gi0, x_ready)
    C.ffn.carve(C.arena)

    def save_x(dch, tok):
        toks.append(P.dma("sp", xo[dch * 128:(dch + 1) * 128, :], C.xT[:, dch, :], xs_sem, waits=[tok]))

    emit_ffn(C, C.ffn, w1_in, w1_out, h, on_x_final=save_x)
    P.barrier()
    C.arena.reset()

    def save_h(t, ts, tok):
        toks.append(P.dma("sp", ho[:, ts].rearrange("(c p) t -> p c t", p=128), C.hT[:, :, ts], hs_sem, waits=[tok]))

    h = emit_norm(C, gi0 + 1, None, on_chunk=save_h)
    C.inp.carve(C.arena)
    toks += emit_inproj(C, C.inp, wmix, h, qT, kT, v, uf)
    return toks


def build_A():
    nc = bass.Bass("TRN2", target_bir_lowering=False)
    x_d = _dram(nc, "xT_in", [D, NTOK], F32, "ExternalInput")
    g_d = _dram(nc, "gains", [128, 56], F32, "ExternalInput")
    P = Prog(nc)
    C = Ctx(P)
    C.ffn = FFNBufs(P)
    C.inp = InprojBufs(P)
    x_toks, _ = _load_xh(P, C, x_d, None, g_d)
    toks = _stage_front(nc, P, C, 0, x_toks, True)
    P.wait("sp", toks)
    P.emit()
    return nc


def build_NF():
    nc = bass.Bass("TRN2", target_bir_lowering=False)
    qT = _dram(nc, "qT", [512, NTOK], BF16, "ExternalInput")
    kTe = _dram(nc, "kTe", [512, 2560], BF16, "ExternalInput")
    ve = _dram(nc, "ve", [2560, 512], BF16, "ExternalInput")
    bias = _dram(nc, "bias", [8, 128, 3200], F32, "ExternalInput")
    u = _dram(nc, "u", [128, 8192], BF16, "ExternalInput")
    E = _dram(nc, "E", [128, 16384], BF16, "ExternalInput")
    R = _dram(nc, "R", [128, 512], BF16, "ExternalInput")
    CS = _dram(nc, "CS", [128, 128], BF16, "ExternalInput")
    oT = _dram(nc, "oT", [512, NTOK], BF16, "ExternalOutput")
    YT = _dram(nc, "YT", [128, 8192], BF16, "ExternalOutput")
    P = Prog(nc)
    C = Ctx(P, with_x=False, arena=92000)
    na = NABufs(P)
    fn = FNBufs(P)
    import os
    which = os.environ.get("NF_ONLY", "both")
    C.arena.reset()
    if which == "both":
        na.carve(C.arena)
        fn.carve(C.arena)
        gen = emit_fnet_gen(C, fn, u, E, R, CS, YT, fixed_pair=6)
        toks = emit_na(C, na, qT, kTe, ve, bias, oT, filler=gen)
        toks = toks + fn.out_toks
    elif which == "na":
        na.carve(C.arena)
        toks = emit_na(C, na, qT, kTe, ve, bias, oT)
    else:
        fn.carve(C.arena)
        toks = emit_fnet(C, fn, u, E, R, CS, YT)
    P.wait("sp", toks)
    P.emit()
    return nc


def build_E(final, gi0):
    nc = bass.Bass("TRN2", target_bir_lowering=False)
    x_d = _dram(nc, "xT_in", [D, NTOK], F32, "ExternalInput")
    h_d = _dram(nc, "hT_in", [D, NTOK], BF16, "ExternalInput")
    g_d = _dram(nc, "gains", [128, 56], F32, "ExternalInput")
    oT = _dram(nc, "oT", [512, NTOK], BF16, "ExternalInput")
    yfT = _dram(nc, "yfT", [512, NTOK], BF16, "ExternalInput")
    wmixa = _dram(nc, "wmixa", [D, 4096], F32, "ExternalInput")
    w_na = _dram(nc, "w_na", [512, D], F32, "ExternalInput")
    w_f = _dram(nc, "w_f", [512, D], F32, "ExternalInput")
    w_o = _dram(nc, "w_o", [D, D], F32, "ExternalInput")
    gb = _dram(nc, "gb", [128, 16], F32, "ExternalInput")
    w2_in = _dram(nc, "w2_in", [D, 2 * DFF], F32, "ExternalInput")
    w2_out = _dram(nc, "w2_out", [DFF, D], F32, "ExternalInput")
    P = Prog(nc)
    C = Ctx(P)
    C.ffn = FFNBufs(P)
    C.inp = InprojBufs(P)
    ob = OutBufs(P)
    x_toks, h_toks = _load_xh(P, C, x_d, h_d, g_d)
    C.arena.reset()
    ob.carve(C.arena)
    emit_outproj(C, ob, oT, yfT, wmixa, w_na, w_f, w_o, gb, h_ready=h_toks, x_ready=x_toks)
    P.barrier()
    C.arena.reset()
    h = emit_norm(C, gi0 + 2, None)
    C.ffn.carve(C.arena)
    emit_ffn(C, C.ffn, w2_in, w2_out, h)
    if final:
        out = _dram(nc, "outT", [D, NTOK], F32, "ExternalOutput")
        P.barrier()
        C.arena.reset()
        toks = emit_final_norm(C, 6, out)
    else:
        toks = _stage_front(nc, P, C, gi0 + 3, None, False)
    P.wait("sp", toks)
    P.emit()
    return nc


import ml_dtypes
BF = ml_dtypes.bfloat16
NEG = -30000.0


def _tables():
    r = np.arange(128, dtype=np.int64)[:, None, None]
    c = np.arange(64, dtype=np.int64)[None, :, None]
    kr = np.arange(128, dtype=np.int64)[None, None, :]
    th = 2 * np.pi * ((kr * (64 * r + c)) % 8192) / 8192.0
    E = np.concatenate([np.cos(th), -np.sin(th)], axis=2) * 2.0 ** -10
    ch = np.arange(128, dtype=np.int64)
    th2 = 2 * np.pi * ((ch[:, None] * ch[None, :]) % 128) / 128.0
    Cm, Sm = np.cos(th2), np.sin(th2)
    R = np.concatenate([Cm, -Sm, Sm, Cm], axis=1)
    cc = np.arange(64, dtype=np.int64)
    th3 = 2 * np.pi * ((cc[:, None] * cc[None, :]) % 64) / 64.0
    CS64 = np.concatenate([np.cos(th3), np.sin(th3)], axis=1)
    CS = np.concatenate([CS64, CS64], axis=0)
    return (np.ascontiguousarray(E.reshape(128, 64 * 256)).astype(BF), R.astype(BF), CS.astype(BF))


def _ext_chunk(jj, e):
    gc = jj * 16 + e - 2
    if gc < 0:
        return 3 if e == 0 else None
    if gc > 63:
        return 60 if e == 19 else None
    return gc


def _bias_tiles(rpb, jj):
    out = np.full((8, 128, 5, 640), NEG, np.float32)
    qi = np.arange(2)[None, None, :, None]
    qc = np.arange(64)[None, None, None, :]
    ki = np.arange(2)[:, None, None, None]
    kcl = np.arange(64)[None, :, None, None]
    ws = np.clip(qc - 8, 0, 48)
    col_ok = (kcl >= ws) & (kcl < ws + 16)
    dc = np.clip(kcl - qc, -15, 15) + 15
    for typ, ml in ((0, 0), (1, 1), (2, 5), (3, 14), (4, 15)):
        m = jj * 16 + ml
        r = 2 * m + qi
        rs = np.clip(r - 4, 0, 120)
        for kc in range(5):
            gc = _ext_chunk(jj, ml + kc)
            if gc is None:
                continue
            krow = 2 * gc + ki
            ok = (krow >= rs) & (krow < rs + 8) & col_ok
            dr = np.clip(krow - r + 7, 0, 14)
            drb = np.broadcast_to(dr, ok.shape)
            dcb = np.broadcast_to(dc, ok.shape)
            vals = rpb[:, drb, dcb]
            tile = np.where(ok[None], vals, np.float32(NEG)).reshape(8, 128, 128)
            out[:, :, typ, kc * 128:(kc + 1) * 128] = tile
    return np.ascontiguousarray(out.reshape(8, 128, 3200))


def _gain_layout(g):
    n = g.shape[0]
    return np.ascontiguousarray(g.reshape(n, 8, 128).transpose(2, 0, 1).reshape(128, n * 8)).astype(np.float32)


def _run(nc, in_maps):
    res = run_bass_kernel_spmd(nc, in_maps, core_ids=list(range(8)))
    return res.results


def _exchange_front(outs, rpb_l, tabs):
    E, R, CS = tabs
    ims = []
    for core in range(8):
        b, jj = divmod(core, 4)
        kT_all = np.concatenate([outs[b * 4 + q]["kT"] for q in range(4)], axis=1)
        v_all = np.concatenate([outs[b * 4 + q]["v"] for q in range(4)], axis=0)
        uf_all = np.concatenate([outs[b * 4 + q]["uf"] for q in range(4)], axis=0)
        kTe = np.zeros((512, 2560), BF)
        ve = np.zeros((2560, 512), BF)
        for e in range(20):
            gc = _ext_chunk(jj, e)
            if gc is None:
                continue
            kTe[:, e * 128:(e + 1) * 128] = kT_all[:, gc * 128:(gc + 1) * 128]
            ve[e * 128:(e + 1) * 128, :] = v_all[gc * 128:(gc + 1) * 128, :]
        g = jj
        u = np.ascontiguousarray(uf_all[:, g * 128:(g + 1) * 128]).reshape(128, 64 * 128)
        ims.append({"qT": outs[core]["qT"], "kTe": kTe, "ve": ve, "bias": _bias_tiles(rpb_l, jj),
                    "u": u, "E": E, "R": R, "CS": CS})
    return ims


def kernel(x, ffn1_norm, ffn1_w_in, ffn1_w_out, mix_norm, mix_w_in, mix_gate_bias, na_rpb, na_w_out,
           f_w_out, mix_w_o, ffn2_norm, ffn2_w_in, ffn2_w_out, final_norm):
    f32 = lambda a: np.ascontiguousarray(np.asarray(a, dtype=np.float32))
    x = f32(x)
    gains = _gain_layout(np.stack([f32(ffn1_norm)[0], f32(mix_norm)[0], f32(ffn2_norm)[0],
                                   f32(ffn1_norm)[1], f32(mix_norm)[1], f32(ffn2_norm)[1], f32(final_norm)]))
    tabs = _tables()
    rpb = f32(na_rpb)
    ims = []
    for core in range(8):
        b, jj = divmod(core, 4)
        ims.append({"xT_in": np.ascontiguousarray(x[b, jj * NTOK:(jj + 1) * NTOK, :].T), "gains": gains,
                    "w1_in": f32(ffn1_w_in[0]), "w1_out": f32(ffn1_w_out[0]), "wmixb": f32(mix_w_in[0])})
    front = _run(build_A(), ims)
    out = None
    for l in range(L):
        nf = _run(build_NF(), _exchange_front(front, rpb[l], tabs))
        ims = []
        gb = _gain_layout(f32(mix_gate_bias[l]))
        for core in range(8):
            b, jj = divmod(core, 4)
            yfT = np.concatenate([nf[b * 4 + g]["YT"][:, jj * NTOK:(jj + 1) * NTOK] for g in range(4)], axis=0)
            im = {"xT_in": front[core]["xT_out"], "hT_in": front[core]["hT_out"], "gains": gains,
                  "oT": nf[core]["oT"], "yfT": np.ascontiguousarray(yfT), "wmixa": f32(mix_w_in[l]),
                  "w_na": f32(na_w_out[l]), "w_f": f32(f_w_out[l]), "w_o": f32(mix_w_o[l]), "gb": gb,
                  "w2_in": f32(ffn2_w_in[l]), "w2_out": f32(ffn2_w_out[l])}
            if l + 1 < L:
                im.update({"w1_in": f32(ffn1_w_in[l + 1]), "w1_out": f32(ffn1_w_out[l + 1]),
                           "wmixb": f32(mix_w_in[l + 1])})
            ims.append(im)
        res = _run(build_E(final=(l + 1 == L), gi0=3 * l), ims)
        if l + 1 < L:
            front = res
        else:
            out = np.empty((2, 8192, D), np.float32)
            for core in range(8):
                b, jj = divmod(core, 4)
                out[b, jj * NTOK:(jj + 1) * NTOK, :] = res[core]["outT"].T
    return out
```

```python
import numpy as np
from contextlib import ExitStack
import concourse.bass as bass
import concourse.mybir as mybir
from concourse.bass_utils import run_bass_kernel_spmd

F32 = mybir.dt.float32
BF16 = mybir.dt.bfloat16
AF = mybir.ActivationFunctionType
ALU = mybir.AluOpType

D = 1024
NTOK = 2048
DFF = 2816
NJ = 22
JG = 11
EPS = 1e-6
L = 2


class Eng:
    def __init__(self, name):
        self.name = name
        self.ops = []
        self.cnt = 0
        self.sem = None
        self.seen = {}
        self.last_sig = True


class DSem:
    def __init__(self, sem):
        self.sem = sem
        self.cnt = 0


class Prog:
    def __init__(self, nc):
        self.nc = nc
        self.stack = ExitStack()
        self.eng = {n: Eng(n) for n in ("pe", "act", "dve", "pool", "sp")}
        for n, e in self.eng.items():
            e.sem = self.stack.enter_context(nc.semaphore("s_" + n))
        self.nds = 0

    def sb(self, name, shape, dt):
        return self.stack.enter_context(self.nc.sbuf_tensor("sb_" + name, list(shape), dt))

    def ps(self, name, shape, dt=F32):
        return self.stack.enter_context(self.nc.psum_tensor("ps_" + name, list(shape), dt))

    def dsem(self):
        self.nds += 1
        return DSem(self.stack.enter_context(self.nc.semaphore("d%d" % self.nds)))

    def _waits(self, e, waits):
        for tok in waits:
            if tok is None:
                continue
            sem, val = tok
            if e.seen.get(id(sem), 0) >= val:
                continue
            e.seen[id(sem)] = val
            e.ops.append(("wait", sem, val))

    def op(self, eng, fn, waits=(), sig=True):
        e = self.eng[eng]
        self._waits(e, waits)
        e.last_sig = sig
        if sig:
            e.cnt += 1
            e.ops.append(("op", fn, e.sem, 1))
            return (e.sem, e.cnt)
        e.ops.append(("op", fn, None, 0))
        return None

    def dma(self, queue, out, in_, dsem, waits=()):
        e = self.eng[queue]
        self._waits(e, waits)
        dsem.cnt += 16
        e.ops.append(("op", lambda g: g.dma_start(out=out, in_=in_), dsem.sem, 16))
        return (dsem.sem, dsem.cnt)

    def wait(self, eng, waits):
        self._waits(self.eng[eng], waits)

    def barrier(self, extra=()):
        toks = list(extra)
        for n in ("pe", "act", "dve", "pool"):
            e = self.eng[n]
            if e.cnt:
                assert e.last_sig, "engine %s: last op before barrier must signal" % n
                toks.append((e.sem, e.cnt))
        for n in ("pe", "act", "dve", "pool", "sp"):
            self._waits(self.eng[n], toks)

    def emit(self):
        def replay(e, g):
            for it in e.ops:
                if it[0] == "wait":
                    g.wait_ge(it[1], it[2])
                else:
                    ins = it[1](g)
                    if it[2] is not None:
                        ins.then_inc(it[2], it[3])

        with self.nc.Block() as block:
            @block.tensor
            def _(g):
                replay(self.eng["pe"], g)

            @block.scalar
            def _(g):
                replay(self.eng["act"], g)

            @block.vector
            def _(g):
                replay(self.eng["dve"], g)

            @block.gpsimd
            def _(g):
                replay(self.eng["pool"], g)

            @block.sync
            def _(g):
                replay(self.eng["sp"], g)
        self.stack.close()


ARENA = 51500


class Arena:
    def __init__(self, P, n=ARENA):
        self.t = P.sb("arena", [128, n], BF16)
        self.n = n
        self.off = 0

    def reset(self):
        self.off = 0

    def take(self, shape, dt):
        n_el = 1
        for k in shape[1:]:
            n_el *= k
        units = n_el * (2 if dt == F32 else 1)
        units = (units + 15) // 16 * 16
        assert self.off + units <= self.n, ("arena overflow", self.off, units, self.n)
        v = self.t[0:shape[0], self.off:self.off + units]
        self.off += units
        if dt == F32:
            v = v.bitcast(F32)
        v = v[:, 0:n_el]
        if len(shape) == 3:
            v = v.rearrange("p (a b) -> p a b", a=shape[1])
        elif len(shape) == 4:
            v = v.rearrange("p (a b c) -> p a b c", a=shape[1], b=shape[2])
        return v


class Ctx:
    def __init__(self, P, with_x=True, arena=ARENA):
        self.P = P
        if with_x:
            self.xT = P.sb("xT", [128, 8, NTOK], F32)
            self.hT = P.sb("hT", [128, 8, NTOK], BF16)
        self.psum = P.ps("psum", [128, 4096], F32)
        self.ones = P.sb("ones", [128, 128], BF16)
        self.gains = P.sb("gains", [128, 7, 8], F32)
        self.arena = Arena(P, arena)
        self.sq = None
        self.rstd = [P.sb("rstd%d" % i, [128, 512], F32) for i in range(2)]
        self.bank_free = [None] * 8
        self.x_tok = None
        self.h_tok = None
        self.h_free = None
        self.sq_free = None
        self.rstd_free = [None, None]
        self.nnorm = 0
        self.gains_tok = None
        self.fin_sem0 = P.dsem()
        self.fin_sem1 = P.dsem()
        self.tok_ones = P.op("pool", lambda g: g.memset(self.ones[:], 1.0))

    def bank(self, b, n=1):
        return self.psum[:, b * 512:(b + n) * 512]


def emit_norm(C, gi, x_ready, out_fn=None, on_chunk=None):
    P = C.P
    toks = []
    C.sq = C.arena.take([128, 8, 512], BF16)
    C.sq_free = None
    for t in range(4):
        ts = slice(t * 512, (t + 1) * 512)
        xr = x_ready[t] if isinstance(x_ready, list) else x_ready
        r = C.rstd[C.nnorm % 2]
        k = C.nnorm % 2
        C.nnorm += 1
        tsq = P.op("act", lambda g, ts=ts: g.activation(out=C.sq, in_=C.xT[:, :, ts], func=AF.Square),
                   waits=[xr, C.sq_free])
        b = 7 if (t % 2) else 3
        for c in range(8):
            tk = P.op("pe", lambda g, c=c, b=b: g.matmul(C.bank(b), lhsT=C.ones[:], rhs=C.sq[:, c, :],
                                                         start=(c == 0), stop=(c == 7)),
                      waits=[tsq, C.tok_ones, C.bank_free[b]] if c == 0 else (), sig=(c == 7))
        C.sq_free = tk
        t1 = P.op("dve", lambda g, r=r, b=b: g.tensor_scalar(out=r[:], in0=C.bank(b), scalar1=1.0 / D, scalar2=EPS,
                                                             op0=ALU.mult, op1=ALU.add),
                  waits=[tk, C.rstd_free[k]])
        C.bank_free[b] = t1
        t1b = P.op("act", lambda g, r=r: g.activation(out=r[:], in_=r[:], func=AF.Sqrt), waits=[t1])
        t2 = P.op("dve", lambda g, r=r: g.reciprocal(out=r[:], in_=r[:]), waits=[t1b])
        last = None
        for c in range(8):
            if out_fn is None:
                last = P.op("dve", lambda g, c=c, ts=ts, r=r: g.scalar_tensor_tensor(
                    out=C.hT[:, c, ts], in0=C.xT[:, c, ts], scalar=C.gains[:, gi, c:c + 1], in1=r[:],
                    op0=ALU.mult, op1=ALU.mult), waits=[t2, C.h_free, C.gains_tok] if c == 0 else (), sig=(c == 7))
            else:
                last = out_fn(c, t, ts, r, t2)
        C.rstd_free[k] = last
        toks.append(last)
        if on_chunk is not None:
            on_chunk(t, ts, last)
    return toks


class FFNBufs:
    def __init__(self, P):
        self.wi_sem = [P.dsem() for _ in range(4)]
        self.wo_sem = [P.dsem() for _ in range(3)]
        self.nwi = 0
        self.nwo = 0
        self.nunit = 0

    def carve(self, A):
        self.aT = A.take([128, JG, NTOK], BF16)
        self.wi = [A.take([128, 8, 2, 128], BF16) for i in range(4)]
        self.wo = [A.take([128, JG, 256], BF16) for i in range(3)]
        self.sg = [A.take([128, 1024], F32) for i in range(2)]
        self.wi_free = [None] * 4
        self.wo_free = [None] * 3
        self.sg_free = [None] * 2
        self.aT_free = None


def emit_ffn(C, B, w_in, w_out, h_ready, on_x_final=None):
    P = C.P
    x_tok = None
    for grp in range(2):
        a_tok = None
        unit_tok = []
        for jl in range(JG):
            j = grp * JG + jl
            slot = B.nwi % 4
            B.nwi += 1
            src_g = w_in[:, j * 128:(j + 1) * 128].rearrange("(c p) f -> p c f", p=128)
            src_u = w_in[:, DFF + j * 128:DFF + (j + 1) * 128].rearrange("(c p) f -> p c f", p=128)
            P.dma("pool", B.wi[slot][:, :, 0, :], src_g, B.wi_sem[slot], waits=[B.wi_free[slot]])
            wtok = P.dma("pool", B.wi[slot][:, :, 1, :], src_u, B.wi_sem[slot])
            for half in range(2):
                pb = 4 * (B.nunit % 2)
                sgi = B.nunit % 2
                B.nunit += 1
                last = None
                for which in range(2):
                    for d in range(8):
                        for t2 in range(2):
                            bk = pb + which * 2 + t2
                            first = (which == 0 and d == 0 and t2 == 0)
                            fin = (which == 1 and d == 7 and t2 == 1)
                            tsl = slice(half * 1024 + t2 * 512, half * 1024 + (t2 + 1) * 512)
                            hr = h_ready[2 * half + 1] if isinstance(h_ready, list) else h_ready
                            w = [wtok, hr, C.bank_free[pb], C.bank_free[pb + 1], C.bank_free[pb + 2],
                                 C.bank_free[pb + 3]] if first else ()
                            last = P.op("pe", lambda g, bk=bk, slot=slot, d=d, which=which, tsl=tsl: g.matmul(
                                C.bank(bk), lhsT=B.wi[slot][:, d, which, :], rhs=C.hT[:, d, tsl],
                                start=(d == 0), stop=(d == 7)), waits=w, sig=fin)
                if half == 1:
                    B.wi_free[slot] = last
                C.h_free = last
                ts = P.op("act", lambda g, pb=pb, sgi=sgi: g.activation(out=B.sg[sgi][:], in_=C.bank(pb, 2), func=AF.Silu),
                          waits=[last, B.sg_free[sgi]])
                hs = slice(half * 1024, (half + 1) * 1024)
                tm = P.op("dve", lambda g, pb=pb, sgi=sgi, jl=jl, hs=hs: g.tensor_tensor(
                    out=B.aT[:, jl, hs], in0=B.sg[sgi][:], in1=C.bank(pb + 2, 2), op=ALU.mult),
                    waits=[ts, last, B.aT_free])
                B.sg_free[sgi] = tm
                for k in range(4):
                    C.bank_free[pb + k] = tm
                a_tok = tm
        for dp in range(4):
            slot = B.nwo % 3
            B.nwo += 1
            src = w_out[grp * JG * 128:(grp + 1) * JG * 128, dp * 256:(dp + 1) * 256].rearrange("(j p) d -> p j d", p=128)
            wtok = P.dma("pool", B.wo[slot][:], src, B.wo_sem[slot], waits=[B.wo_free[slot]])
            for ds_ in range(2):
                dch = dp * 2 + ds_
                pb = 4 * (B.nunit % 2)
                B.nunit += 1
                last = None
                for jl in range(JG):
                    for t in range(4):
                        first = (jl == 0 and t == 0)
                        fin = (jl == JG - 1 and t == 3)
                        w = [wtok, a_tok, C.bank_free[pb], C.bank_free[pb + 1], C.bank_free[pb + 2],
                             C.bank_free[pb + 3]] if first else ()
                        last = P.op("pe", lambda g, pb=pb, t=t, slot=slot, jl=jl, ds_=ds_: g.matmul(
                            C.bank(pb + t), lhsT=B.wo[slot][:, jl, ds_ * 128:(ds_ + 1) * 128],
                            rhs=B.aT[:, jl, t * 512:(t + 1) * 512], start=(jl == 0), stop=(jl == JG - 1)),
                            waits=w, sig=fin)
                if ds_ == 1:
                    B.wo_free[slot] = last
                te = P.op("dve", lambda g, pb=pb, dch=dch: g.scalar_tensor_tensor(
                    out=C.xT[:, dch, :], in0=C.bank(pb, 4), scalar=0.5, in1=C.xT[:, dch, :],
                    op0=ALU.mult, op1=ALU.add), waits=[last])
                for k in range(4):
                    C.bank_free[pb + k] = te
                x_tok = te
                B.aT_free = last
                if grp == 1 and on_x_final is not None:
                    on_x_final(dch, te)
    return x_tok


class InprojBufs:
    def __init__(self, P):
        self.wq_sem = [P.dsem() for _ in range(4)]
        self.st_sem = [P.dsem() for _ in range(2)]
        self.nst = 0
        self.nunit = 0

    def carve(self, A):
        self.w = A.take([128, 8, 2048], BF16)
        self.st = [A.take([128, 2048], BF16) for i in range(2)]
        self.st_free = [None, None]


def emit_inproj(C, B, w_in, h_ready, qT_d, kT_d, v_d, uf_d, w_free=None):
    P = C.P
    wq = {}
    for q in (2, 3, 0, 1):
        wq[q] = P.dma("pool", B.w[:, :, q * 512:(q + 1) * 512],
                      w_in[:, q * 512:(q + 1) * 512].rearrange("(c p) f -> p c f", p=128), B.wq_sem[q], waits=[w_free])
    out_toks = []
    for which in range(2):
        for tq in range(4):
            pb = 4 * (B.nunit % 2)
            B.nunit += 1
            last = None
            for ti in range(4):
                tt = tq * 4 + ti
                for d in range(8):
                    first = (d == 0 and ti == 0)
                    w = [wq[2 + which], h_ready[tq] if isinstance(h_ready, list) else h_ready] + [C.bank_free[pb + k] for k in range(4)] if first else ()
                    last = P.op("pe", lambda g, pb=pb, ti=ti, d=d, tt=tt, which=which: g.matmul(
                        C.bank(pb + ti), lhsT=C.hT[:, d, tt * 128:(tt + 1) * 128],
                        rhs=B.w[:, d, 1024 + which * 512:1024 + (which + 1) * 512],
                        start=(d == 0), stop=(d == 7)), waits=w, sig=(d == 7 and ti == 3))
            si = B.nst % 2
            B.nst += 1
            if tq % 2 == 0:
                te = P.op("act", lambda g, pb=pb, si=si: g.activation(out=B.st[si][:], in_=C.bank(pb, 4), func=AF.Copy),
                          waits=[last, B.st_free[si]])
            else:
                te = P.op("dve", lambda g, pb=pb, si=si: g.tensor_copy(out=B.st[si][:], in_=C.bank(pb, 4)),
                          waits=[last, B.st_free[si]])
            for k in range(4):
                C.bank_free[pb + k] = te
            dst = (v_d if which == 0 else uf_d)[tq * 512:(tq + 1) * 512, :].rearrange("(i p) f -> p i f", p=128)
            B.st_free[si] = P.dma("sp", dst, B.st[si][:].rearrange("p (i f) -> p i f", i=4), B.st_sem[si], waits=[te])
            out_toks.append(B.st_free[si])
            C.h_free = last
    for m in range(8):
        pb = 4 * (B.nunit % 2)
        B.nunit += 1
        last = None
        for d in range(8):
            for t in range(4):
                first = (d == 0 and t == 0)
                w = [wq[m // 4], h_ready[3] if isinstance(h_ready, list) else h_ready] + [C.bank_free[pb + k] for k in range(4)] if first else ()
                last = P.op("pe", lambda g, pb=pb, t=t, d=d, m=m: g.matmul(
                    C.bank(pb + t), lhsT=B.w[:, d, m * 128:(m + 1) * 128], rhs=C.hT[:, d, t * 512:(t + 1) * 512],
                    start=(d == 0), stop=(d == 7)), waits=w, sig=(d == 7 and t == 3))
        si = B.nst % 2
        B.nst += 1
        if m < 4:
            te = P.op("dve", lambda g, pb=pb, si=si: g.tensor_scalar(out=B.st[si], in0=C.bank(pb, 4), scalar1=0.125, scalar2=None, op0=ALU.mult),
                      waits=[last, B.st_free[si]])
        else:
            te = P.op("act", lambda g, pb=pb, si=si: g.activation(out=B.st[si], in_=C.bank(pb, 4), func=AF.Copy),
                      waits=[last, B.st_free[si]])
        for k in range(4):
            C.bank_free[pb + k] = te
        dst = (qT_d if m < 4 else kT_d)[(m % 4) * 128:(m % 4 + 1) * 128, :]
        B.st_free[si] = P.dma("sp", dst, B.st[si][:], B.st_sem[si], waits=[te])
        out_toks.append(B.st_free[si])
    C.h_free = last
    return out_toks


class NABufs:
    def __init__(self, P):
        self.ldm = [P.dsem() for _ in range(4)]
        self.b_sem = [P.dsem() for _ in range(2)]
        self.o_sem = [P.dsem() for _ in range(2)]

    def carve(self, A):
        self.qT = A.take([128, 4, NTOK], BF16)
        self.kT = A.take([128, 4, 2560], BF16)
        self.v = A.take([128, 20, 512], BF16)
        self.bias = [A.take([128, 5, 640], F32) for i in range(2)]
        self.tmp = [A.take([128, 640], F32) for i in range(3)]
        self.pT = [A.take([128, 640], BF16) for i in range(3)]
        self.rec = [A.take([64, 128], F32) for i in range(3)]
        self.ost = [A.take([64, NTOK], BF16) for i in range(2)]
        self.ones = A.take([128, 64], BF16)


def emit_na(C, B, qT_d, kTe_d, ve_d, bias_d, oT_d, in_ready=None, filler=None):
    P = C.P
    t_ones = P.op("pool", lambda g: g.memset(B.ones[:], 1.0))
    SKEW = 2
    NB = SKEW + 1
    b_free = [None, None]
    tmp_free = [None] * NB
    pT_free = [None] * NB
    rec_free = [None] * NB
    s_free = [None, None]
    ob_free = [None, None]
    o_free = [None, None]
    btok = {}
    out_toks = []
    units = [(h, ml) for h in range(8) for ml in range(16)]
    st = {}

    def load_bias(h):
        bs = h % 2
        btok[h] = P.dma("sp", B.bias[bs][:].rearrange("p a b -> p (a b)"), bias_d[h], B.b_sem[bs], waits=[b_free[bs]])

    def emit_S(i):
        h, ml = units[i]
        m, po = h // 2, (h % 2) * 64
        u2 = i % NB
        us = i % 2
        sc = us * 1024
        if ml == 0 and h + 1 < 8:
            load_bias(h + 1)
        lastS = None
        for kc in range(5):
            w = [ldm[m], s_free[us]] if kc == 0 else ()
            lastS = P.op("pe", lambda g, sc=sc, kc=kc, m=m, po=po, ml=ml: g.matmul(
                C.psum[:, sc + kc * 128: sc + (kc + 1) * 128],
                lhsT=B.kT[po:po + 64, m, (ml + kc) * 128:(ml + kc + 1) * 128],
                rhs=B.qT[po:po + 64, m, ml * 128:(ml + 1) * 128], start=True, stop=True),
                waits=w, sig=(kc == 4))
        bs = h % 2
        typ = {0: 0, 1: 1, 14: 3, 15: 4}.get(ml, 2)
        tt = P.op("dve", lambda g, sc=sc, u2=u2, bs=bs, typ=typ: g.tensor_tensor(
            out=B.tmp[u2][:], in0=C.psum[:, sc: sc + 640], in1=B.bias[bs][:, typ, :], op=ALU.add),
            waits=[lastS, btok[h], tmp_free[u2]])
        s_free[us] = tt
        b_free[bs] = tt
        st[("tt", i)] = tt

    def emit_S2(i):
        u2 = i % NB
        tt = st.pop(("tt", i))
        te = P.op("act", lambda g, u2=u2: g.activation(out=B.pT[u2][:], in_=B.tmp[u2][:], func=AF.Exp),
                  waits=[tt, pT_free[u2]])
        tmp_free[u2] = te
        st[i] = te

    def emit_rest(i):
        h, ml = units[i]
        u2 = i % NB
        us = i % 2
        oc = 2048 + us * 512
        os_ = h % 2
        te = st.pop(i)
        for kc in range(5):
            w = [te, t_ones, ob_free[us]] if kc == 0 else ()
            P.op("pe", lambda g, oc=oc, kc=kc, ml=ml, h=h, u2=u2: g.matmul(
                C.psum[0:64, oc: oc + 128], lhsT=B.v[:, ml + kc, h * 64:(h + 1) * 64],
                rhs=B.pT[u2][:, kc * 128:(kc + 1) * 128], start=(kc == 0), stop=(kc == 4)), waits=w, sig=False)
        lastO = None
        for kc in range(5):
            lastO = P.op("pe", lambda g, oc=oc, kc=kc, u2=u2: g.matmul(
                C.psum[0:64, oc + 128: oc + 256], lhsT=B.ones[:],
                rhs=B.pT[u2][:, kc * 128:(kc + 1) * 128], start=(kc == 0), stop=(kc == 4)), sig=(kc == 4))
        pT_free[u2] = lastO
        tl = P.op("act", lambda g, oc=oc, u2=u2: g.activation(out=B.rec[u2][:], in_=C.psum[0:64, oc + 128: oc + 256], func=AF.Ln),
                  waits=[lastO, rec_free[u2]])
        tr = P.op("act", lambda g, u2=u2: g.activation(out=B.rec[u2][:], in_=B.rec[u2][:], func=AF.Exp, scale=-1.0),
                  waits=[tl])
        st[("tail", i)] = (tr, lastO)

    def emit_rest2(i):
        h, ml = units[i]
        u2 = i % NB
        us = i % 2
        oc = 2048 + us * 512
        os_ = h % 2
        tr, lastO = st.pop(("tail", i))
        lastw = P.op("dve", lambda g, oc=oc, u2=u2, os_=os_, ml=ml: g.tensor_tensor(
            out=B.ost[os_][:, ml * 128:(ml + 1) * 128], in0=C.psum[0:64, oc: oc + 128],
            in1=B.rec[u2][:], op=ALU.mult), waits=[tr, lastO, o_free[os_]] if ml == 0 else [tr, lastO])
        rec_free[u2] = lastw
        ob_free[us] = lastw
        if ml == 15:
            o_free[os_] = P.dma("sp", oT_d[h * 64:(h + 1) * 64, :], B.ost[os_], B.o_sem[os_], waits=[lastw])
            out_toks.append(o_free[os_])

    ldm = {}
    for m in range(4):
        P.dma("sp", B.qT[:, m, :], qT_d[m * 128:(m + 1) * 128, :], B.ldm[m], waits=[in_ready])
        P.dma("sp", B.kT[:, m, :], kTe_d[m * 128:(m + 1) * 128, :], B.ldm[m])
        ldm[m] = P.dma("sp", B.v[:, :, m * 128:(m + 1) * 128],
                       ve_d[:, m * 128:(m + 1) * 128].rearrange("(i p) f -> p i f", p=128), B.ldm[m])
        if m == 0:
            load_bias(0)
    n = len(units)
    for i in range(n + SKEW):
        if i < n:
            emit_S(i)
        if i >= SKEW:
            emit_rest(i - SKEW)
        if i < n:
            emit_S2(i)
        if i >= SKEW:
            emit_rest2(i - SKEW)
        if filler is not None and i % 5 in (1, 3):
            next(filler, None)
    if filler is not None:
        for _ in filler:
            pass
    return out_toks


class FNBufs:
    def __init__(self, P):
        self.ld = P.dsem()
        self.e_sem = [P.dsem() for _ in range(2)]
        self.st = P.dsem()

    def carve(self, A):
        self.u = A.take([128, 64, 128], BF16)
        self.E = [A.take([128, 8, 256], BF16) for i in range(2)]
        self.Bsb = A.take([128, 256, 64], BF16)
        self.R = A.take([128, 512], BF16)
        self.CS = A.take([128, 128], BF16)
        self.G = [A.take([128, 4, 256], BF16) for i in range(2)]
        self.YT = A.take([128, 64, 128], BF16)


def emit_fnet_gen(C, B, u_d, E_d, R_d, CS_d, YT_d, in_ready=None, fixed_pair=None):
    P = C.P
    P.dma("sp", B.u[:].rearrange("p a b -> p (a b)"), u_d, B.ld, waits=[in_ready])
    P.dma("sp", B.R[:], R_d, B.ld)
    ld = P.dma("sp", B.CS[:], CS_d, B.ld)
    e_free = [None, None]
    unit = 0
    s1_act = s1_dve = None
    for ec in range(8):
        es = ec % 2
        etok = P.dma("sp", B.E[es][:].rearrange("p a b -> p (a b)"), E_d[:, ec * 2048:(ec + 1) * 2048], B.e_sem[es],
                     waits=[e_free[es]])
        for half in range(2):
            pb = fixed_pair if fixed_pair is not None else 2 * (unit % 4)
            unit += 1
            last = None
            for ci in range(4):
                cl = half * 4 + ci
                c = ec * 8 + cl
                w = [ld, etok, C.bank_free[pb], C.bank_free[pb + 1]] if ci == 0 else ()
                last = P.op("pe", lambda g, pb=pb, ci=ci, c=c, cl=cl, es=es: g.matmul(
                    C.psum[:, pb * 512 + ci * 256: pb * 512 + (ci + 1) * 256], lhsT=B.u[:, c, :], rhs=B.E[es][:, cl, :],
                    start=True, stop=True), waits=w, sig=(ci == 3))
            c0 = ec * 8 + half * 4
            if unit % 2 == 0:
                te = P.op("act", lambda g, pb=pb, c0=c0: g.activation(
                    out=B.Bsb[:, :, c0:c0 + 4].rearrange("p n c -> p c n"),
                    in_=C.bank(pb, 2).rearrange("p (c n) -> p c n", c=4), func=AF.Copy), waits=[last])
            else:
                te = P.op("dve", lambda g, pb=pb, c0=c0: g.tensor_copy(
                    out=B.Bsb[:, :, c0:c0 + 4].rearrange("p n c -> p c n"),
                    in_=C.bank(pb, 2).rearrange("p (c n) -> p c n", c=4)), waits=[last])
            C.bank_free[pb] = te
            C.bank_free[pb + 1] = te
            if unit % 2 == 0:
                s1_act = te
            else:
                s1_dve = te
            yield
        e_free[es] = last
    g_free = [None, None]
    y_tok = None
    import os
    nst = int(os.environ.get("FN_STAGES", "3"))
    for kq in range(16 if nst >= 2 else 0):
        gs = kq % 2
        pb = fixed_pair if fixed_pair is not None else 2 * (unit % 4)
        unit += 1
        last = None
        for ki in range(4):
            kp = kq * 4 + ki
            for part in range(2):
                w = [s1_act, s1_dve, C.bank_free[pb], C.bank_free[pb + 1]] if (ki == 0 and part == 0) else ()
                lhs = B.Bsb[:, part * 128 + 2 * kp: part * 128 + 2 * kp + 2, :].rearrange("p k c -> p (k c)")
                last = P.op("pe", lambda g, pb=pb, ki=ki, part=part, lhs=lhs: g.matmul(
                    C.psum[:, pb * 512 + ki * 256: pb * 512 + (ki + 1) * 256], lhsT=lhs,
                    rhs=B.R[:, part * 256:(part + 1) * 256], start=(part == 0), stop=(part == 1)),
                    waits=w, sig=(ki == 3 and part == 1))
        if kq % 2 == 0:
            te = P.op("act", lambda g, pb=pb, gs=gs: g.activation(out=B.G[gs][:].rearrange("p a b -> p (a b)"), in_=C.bank(pb, 2), func=AF.Copy),
                      waits=[last, g_free[gs]])
        else:
            te = P.op("dve", lambda g, pb=pb, gs=gs: g.tensor_copy(out=B.G[gs][:].rearrange("p a b -> p (a b)"), in_=C.bank(pb, 2)),
                      waits=[last, g_free[gs]])
        C.bank_free[pb] = te
        C.bank_free[pb + 1] = te
        if nst < 3:
            y_tok = te
            continue
        yield
        yb = fixed_pair if fixed_pair is not None else 2 * (unit % 4)
        unit += 1
        last3 = None
        for ki in range(4):
            for k2 in range(2):
                po = k2 * 64
                for part in range(2):
                    w = [te, C.bank_free[yb], C.bank_free[yb + 1]] if (ki == 0 and k2 == 0 and part == 0) else ()
                    last3 = P.op("pe", lambda g, yb=yb, gs=gs, ki=ki, k2=k2, po=po, part=part: g.matmul(
                        C.psum[:, (yb + k2) * 512 + ki * 64: (yb + k2) * 512 + (ki + 1) * 64],
                        lhsT=B.G[gs][po:po + 64, ki, part * 128:(part + 1) * 128],
                        rhs=B.CS[po:po + 64, part * 64:(part + 1) * 64], start=(part == 0), stop=(part == 1)),
                        waits=w, sig=(ki == 3 and k2 == 1 and part == 1))
        g_free[gs] = last3
        kr0 = kq * 8
        ty = P.op("dve", lambda g, yb=yb, kr0=kr0: g.tensor_copy(
            out=B.YT[:, :, kr0:kr0 + 8].rearrange("p x (a k) -> p x a k", k=2),
            in_=C.bank(yb, 2).rearrange("p (k a x) -> p k a x", k=2, a=8)[:, :, 0:4, :].rearrange("p k a x -> p x a k")),
            waits=[last3])
        C.bank_free[yb] = ty
        C.bank_free[yb + 1] = ty
        y_tok = ty
        yield
    out = P.dma("sp", YT_d, B.YT[:].rearrange("p a b -> p (a b)"), B.st, waits=[y_tok, s1_act, s1_dve])
    B.out_toks = [out]


def emit_fnet(C, B, *a, **k):
    for _ in emit_fnet_gen(C, B, *a, **k):
        pass
    return B.out_toks


class OutBufs:
    def __init__(self, P):
        self.ld = P.dsem()
        self.w_sems = [P.dsem() for _ in range(11)]
        self.in_sem = [P.dsem() for _ in range(2)]

    def carve(self, A):
        self.oT = [A.take([128, 4, 512], BF16) for i in range(2)]
        self.yfT = [A.take([128, 4, 512], BF16) for i in range(2)]
        self.wna = A.take([128, 4, D], BF16)
        self.wf = A.take([128, 4, D], BF16)
        self.wo = A.take([128, 8, D], BF16)
        self.wg = A.take([128, 8, 2 * D], BF16)
        self.gb = A.take([128, 2, 8], F32)
        self.mT = A.take([128, 8, 512], BF16)
        self.sga = [A.take([128, 512], F32) for i in range(2)]
        self.sgf = [A.take([128, 512], F32) for i in range(2)]


def emit_outproj(C, B, oT_d, yfT_d, w_in, w_na, w_f, w_o, gb_d, in_ready=None, h_ready=None, x_ready=None):
    P = C.P
    ld = P.dma("sp", B.gb.rearrange("p a b -> p (a b)"), gb_d, B.ld, waits=[in_ready])
    t_wna = P.dma("pool", B.wna, w_na.rearrange("(m p) d -> p m d", p=128), B.w_sems[0])
    t_wg = {}
    t_wf = None
    for q in range(4):
        t_wg[(0, q)] = P.dma("pool", B.wg[:, :, q * 256:(q + 1) * 256],
                             w_in[:, 2048 + q * 256:2048 + (q + 1) * 256].rearrange("(c p) f -> p c f", p=128), B.w_sems[1 + q])
        if q == 0:
            t_wf = P.dma("pool", B.wf, w_f.rearrange("(g p) d -> p g d", p=128), B.w_sems[9])
        t_wg[(1, q)] = P.dma("pool", B.wg[:, :, D + q * 256:D + (q + 1) * 256],
                             w_in[:, 3072 + q * 256:3072 + (q + 1) * 256].rearrange("(c p) f -> p c f", p=128), B.w_sems[5 + q])
    t_wo = P.dma("pool", B.wo, w_o.rearrange("(c p) d -> p c d", p=128), B.w_sems[10])
    in_free = [None, None]
    s_free = [None, None]
    m_free = None
    x_tok = None
    unit = 0

    def load_in(t):
        k = t % 2
        tsl = slice(t * 512, (t + 1) * 512)
        P.dma("sp", B.oT[k], oT_d[:, tsl].rearrange("(m p) t -> p m t", p=128), B.in_sem[k], waits=[in_ready, in_free[k]])
        return P.dma("sp", B.yfT[k], yfT_d[:, tsl].rearrange("(g p) t -> p g t", p=128), B.in_sem[k])

    in_tok = {0: load_in(0)}
    for t in range(4):
        ts = slice(t * 512, (t + 1) * 512)
        k = t % 2
        if t + 1 < 4:
            in_tok[t + 1] = load_in(t + 1)
        d3 = None
        for dch in range(8):
            pb = 4 * (unit % 2)
            k2 = unit % 2
            unit += 1
            dsl = slice(dch * 128, (dch + 1) * 128)
            w0 = [ld, t_wna, t_wf, t_wg[(0, dch // 2)], t_wg[(1, dch // 2)], in_tok[t], h_ready[t] if isinstance(h_ready, list) else h_ready] + [C.bank_free[pb + q] for q in range(4)]
            for mm in range(4):
                P.op("pe", lambda g, pb=pb, mm=mm, dsl=dsl, k=k: g.matmul(
                    C.bank(pb), lhsT=B.wna[:, mm, dsl], rhs=B.oT[k][:, mm, :], start=(mm == 0), stop=(mm == 3)),
                    waits=w0 if mm == 0 else (), sig=False)
            for d in range(8):
                tB = P.op("pe", lambda g, pb=pb, d=d, dsl=dsl, ts=ts: g.matmul(
                    C.bank(pb + 1), lhsT=B.wg[:, d, dsl], rhs=C.hT[:, d, ts], start=(d == 0), stop=(d == 7)), sig=(d == 7))
            for gg in range(4):
                P.op("pe", lambda g, pb=pb, gg=gg, dsl=dsl, k=k: g.matmul(
                    C.bank(pb + 2), lhsT=B.wf[:, gg, dsl], rhs=B.yfT[k][:, gg, :], start=(gg == 0), stop=(gg == 3)), sig=False)
            for d in range(8):
                tD = P.op("pe", lambda g, pb=pb, d=d, dch=dch, ts=ts: g.matmul(
                    C.bank(pb + 3), lhsT=B.wg[:, d, D + dch * 128:D + (dch + 1) * 128], rhs=C.hT[:, d, ts],
                    start=(d == 0), stop=(d == 7)), sig=(d == 7))
            sa = P.op("act", lambda g, pb=pb, k2=k2, dch=dch: g.activation(
                out=B.sga[k2], in_=C.bank(pb + 1), func=AF.Sigmoid, bias=B.gb[:, 0, dch:dch + 1], scale=1.0),
                waits=[tB, s_free[k2]])
            sf = P.op("act", lambda g, pb=pb, k2=k2, dch=dch: g.activation(
                out=B.sgf[k2], in_=C.bank(pb + 3), func=AF.Sigmoid, bias=B.gb[:, 1, dch:dch + 1], scale=1.0),
                waits=[tD])
            d1 = P.op("dve", lambda g, pb=pb, k2=k2: g.tensor_tensor(out=B.sga[k2], in0=C.bank(pb), in1=B.sga[k2], op=ALU.mult),
                      waits=[sa, tD])
            d2 = P.op("dve", lambda g, pb=pb, k2=k2: g.tensor_tensor(out=B.sgf[k2], in0=C.bank(pb + 2), in1=B.sgf[k2], op=ALU.mult),
                      waits=[sf, d1])
            for q in range(4):
                C.bank_free[pb + q] = d2
            d3 = P.op("pool", lambda g, k2=k2, dch=dch: g.tensor_tensor(out=B.mT[:, dch, :], in0=B.sga[k2], in1=B.sgf[k2], op=ALU.add),
                      waits=[d2, m_free] if dch == 0 else [d2])
            s_free[k2] = d3
        in_free[k] = tD
        C.h_free = tD
        for dq in range(2):
            pb = 4 * (unit % 2)
            unit += 1
            last = None
            for di in range(4):
                do = dq * 4 + di
                for dch in range(8):
                    w = [d3, t_wo] + [C.bank_free[pb + q] for q in range(4)] if (di == 0 and dch == 0) else ()
                    last = P.op("pe", lambda g, pb=pb, di=di, dch=dch, do=do: g.matmul(
                        C.bank(pb + di), lhsT=B.wo[:, dch, do * 128:(do + 1) * 128], rhs=B.mT[:, dch, :],
                        start=(dch == 0), stop=(dch == 7)), waits=w, sig=(di == 3 and dch == 7))
            for di in range(4):
                do = dq * 4 + di
                x_tok = P.op("dve", lambda g, pb=pb, di=di, do=do, ts=ts: g.tensor_tensor(
                    out=C.xT[:, do, ts], in0=C.bank(pb + di), in1=C.xT[:, do, ts], op=ALU.add),
                    waits=[last, x_ready[t] if isinstance(x_ready, list) else x_ready])
                C.bank_free[pb + di] = x_tok
            m_free = last
    return x_tok


def emit_final_norm(C, gi, out_d):
    P = C.P
    A = C.arena
    st = [A.take([128, 8, 512], F32) for _ in range(2)]
    sem = [C.fin_sem0, C.fin_sem1]
    free = [None, None]
    toks = []

    def out_fn(c, t, ts, r, t2):
        k = t % 2
        tk = P.op("dve", lambda g, c=c, ts=ts, r=r, k=k: g.scalar_tensor_tensor(
            out=st[k][:, c, :], in0=C.xT[:, c, ts], scalar=C.gains[:, gi, c:c + 1], in1=r[:],
            op0=ALU.mult, op1=ALU.mult), waits=[t2, free[k], C.gains_tok] if c == 0 else (), sig=(c == 7))
        if c == 7:
            free[k] = P.dma("sp", out_d[:, ts].rearrange("(c p) t -> p c t", p=128), st[k], sem[k], waits=[tk])
            toks.append(free[k])
        return tk

    emit_norm(C, gi, None, out_fn=out_fn)
    return toks


def _load_xh(P, C, x_d, h_d, g_d):
    C.gains_tok = P.dma("sp", C.gains[:].rearrange("p a b -> p (a b)"), g_d, P.dsem())
    h_toks = None
    if h_d is not None:
        h_toks = []
        for t in range(4):
            ts = slice(t * 512, (t + 1) * 512)
            h_toks.append(P.dma("sp", C.hT[:, :, ts], h_d[:, ts].rearrange("(c p) t -> p c t", p=128), P.dsem()))
    x_toks = []
    for t in range(4):
        ts = slice(t * 512, (t + 1) * 512)
        x_toks.append(P.dma("sp", C.xT[:, :, ts], x_d[:, ts].rearrange("(c p) t -> p c t", p=128), P.dsem()))
    return x_toks, h_toks


def _dram(nc, name, shape, dt, kind):
    return nc.dram_tensor(name, list(shape), dt, kind=kind).ap()


def _stage_front(nc, P, C, gi0, x_ready, first):
    toks = []
    w1_in = _dram(nc, "w1_in", [D, 2 * DFF], F32, "ExternalInput")
    w1_out = _dram(nc, "w1_out", [DFF, D], F32, "ExternalInput")
    wmix = _dram(nc, "wmixb", [D, 4096], F32, "ExternalInput")
    qT = _dram(nc, "qT", [512, NTOK], BF16, "ExternalOutput")
    kT = _dram(nc, "kT", [512, NTOK], BF16, "ExternalOutput")
    v = _dram(nc, "v", [NTOK, 512], BF16, "ExternalOutput")
    uf = _dram(nc, "uf", [NTOK, 512], BF16, "ExternalOutput")
    xo = _dram(nc, "xT_out", [D, NTOK], F32, "ExternalOutput")
    ho = _dram(nc, "hT_out", [D, NTOK], BF16, "ExternalOutput")
    xs_sem = P.dsem()
    hs_sem = P.dsem()
    if not first:
        P.barrier()
    C.arena.reset()
    h = emit_norm(C, gi0, x_ready)
    C.ffn.carve(C.arena)

    def save_x(dch, tok):
        toks.append(P.dma("sp", xo[dch * 128:(dch + 1) * 128, :], C.xT[:, dch, :], xs_sem, waits=[tok]))

    emit_ffn(C, C.ffn, w1_in, w1_out, h, on_x_final=save_x)
    P.barrier()
    C.arena.reset()

    def save_h(t, ts, tok):
        toks.append(P.dma("sp", ho[:, ts].rearrange("(c p) t -> p c t", p=128), C.hT[:, :, ts], hs_sem, waits=[tok]))

    h = emit_norm(C, gi0 + 1, None, on_chunk=save_h)
    C.inp.carve(C.arena)
    toks += emit_inproj(C, C.inp, wmix, h, qT, kT, v, uf)
    return toks


def build_A():
    nc = bass.Bass("TRN2", target_bir_lowering=False)
    x_d = _dram(nc, "xT_in", [D, NTOK], F32, "ExternalInput")
    g_d = _dram(nc, "gains", [128, 56], F32, "ExternalInput")
    P = Prog(nc)
    C = Ctx(P)
    C.ffn = FFNBufs(P)
    C.inp = InprojBufs(P)
    x_toks, _ = _load_xh(P, C, x_d, None, g_d)
    toks = _stage_front(nc, P, C, 0, x_toks, True)
    P.wait("sp", toks)
    P.emit()
    return nc


def build_NF():
    nc = bass.Bass("TRN2", target_bir_lowering=False)
    qT = _dram(nc, "qT", [512, NTOK], BF16, "ExternalInput")
    kTe = _dram(nc, "kTe", [512, 2560], BF16, "ExternalInput")
    ve = _dram(nc, "ve", [2560, 512], BF16, "ExternalInput")
    bias = _dram(nc, "bias", [8, 128, 3200], F32, "ExternalInput")
    u = _dram(nc, "u", [128, 8192], BF16, "ExternalInput")
    E = _dram(nc, "E", [128, 16384], BF16, "ExternalInput")
    R = _dram(nc, "R", [128, 512], BF16, "ExternalInput")
    CS = _dram(nc, "CS", [128, 128], BF16, "ExternalInput")
    oT = _dram(nc, "oT", [512, NTOK], BF16, "ExternalOutput")
    YT = _dram(nc, "YT", [128, 8192], BF16, "ExternalOutput")
    P = Prog(nc)
    C = Ctx(P, with_x=False, arena=92000)
    na = NABufs(P)
    fn = FNBufs(P)
    import os
    which = os.environ.get("NF_ONLY", "both")
    C.arena.reset()
    if which == "both":
        na.carve(C.arena)
        fn.carve(C.arena)
        gen = emit_fnet_gen(C, fn, u, E, R, CS, YT, fixed_pair=6)
        toks = emit_na(C, na, qT, kTe, ve, bias, oT, filler=gen)
        toks = toks + fn.out_toks
    elif which == "na":
        na.carve(C.arena)
        toks = emit_na(C, na, qT, kTe, ve, bias, oT)
    else:
        fn.carve(C.arena)
        toks = emit_fnet(C, fn, u, E, R, CS, YT)
    P.wait("sp", toks)
    P.emit()
    return nc


def build_E(final, gi0):
    nc = bass.Bass("TRN2", target_bir_lowering=False)
    x_d = _dram(nc, "xT_in", [D, NTOK], F32, "ExternalInput")
    h_d = _dram(nc, "hT_in", [D, NTOK], BF16, "ExternalInput")
    g_d = _dram(nc, "gains", [128, 56], F32, "ExternalInput")
    oT = _dram(nc, "oT", [512, NTOK], BF16, "ExternalInput")
    yfT = _dram(nc, "yfT", [512, NTOK], BF16, "ExternalInput")
    wmixa = _dram(nc, "wmixa", [D, 4096], F32, "ExternalInput")
    w_na = _dram(nc, "w_na", [512, D], F32, "ExternalInput")
    w_f = _dram(nc, "w_f", [512, D], F32, "ExternalInput")
    w_o = _dram(nc, "w_o", [D, D], F32, "ExternalInput")
    gb = _dram(nc, "gb", [128, 16], F32, "ExternalInput")
    w2_in = _dram(nc, "w2_in", [D, 2 * DFF], F32, "ExternalInput")
    w2_out = _dram(nc, "w2_out", [DFF, D], F32, "ExternalInput")
    P = Prog(nc)
    C = Ctx(P)
    C.ffn = FFNBufs(P)
    C.inp = InprojBufs(P)
    ob = OutBufs(P)
    x_toks, h_toks = _load_xh(P, C, x_d, h_d, g_d)
    C.arena.reset()
    ob.carve(C.arena)
    emit_outproj(C, ob, oT, yfT, wmixa, w_na, w_f, w_o, gb, h_ready=h_toks, x_ready=x_toks)
    P.barrier()
    C.arena.reset()
    h = emit_norm(C, gi0 + 2, None)
    C.ffn.carve(C.arena)
    emit_ffn(C, C.ffn, w2_in, w2_out, h)
    if final:
        out = _dram(nc, "outT", [D, NTOK], F32, "ExternalOutput")
        P.barrier()
        C.arena.reset()
        toks = emit_final_norm(C, 6, out)
    else:
        toks = _stage_front(nc, P, C, gi0 + 3, None, False)
    P.wait("sp", toks)
    P.emit()
    return nc


import ml_dtypes
BF = ml_dtypes.bfloat16
NEG = -30000.0


def _tables():
    r = np.arange(128, dtype=np.int64)[:, None, None]
    c = np.arange(64, dtype=np.int64)[None, :, None]
    kr = np.arange(128, dtype=np.int64)[None, None, :]
    th = 2 * np.pi * ((kr * (64 * r + c)) % 8192) / 8192.0
    E = np.concatenate([np.cos(th), -np.sin(th)], axis=2) * 2.0 ** -10
    ch = np.arange(128, dtype=np.int64)
    th2 = 2 * np.pi * ((ch[:, None] * ch[None, :]) % 128) / 128.0
    Cm, Sm = np.cos(th2), np.sin(th2)
    R = np.concatenate([Cm, -Sm, Sm, Cm], axis=1)
    cc = np.arange(64, dtype=np.int64)
    th3 = 2 * np.pi * ((cc[:, None] * cc[None, :]) % 64) / 64.0
    CS64 = np.concatenate([np.cos(th3), np.sin(th3)], axis=1)
    CS = np.concatenate([CS64, CS64], axis=0)
    return (np.ascontiguousarray(E.reshape(128, 64 * 256)).astype(BF), R.astype(BF), CS.astype(BF))


def _ext_chunk(jj, e):
    gc = jj * 16 + e - 2
    if gc < 0:
        return 3 if e == 0 else None
    if gc > 63:
        return 60 if e == 19 else None
    return gc


def _bias_tiles(rpb, jj):
    out = np.full((8, 128, 5, 640), NEG, np.float32)
    qi = np.arange(2)[None, None, :, None]
    qc = np.arange(64)[None, None, None, :]
    ki = np.arange(2)[:, None, None, None]
    kcl = np.arange(64)[None, :, None, None]
    ws = np.clip(qc - 8, 0, 48)
    col_ok = (kcl >= ws) & (kcl < ws + 16)
    dc = np.clip(kcl - qc, -15, 15) + 15
    for typ, ml in ((0, 0), (1, 1), (2, 5), (3, 14), (4, 15)):
        m = jj * 16 + ml
        r = 2 * m + qi
        rs = np.clip(r - 4, 0, 120)
        for kc in range(5):
            gc = _ext_chunk(jj, ml + kc)
            if gc is None:
                continue
            krow = 2 * gc + ki
            ok = (krow >= rs) & (krow < rs + 8) & col_ok
            dr = np.clip(krow - r + 7, 0, 14)
            drb = np.broadcast_to(dr, ok.shape)
            dcb = np.broadcast_to(dc, ok.shape)
            vals = rpb[:, drb, dcb]
            tile = np.where(ok[None], vals, np.float32(NEG)).reshape(8, 128, 128)
            out[:, :, typ, kc * 128:(kc + 1) * 128] = tile
    return np.ascontiguousarray(out.reshape(8, 128, 3200))


def _gain_layout(g):
    n = g.shape[0]
    return np.ascontiguousarray(g.reshape(n, 8, 128).transpose(2, 0, 1).reshape(128, n * 8)).astype(np.float32)


def _run(nc, in_maps):
    res = run_bass_kernel_spmd(nc, in_maps, core_ids=list(range(8)))
    return res.results


def _exchange_front(outs, rpb_l, tabs):
    E, R, CS = tabs
    ims = []
    for core in range(8):
        b, jj = divmod(core, 4)
        kT_all = np.concatenate([outs[b * 4 + q]["kT"] for q in range(4)], axis=1)
        v_all = np.concatenate([outs[b * 4 + q]["v"] for q in range(4)], axis=0)
        uf_all = np.concatenate([outs[b * 4 + q]["uf"] for q in range(4)], axis=0)
        kTe = np.zeros((512, 2560), BF)
        ve = np.zeros((2560, 512), BF)
        for e in range(20):
            gc = _ext_chunk(jj, e)
            if gc is None:
                continue
            kTe[:, e * 128:(e + 1) * 128] = kT_all[:, gc * 128:(gc + 1) * 128]
            ve[e * 128:(e + 1) * 128, :] = v_all[gc * 128:(gc + 1) * 128, :]
        g = jj
        u = np.ascontiguousarray(uf_all[:, g * 128:(g + 1) * 128]).reshape(128, 64 * 128)
        ims.append({"qT": outs[core]["qT"], "kTe": kTe, "ve": ve, "bias": _bias_tiles(rpb_l, jj),
                    "u": u, "E": E, "R": R, "CS": CS})
    return ims


def kernel(x, ffn1_norm, ffn1_w_in, ffn1_w_out, mix_norm, mix_w_in, mix_gate_bias, na_rpb, na_w_out,
           f_w_out, mix_w_o, ffn2_norm, ffn2_w_in, ffn2_w_out, final_norm):
    f32 = lambda a: np.ascontiguousarray(np.asarray(a, dtype=np.float32))
    x = f32(x)
    gains = _gain_layout(np.stack([f32(ffn1_norm)[0], f32(mix_norm)[0], f32(ffn2_norm)[0],
                                   f32(ffn1_norm)[1], f32(mix_norm)[1], f32(ffn2_norm)[1], f32(final_norm)]))
    tabs = _tables()
    rpb = f32(na_rpb)
    ims = []
    for core in range(8):
        b, jj = divmod(core, 4)
        ims.append({"xT_in": np.ascontiguousarray(x[b, jj * NTOK:(jj + 1) * NTOK, :].T), "gains": gains,
                    "w1_in": f32(ffn1_w_in[0]), "w1_out": f32(ffn1_w_out[0]), "wmixb": f32(mix_w_in[0])})
    front = _run(build_A(), ims)
    out = None
    for l in range(L):
        nf = _run(build_NF(), _exchange_front(front, rpb[l], tabs))
        ims = []
        gb = _gain_layout(f32(mix_gate_bias[l]))
        for core in range(8):
            b, jj = divmod(core, 4)
            yfT = np.concatenate([nf[b * 4 + g]["YT"][:, jj * NTOK:(jj + 1) * NTOK] for g in range(4)], axis=0)
            im = {"xT_in": front[core]["xT_out"], "hT_in": front[core]["hT_out"], "gains": gains,
                  "oT": nf[core]["oT"], "yfT": np.ascontiguousarray(yfT), "wmixa": f32(mix_w_in[l]),
                  "w_na": f32(na_w_out[l]), "w_f": f32(f_w_out[l]), "w_o": f32(mix_w_o[l]), "gb": gb,
                  "w2_in": f32(ffn2_w_in[l]), "w2_out": f32(ffn2_w_out[l])}
            if l + 1 < L:
                im.update({"w1_in": f32(ffn1_w_in[l + 1]), "w1_out": f32(ffn1_w_out[l + 1]),
                           "wmixb": f32(mix_w_in[l + 1])})
            ims.append(im)
        res = _run(build_E(final=(l + 1 == L), gi0=3 * l), ims)
        if l + 1 < L:
            front = res
        else:
            out = np.empty((2, 8192, D), np.float32)
            for core in range(8):
                b, jj = divmod(core, 4)
                out[b, jj * NTOK:(jj + 1) * NTOK, :] = res[core]["outT"].T
    return out
```

```python
import numpy as np
from contextlib import ExitStack
import concourse.bass as bass
import concourse.mybir as mybir
from concourse.bass_utils import run_bass_kernel_spmd

F32 = mybir.dt.float32
BF16 = mybir.dt.bfloat16
AF = mybir.ActivationFunctionType
ALU = mybir.AluOpType

D = 1024
NTOK = 2048
DFF = 2816
NJ = 22
JG = 11
EPS = 1e-6
L = 2


class Eng:
    def __init__(self, name):
        self.name = name
        self.ops = []
        self.cnt = 0
        self.sem = None
        self.seen = {}
        self.last_sig = True


class DSem:
    def __init__(self, sem):
        self.sem = sem
        self.cnt = 0


class Prog:
    def __init__(self, nc):
        self.nc = nc
        self.stack = ExitStack()
        self.eng = {n: Eng(n) for n in ("pe", "act", "dve", "pool", "sp")}
        for n, e in self.eng.items():
            e.sem = self.stack.enter_context(nc.semaphore("s_" + n))
        self.nds = 0

    def sb(self, name, shape, dt):
        return self.stack.enter_context(self.nc.sbuf_tensor("sb_" + name, list(shape), dt))

    def ps(self, name, shape, dt=F32):
        return self.stack.enter_context(self.nc.psum_tensor("ps_" + name, list(shape), dt))

    def dsem(self):
        self.nds += 1
        return DSem(self.stack.enter_context(self.nc.semaphore("d%d" % self.nds)))

    def _waits(self, e, waits):
        for tok in waits:
            if tok is None:
                continue
            sem, val = tok
            if e.seen.get(id(sem), 0) >= val:
                continue
            e.seen[id(sem)] = val
            e.ops.append(("wait", sem, val))

    def op(self, eng, fn, waits=(), sig=True):
        e = self.eng[eng]
        self._waits(e, waits)
        e.last_sig = sig
        if sig:
            e.cnt += 1
            e.ops.append(("op", fn, e.sem, 1))
            return (e.sem, e.cnt)
        e.ops.append(("op", fn, None, 0))
        return None

    def dma(self, queue, out, in_, dsem, waits=()):
        e = self.eng[queue]
        self._waits(e, waits)
        dsem.cnt += 16
        e.ops.append(("op", lambda g: g.dma_start(out=out, in_=in_), dsem.sem, 16))
        return (dsem.sem, dsem.cnt)

    def wait(self, eng, waits):
        best = {}
        for tok in waits:
            if tok is None:
                continue
            sem, val = tok
            if id(sem) not in best or best[id(sem)][1] < val:
                best[id(sem)] = (sem, val)
        self._waits(self.eng[eng], list(best.values()))

    def barrier(self, extra=()):
        toks = list(extra)
        for n in ("pe", "act", "dve", "pool"):
            e = self.eng[n]
            if e.cnt:
                assert e.last_sig, "engine %s: last op before barrier must signal" % n
                toks.append((e.sem, e.cnt))
        for n in ("pe", "act", "dve", "pool", "sp"):
            self._waits(self.eng[n], toks)

    def emit(self):
        def replay(e, g):
            for it in e.ops:
                if it[0] == "wait":
                    g.wait_ge(it[1], it[2])
                else:
                    ins = it[1](g)
                    if it[2] is not None:
                        ins.then_inc(it[2], it[3])

        with self.nc.Block() as block:
            @block.tensor
            def _(g):
                replay(self.eng["pe"], g)

            @block.scalar
            def _(g):
                replay(self.eng["act"], g)

            @block.vector
            def _(g):
                replay(self.eng["dve"], g)

            @block.gpsimd
            def _(g):
                replay(self.eng["pool"], g)

            @block.sync
            def _(g):
                replay(self.eng["sp"], g)
        self.stack.close()


ARENA = 51500


class Arena:
    def __init__(self, P, n=ARENA):
        self.t = P.sb("arena", [128, n], BF16)
        self.n = n
        self.off = 0

    def reset(self):
        self.off = 0

    def take(self, shape, dt):
        n_el = 1
        for k in shape[1:]:
            n_el *= k
        units = n_el * (2 if dt == F32 else 1)
        units = (units + 15) // 16 * 16
        assert self.off + units <= self.n, ("arena overflow", self.off, units, self.n)
        v = self.t[0:shape[0], self.off:self.off + units]
        self.off += units
        if dt == F32:
            v = v.bitcast(F32)
        v = v[:, 0:n_el]
        if len(shape) == 3:
            v = v.rearrange("p (a b) -> p a b", a=shape[1])
        elif len(shape) == 4:
            v = v.rearrange("p (a b c) -> p a b c", a=shape[1], b=shape[2])
        return v


class Ctx:
    def __init__(self, P, with_x=True, arena=ARENA):
        self.P = P
        if with_x:
            self.xT = P.sb("xT", [128, 8, NTOK], F32)
            self.hT = P.sb("hT", [128, 8, NTOK], BF16)
        self.psum = P.ps("psum", [128, 4096], F32)
        self.ones = P.sb("ones", [128, 128], BF16)
        self.gains = P.sb("gains", [128, 7, 8], F32)
        self.arena = Arena(P, arena)
        self.sq = None
        self.rstd = [P.sb("rstd%d" % i, [128, 512], F32) for i in range(2)]
        self.bank_free = [None] * 8
        self.x_tok = None
        self.h_tok = None
        self.h_free = None
        self.sq_free = None
        self.rstd_free = [None, None]
        self.nnorm = 0
        self.gains_tok = None
        self.fin_sem0 = P.dsem()
        self.fin_sem1 = P.dsem()
        self.tok_ones = P.op("pool", lambda g: g.memset(self.ones[:], 1.0))

    def bank(self, b, n=1):
        return self.psum[:, b * 512:(b + n) * 512]


def emit_norm(C, gi, x_ready, out_fn=None, on_chunk=None):
    P = C.P
    toks = []
    C.sq = C.arena.take([128, 8, 512], BF16)
    C.sq_free = None
    for t in range(4):
        ts = slice(t * 512, (t + 1) * 512)
        xr = x_ready[t] if isinstance(x_ready, list) else x_ready
        r = C.rstd[C.nnorm % 2]
        k = C.nnorm % 2
        C.nnorm += 1
        tsq = P.op("act", lambda g, ts=ts: g.activation(out=C.sq, in_=C.xT[:, :, ts], func=AF.Square),
                   waits=[xr, C.sq_free])
        b = 7 if (t % 2) else 3
        for c in range(8):
            tk = P.op("pe", lambda g, c=c, b=b: g.matmul(C.bank(b), lhsT=C.ones[:], rhs=C.sq[:, c, :],
                                                         start=(c == 0), stop=(c == 7)),
                      waits=[tsq, C.tok_ones, C.bank_free[b]] if c == 0 else (), sig=(c == 7))
        C.sq_free = tk
        t1 = P.op("dve", lambda g, r=r, b=b: g.tensor_scalar(out=r[:], in0=C.bank(b), scalar1=1.0 / D, scalar2=EPS,
                                                             op0=ALU.mult, op1=ALU.add),
                  waits=[tk, C.rstd_free[k]])
        C.bank_free[b] = t1
        t1b = P.op("act", lambda g, r=r: g.activation(out=r[:], in_=r[:], func=AF.Sqrt), waits=[t1])
        t2 = P.op("dve", lambda g, r=r: g.reciprocal(out=r[:], in_=r[:]), waits=[t1b])
        last = None
        for c in range(8):
            if out_fn is None:
                last = P.op("dve", lambda g, c=c, ts=ts, r=r: g.scalar_tensor_tensor(
                    out=C.hT[:, c, ts], in0=C.xT[:, c, ts], scalar=C.gains[:, gi, c:c + 1], in1=r[:],
                    op0=ALU.mult, op1=ALU.mult), waits=[t2, C.h_free, C.gains_tok] if c == 0 else (), sig=(c == 7))
            else:
                last = out_fn(c, t, ts, r, t2)
        C.rstd_free[k] = last
        toks.append(last)
        if on_chunk is not None:
            on_chunk(t, ts, last)
    return toks


class FFNBufs:
    def __init__(self, P):
        self.wi_sem = [P.dsem() for _ in range(4)]
        self.wo_sem = [P.dsem() for _ in range(3)]
        self.nwi = 0
        self.nwo = 0
        self.nunit = 0

    def carve(self, A):
        self.aT = A.take([128, JG, NTOK], BF16)
        self.wi = [A.take([128, 8, 2, 128], BF16) for i in range(4)]
        self.wo = [A.take([128, JG, 256], BF16) for i in range(3)]
        self.sg = [A.take([128, 1024], F32) for i in range(2)]
        self.wi_free = [None] * 4
        self.wo_free = [None] * 3
        self.sg_free = [None] * 2
        self.aT_free = None


def emit_ffn(C, B, w_in, w_out, h_ready, on_x_final=None):
    P = C.P
    x_tok = None
    for grp in range(2):
        a_tok = None
        unit_tok = []
        for jl in range(JG):
            j = grp * JG + jl
            slot = B.nwi % 4
            B.nwi += 1
            src_g = w_in[:, j * 128:(j + 1) * 128].rearrange("(c p) f -> p c f", p=128)
            src_u = w_in[:, DFF + j * 128:DFF + (j + 1) * 128].rearrange("(c p) f -> p c f", p=128)
            P.dma("pool", B.wi[slot][:, :, 0, :], src_g, B.wi_sem[slot], waits=[B.wi_free[slot]])
            wtok = P.dma("pool", B.wi[slot][:, :, 1, :], src_u, B.wi_sem[slot])
            for half in range(2):
                pb = 4 * (B.nunit % 2)
                sgi = B.nunit % 2
                B.nunit += 1
                last = None
                for which in range(2):
                    for d in range(8):
                        for t2 in range(2):
                            bk = pb + which * 2 + t2
                            first = (which == 0 and d == 0 and t2 == 0)
                            fin = (which == 1 and d == 7 and t2 == 1)
                            tsl = slice(half * 1024 + t2 * 512, half * 1024 + (t2 + 1) * 512)
                            hr = h_ready[2 * half + 1] if isinstance(h_ready, list) else h_ready
                            w = [wtok, hr, C.bank_free[pb], C.bank_free[pb + 1], C.bank_free[pb + 2],
                                 C.bank_free[pb + 3]] if first else ()
                            last = P.op("pe", lambda g, bk=bk, slot=slot, d=d, which=which, tsl=tsl: g.matmul(
                                C.bank(bk), lhsT=B.wi[slot][:, d, which, :], rhs=C.hT[:, d, tsl],
                                start=(d == 0), stop=(d == 7)), waits=w, sig=fin)
                if half == 1:
                    B.wi_free[slot] = last
                C.h_free = last
                ts = P.op("act", lambda g, pb=pb, sgi=sgi: g.activation(out=B.sg[sgi][:], in_=C.bank(pb, 2), func=AF.Silu),
                          waits=[last, B.sg_free[sgi]])
                hs = slice(half * 1024, (half + 1) * 1024)
                tm = P.op("dve", lambda g, pb=pb, sgi=sgi, jl=jl, hs=hs: g.tensor_tensor(
                    out=B.aT[:, jl, hs], in0=B.sg[sgi][:], in1=C.bank(pb + 2, 2), op=ALU.mult),
                    waits=[ts, last, B.aT_free])
                B.sg_free[sgi] = tm
                for k in range(4):
                    C.bank_free[pb + k] = tm
                a_tok = tm
        for dp in range(4):
            slot = B.nwo % 3
            B.nwo += 1
            src = w_out[grp * JG * 128:(grp + 1) * JG * 128, dp * 256:(dp + 1) * 256].rearrange("(j p) d -> p j d", p=128)
            wtok = P.dma("pool", B.wo[slot][:], src, B.wo_sem[slot], waits=[B.wo_free[slot]])
            for ds_ in range(2):
                dch = dp * 2 + ds_
                pb = 4 * (B.nunit % 2)
                B.nunit += 1
                last = None
                for jl in range(JG):
                    for t in range(4):
                        first = (jl == 0 and t == 0)
                        fin = (jl == JG - 1 and t == 3)
                        w = [wtok, a_tok, C.bank_free[pb], C.bank_free[pb + 1], C.bank_free[pb + 2],
                             C.bank_free[pb + 3]] if first else ()
                        last = P.op("pe", lambda g, pb=pb, t=t, slot=slot, jl=jl, ds_=ds_: g.matmul(
                            C.bank(pb + t), lhsT=B.wo[slot][:, jl, ds_ * 128:(ds_ + 1) * 128],
                            rhs=B.aT[:, jl, t * 512:(t + 1) * 512], start=(jl == 0), stop=(jl == JG - 1)),
                            waits=w, sig=fin)
                if ds_ == 1:
                    B.wo_free[slot] = last
                te = P.op("dve", lambda g, pb=pb, dch=dch: g.scalar_tensor_tensor(
                    out=C.xT[:, dch, :], in0=C.bank(pb, 4), scalar=0.5, in1=C.xT[:, dch, :],
                    op0=ALU.mult, op1=ALU.add), waits=[last])
                for k in range(4):
                    C.bank_free[pb + k] = te
                x_tok = te
                B.aT_free = last
                if grp == 1 and on_x_final is not None:
                    on_x_final(dch, te)
    return x_tok


class InprojBufs:
    def __init__(self, P):
        self.wq_sem = [P.dsem() for _ in range(4)]
        self.st_sem = [P.dsem() for _ in range(2)]
        self.nst = 0
        self.nunit = 0

    def carve(self, A):
        self.w = A.take([128, 8, 2048], BF16)
        self.st = [A.take([128, 2048], BF16) for i in range(2)]
        self.st_free = [None, None]


def emit_inproj(C, B, w_in, h_ready, qT_d, kT_d, v_d, uf_d, w_free=None):
    P = C.P
    wq = {}
    for q in (2, 3, 0, 1):
        wq[q] = P.dma("pool", B.w[:, :, q * 512:(q + 1) * 512],
                      w_in[:, q * 512:(q + 1) * 512].rearrange("(c p) f -> p c f", p=128), B.wq_sem[q], waits=[w_free])
    out_toks = []
    for which in range(2):
        for tq in range(4):
            pb = 4 * (B.nunit % 2)
            B.nunit += 1
            last = None
            for ti in range(4):
                tt = tq * 4 + ti
                for d in range(8):
                    first = (d == 0 and ti == 0)
                    w = [wq[2 + which], h_ready[tq] if isinstance(h_ready, list) else h_ready] + [C.bank_free[pb + k] for k in range(4)] if first else ()
                    last = P.op("pe", lambda g, pb=pb, ti=ti, d=d, tt=tt, which=which: g.matmul(
                        C.bank(pb + ti), lhsT=C.hT[:, d, tt * 128:(tt + 1) * 128],
                        rhs=B.w[:, d, 1024 + which * 512:1024 + (which + 1) * 512],
                        start=(d == 0), stop=(d == 7)), waits=w, sig=(d == 7 and ti == 3))
            si = B.nst % 2
            B.nst += 1
            if tq % 2 == 0:
                te = P.op("act", lambda g, pb=pb, si=si: g.activation(out=B.st[si][:], in_=C.bank(pb, 4), func=AF.Copy),
                          waits=[last, B.st_free[si]])
            else:
                te = P.op("dve", lambda g, pb=pb, si=si: g.tensor_copy(out=B.st[si][:], in_=C.bank(pb, 4)),
                          waits=[last, B.st_free[si]])
            for k in range(4):
                C.bank_free[pb + k] = te
            dst = (v_d if which == 0 else uf_d)[tq * 512:(tq + 1) * 512, :].rearrange("(i p) f -> p i f", p=128)
            B.st_free[si] = P.dma("sp", dst, B.st[si][:].rearrange("p (i f) -> p i f", i=4), B.st_sem[si], waits=[te])
            out_toks.append(B.st_free[si])
            C.h_free = last
    for m in range(8):
        pb = 4 * (B.nunit % 2)
        B.nunit += 1
        last = None
        for d in range(8):
            for t in range(4):
                first = (d == 0 and t == 0)
                w = [wq[m // 4], h_ready[3] if isinstance(h_ready, list) else h_ready] + [C.bank_free[pb + k] for k in range(4)] if first else ()
                last = P.op("pe", lambda g, pb=pb, t=t, d=d, m=m: g.matmul(
                    C.bank(pb + t), lhsT=B.w[:, d, m * 128:(m + 1) * 128], rhs=C.hT[:, d, t * 512:(t + 1) * 512],
                    start=(d == 0), stop=(d == 7)), waits=w, sig=(d == 7 and t == 3))
        si = B.nst % 2
        B.nst += 1
        if m < 4:
            te = P.op("dve", lambda g, pb=pb, si=si: g.tensor_scalar(out=B.st[si], in0=C.bank(pb, 4), scalar1=0.125, scalar2=None, op0=ALU.mult),
                      waits=[last, B.st_free[si]])
        else:
            te = P.op("act", lambda g, pb=pb, si=si: g.activation(out=B.st[si], in_=C.bank(pb, 4), func=AF.Copy),
                      waits=[last, B.st_free[si]])
        for k in range(4):
            C.bank_free[pb + k] = te
        dst = (qT_d if m < 4 else kT_d)[(m % 4) * 128:(m % 4 + 1) * 128, :]
        B.st_free[si] = P.dma("sp", dst, B.st[si][:], B.st_sem[si], waits=[te])
        out_toks.append(B.st_free[si])
    C.h_free = last
    return out_toks


class NABufs:
    def __init__(self, P):
        self.ldm = [P.dsem() for _ in range(4)]
        self.b_sem = [P.dsem() for _ in range(2)]
        self.o_sem = [P.dsem() for _ in range(2)]

    def carve(self, A):
        self.qT = A.take([128, 4, NTOK], BF16)
        self.kT = A.take([128, 4, 2560], BF16)
        self.v = A.take([128, 20, 512], BF16)
        self.bias = [A.take([128, 5, 640], F32) for i in range(2)]
        self.tmp = [A.take([128, 640], F32) for i in range(3)]
        self.pT = [A.take([128, 640], BF16) for i in range(3)]
        self.rec = [A.take([64, 128], F32) for i in range(3)]
        self.ost = [A.take([64, NTOK], BF16) for i in range(2)]
        self.ones = A.take([128, 64], BF16)


def emit_na(C, B, qT_d, kTe_d, ve_d, bias_d, oT_d, in_ready=None, filler=None):
    P = C.P
    t_ones = P.op("pool", lambda g: g.memset(B.ones[:], 1.0))
    SKEW = 2
    NB = SKEW + 1
    b_free = [None, None]
    tmp_free = [None] * NB
    pT_free = [None] * NB
    rec_free = [None] * NB
    s_free = [None, None]
    ob_free = [None, None]
    o_free = [None, None]
    btok = {}
    out_toks = []
    units = [(h, ml) for h in range(8) for ml in range(16)]
    st = {}

    def load_bias(h):
        bs = h % 2
        btok[h] = P.dma("sp", B.bias[bs][:].rearrange("p a b -> p (a b)"), bias_d[h], B.b_sem[bs], waits=[b_free[bs]])

    def emit_S(i):
        h, ml = units[i]
        m, po = h // 2, (h % 2) * 64
        u2 = i % NB
        us = i % 2
        sc = us * 1024
        if ml == 0 and h + 1 < 8:
            load_bias(h + 1)
        lastS = None
        for kc in range(5):
            w = [ldm[m], s_free[us]] if kc == 0 else ()
            lastS = P.op("pe", lambda g, sc=sc, kc=kc, m=m, po=po, ml=ml: g.matmul(
                C.psum[:, sc + kc * 128: sc + (kc + 1) * 128],
                lhsT=B.kT[po:po + 64, m, (ml + kc) * 128:(ml + kc + 1) * 128],
                rhs=B.qT[po:po + 64, m, ml * 128:(ml + 1) * 128], start=True, stop=True),
                waits=w, sig=(kc == 4))
        bs = h % 2
        typ = {0: 0, 1: 1, 14: 3, 15: 4}.get(ml, 2)
        tt = P.op("dve", lambda g, sc=sc, u2=u2, bs=bs, typ=typ: g.tensor_tensor(
            out=B.tmp[u2][:], in0=C.psum[:, sc: sc + 640], in1=B.bias[bs][:, typ, :], op=ALU.add),
            waits=[lastS, btok[h], tmp_free[u2]])
        s_free[us] = tt
        b_free[bs] = tt
        st[("tt", i)] = tt

    def emit_S2(i):
        u2 = i % NB
        tt = st.pop(("tt", i))
        te = P.op("act", lambda g, u2=u2: g.activation(out=B.pT[u2][:], in_=B.tmp[u2][:], func=AF.Exp),
                  waits=[tt, pT_free[u2]])
        tmp_free[u2] = te
        st[i] = te

    def emit_rest(i):
        h, ml = units[i]
        u2 = i % NB
        us = i % 2
        oc = 2048 + us * 512
        os_ = h % 2
        te = st.pop(i)
        for kc in range(5):
            w = [te, t_ones, ob_free[us]] if kc == 0 else ()
            P.op("pe", lambda g, oc=oc, kc=kc, ml=ml, h=h, u2=u2: g.matmul(
                C.psum[0:64, oc: oc + 128], lhsT=B.v[:, ml + kc, h * 64:(h + 1) * 64],
                rhs=B.pT[u2][:, kc * 128:(kc + 1) * 128], start=(kc == 0), stop=(kc == 4)), waits=w, sig=False)
        lastO = None
        for kc in range(5):
            lastO = P.op("pe", lambda g, oc=oc, kc=kc, u2=u2: g.matmul(
                C.psum[0:64, oc + 128: oc + 256], lhsT=B.ones[:],
                rhs=B.pT[u2][:, kc * 128:(kc + 1) * 128], start=(kc == 0), stop=(kc == 4)), sig=(kc == 4))
        pT_free[u2] = lastO
        tl = P.op("act", lambda g, oc=oc, u2=u2: g.activation(out=B.rec[u2][:], in_=C.psum[0:64, oc + 128: oc + 256], func=AF.Ln),
                  waits=[lastO, rec_free[u2]])
        tr = P.op("act", lambda g, u2=u2: g.activation(out=B.rec[u2][:], in_=B.rec[u2][:], func=AF.Exp, scale=-1.0),
                  waits=[tl])
        st[("tail", i)] = (tr, lastO)

    def emit_rest2(i):
        h, ml = units[i]
        u2 = i % NB
        us = i % 2
        oc = 2048 + us * 512
        os_ = h % 2
        tr, lastO = st.pop(("tail", i))
        lastw = P.op("dve", lambda g, oc=oc, u2=u2, os_=os_, ml=ml: g.tensor_tensor(
            out=B.ost[os_][:, ml * 128:(ml + 1) * 128], in0=C.psum[0:64, oc: oc + 128],
            in1=B.rec[u2][:], op=ALU.mult), waits=[tr, lastO, o_free[os_]] if ml == 0 else [tr, lastO])
        rec_free[u2] = lastw
        ob_free[us] = lastw
        if ml == 15:
            o_free[os_] = P.dma("sp", oT_d[h * 64:(h + 1) * 64, :], B.ost[os_], B.o_sem[os_], waits=[lastw])
            out_toks.append(o_free[os_])

    ldm = {}
    for m in range(4):
        P.dma("sp", B.qT[:, m, :], qT_d[m * 128:(m + 1) * 128, :], B.ldm[m], waits=[in_ready])
        P.dma("sp", B.kT[:, m, :], kTe_d[m * 128:(m + 1) * 128, :], B.ldm[m])
        ldm[m] = P.dma("sp", B.v[:, :, m * 128:(m + 1) * 128],
                       ve_d[:, m * 128:(m + 1) * 128].rearrange("(i p) f -> p i f", p=128), B.ldm[m])
        if m == 0:
            load_bias(0)
    n = len(units)
    for i in range(n + SKEW):
        if i < n:
            emit_S(i)
        if i >= SKEW:
            emit_rest(i - SKEW)
        if i < n:
            emit_S2(i)
        if i >= SKEW:
            emit_rest2(i - SKEW)
        if filler is not None and i % 5 in (1, 3):
            next(filler, None)
    if filler is not None:
        for _ in filler:
            pass
    return out_toks


class FNBufs:
    def __init__(self, P):
        self.ld = P.dsem()
        self.e_sem = [P.dsem() for _ in range(2)]
        self.st = P.dsem()

    def carve(self, A):
        self.u = A.take([128, 64, 128], BF16)
        self.E = [A.take([128, 8, 256], BF16) for i in range(2)]
        self.Bsb = A.take([128, 256, 64], BF16)
        self.R = A.take([128, 512], BF16)
        self.CS = A.take([128, 128], BF16)
        self.G = [A.take([128, 4, 256], BF16) for i in range(2)]
        self.YT = A.take([128, 64, 128], BF16)


def emit_fnet_gen(C, B, u_d, E_d, R_d, CS_d, YT_d, in_ready=None, fixed_pair=None):
    P = C.P
    P.dma("sp", B.u[:].rearrange("p a b -> p (a b)"), u_d, B.ld, waits=[in_ready])
    P.dma("sp", B.R[:], R_d, B.ld)
    ld = P.dma("sp", B.CS[:], CS_d, B.ld)
    e_free = [None, None]
    unit = 0
    s1_act = s1_dve = None
    for ec in range(8):
        es = ec % 2
        etok = P.dma("sp", B.E[es][:].rearrange("p a b -> p (a b)"), E_d[:, ec * 2048:(ec + 1) * 2048], B.e_sem[es],
                     waits=[e_free[es]])
        for half in range(2):
            pb = fixed_pair if fixed_pair is not None else 2 * (unit % 4)
            unit += 1
            last = None
            for ci in range(4):
                cl = half * 4 + ci
                c = ec * 8 + cl
                w = [ld, etok, C.bank_free[pb], C.bank_free[pb + 1]] if ci == 0 else ()
                last = P.op("pe", lambda g, pb=pb, ci=ci, c=c, cl=cl, es=es: g.matmul(
                    C.psum[:, pb * 512 + ci * 256: pb * 512 + (ci + 1) * 256], lhsT=B.u[:, c, :], rhs=B.E[es][:, cl, :],
                    start=True, stop=True), waits=w, sig=(ci == 3))
            c0 = ec * 8 + half * 4
            if unit % 2 == 0:
                te = P.op("act", lambda g, pb=pb, c0=c0: g.activation(
                    out=B.Bsb[:, :, c0:c0 + 4].rearrange("p n c -> p c n"),
                    in_=C.bank(pb, 2).rearrange("p (c n) -> p c n", c=4), func=AF.Copy), waits=[last])
            else:
                te = P.op("dve", lambda g, pb=pb, c0=c0: g.tensor_copy(
                    out=B.Bsb[:, :, c0:c0 + 4].rearrange("p n c -> p c n"),
                    in_=C.bank(pb, 2).rearrange("p (c n) -> p c n", c=4)), waits=[last])
            C.bank_free[pb] = te
            C.bank_free[pb + 1] = te
            if unit % 2 == 0:
                s1_act = te
            else:
                s1_dve = te
            yield
        e_free[es] = last
    g_free = [None, None]
    y_tok = None
    import os
    nst = int(os.environ.get("FN_STAGES", "3"))
    for kq in range(16 if nst >= 2 else 0):
        gs = kq % 2
        pb = fixed_pair if fixed_pair is not None else 2 * (unit % 4)
        unit += 1
        last = None
        for ki in range(4):
            kp = kq * 4 + ki
            for part in range(2):
                w = [s1_act, s1_dve, C.bank_free[pb], C.bank_free[pb + 1]] if (ki == 0 and part == 0) else ()
                lhs = B.Bsb[:, part * 128 + 2 * kp: part * 128 + 2 * kp + 2, :].rearrange("p k c -> p (k c)")
                last = P.op("pe", lambda g, pb=pb, ki=ki, part=part, lhs=lhs: g.matmul(
                    C.psum[:, pb * 512 + ki * 256: pb * 512 + (ki + 1) * 256], lhsT=lhs,
                    rhs=B.R[:, part * 256:(part + 1) * 256], start=(part == 0), stop=(part == 1)),
                    waits=w, sig=(ki == 3 and part == 1))
        if kq % 2 == 0:
            te = P.op("act", lambda g, pb=pb, gs=gs: g.activation(out=B.G[gs][:].rearrange("p a b -> p (a b)"), in_=C.bank(pb, 2), func=AF.Copy),
                      waits=[last, g_free[gs]])
        else:
            te = P.op("dve", lambda g, pb=pb, gs=gs: g.tensor_copy(out=B.G[gs][:].rearrange("p a b -> p (a b)"), in_=C.bank(pb, 2)),
                      waits=[last, g_free[gs]])
        C.bank_free[pb] = te
        C.bank_free[pb + 1] = te
        if nst < 3:
            y_tok = te
            continue
        yield
        yb = fixed_pair if fixed_pair is not None else 2 * (unit % 4)
        unit += 1
        last3 = None
        for ki in range(4):
            for k2 in range(2):
                po = k2 * 64
                for part in range(2):
                    w = [te, C.bank_free[yb], C.bank_free[yb + 1]] if (ki == 0 and k2 == 0 and part == 0) else ()
                    last3 = P.op("pe", lambda g, yb=yb, gs=gs, ki=ki, k2=k2, po=po, part=part: g.matmul(
                        C.psum[:, (yb + k2) * 512 + ki * 64: (yb + k2) * 512 + (ki + 1) * 64],
                        lhsT=B.G[gs][po:po + 64, ki, part * 128:(part + 1) * 128],
                        rhs=B.CS[po:po + 64, part * 64:(part + 1) * 64], start=(part == 0), stop=(part == 1)),
                        waits=w, sig=(ki == 3 and k2 == 1 and part == 1))
        g_free[gs] = last3
        kr0 = kq * 8
        ty = P.op("dve", lambda g, yb=yb, kr0=kr0: g.tensor_copy(
            out=B.YT[:, :, kr0:kr0 + 8].rearrange("p x (a k) -> p x a k", k=2),
            in_=C.bank(yb, 2).rearrange("p (k a x) -> p k a x", k=2, a=8)[:, :, 0:4, :].rearrange("p k a x -> p x a k")),
            waits=[last3])
        C.bank_free[yb] = ty
        C.bank_free[yb + 1] = ty
        y_tok = ty
        yield
    out = P.dma("sp", YT_d, B.YT[:].rearrange("p a b -> p (a b)"), B.st, waits=[y_tok, s1_act, s1_dve])
    B.out_toks = [out]


def emit_fnet(C, B, *a, **k):
    for _ in emit_fnet_gen(C, B, *a, **k):
        pass
    return B.out_toks


class OutBufs:
    def __init__(self, P):
        self.ld = P.dsem()
        self.w_sems = [P.dsem() for _ in range(11)]
        self.in_sem = [P.dsem() for _ in range(2)]

    def carve(self, A):
        self.oT = [A.take([128, 4, 512], BF16) for i in range(2)]
        self.yfT = [A.take([128, 4, 512], BF16) for i in range(2)]
        self.wna = A.take([128, 4, D], BF16)
        self.wf = A.take([128, 4, D], BF16)
        self.wo = A.take([128, 8, D], BF16)
        self.wg = A.take([128, 8, 2 * D], BF16)
        self.gb = A.take([128, 2, 8], F32)
        self.mT = A.take([128, 8, 512], BF16)
        self.sga = [A.take([128, 512], F32) for i in range(2)]
        self.sgf = [A.take([128, 512], F32) for i in range(2)]


def emit_outproj(C, B, oT_d, yfT_d, w_in, w_na, w_f, w_o, gb_d, in_ready=None, h_ready=None, x_ready=None):
    P = C.P
    ld = P.dma("sp", B.gb.rearrange("p a b -> p (a b)"), gb_d, B.ld, waits=[in_ready])
    t_wna = P.dma("pool", B.wna, w_na.rearrange("(m p) d -> p m d", p=128), B.w_sems[0])
    t_wg = {}
    t_wf = None
    for q in range(4):
        t_wg[(0, q)] = P.dma("pool", B.wg[:, :, q * 256:(q + 1) * 256],
                             w_in[:, 2048 + q * 256:2048 + (q + 1) * 256].rearrange("(c p) f -> p c f", p=128), B.w_sems[1 + q])
        if q == 0:
            t_wf = P.dma("pool", B.wf, w_f.rearrange("(g p) d -> p g d", p=128), B.w_sems[9])
        t_wg[(1, q)] = P.dma("pool", B.wg[:, :, D + q * 256:D + (q + 1) * 256],
                             w_in[:, 3072 + q * 256:3072 + (q + 1) * 256].rearrange("(c p) f -> p c f", p=128), B.w_sems[5 + q])
    t_wo = P.dma("pool", B.wo, w_o.rearrange("(c p) d -> p c d", p=128), B.w_sems[10])
    in_free = [None, None]
    s_free = [None, None]
    m_free = None
    x_tok = None
    unit = 0

    def load_in(t):
        k = t % 2
        tsl = slice(t * 512, (t + 1) * 512)
        P.dma("sp", B.oT[k], oT_d[:, tsl].rearrange("(m p) t -> p m t", p=128), B.in_sem[k], waits=[in_ready, in_free[k]])
        return P.dma("sp", B.yfT[k], yfT_d[:, tsl].rearrange("(g p) t -> p g t", p=128), B.in_sem[k])

    in_tok = {0: load_in(0)}
    for t in range(4):
        ts = slice(t * 512, (t + 1) * 512)
        k = t % 2
        if t + 1 < 4:
            in_tok[t + 1] = load_in(t + 1)
        d3 = None
        for dch in range(8):
            pb = 4 * (unit % 2)
            k2 = unit % 2
            unit += 1
            dsl = slice(dch * 128, (dch + 1) * 128)
            w0 = [ld, t_wna, t_wf, t_wg[(0, dch // 2)], t_wg[(1, dch // 2)], in_tok[t], h_ready[t] if isinstance(h_ready, list) else h_ready] + [C.bank_free[pb + q] for q in range(4)]
            for mm in range(4):
                P.op("pe", lambda g, pb=pb, mm=mm, dsl=dsl, k=k: g.matmul(
                    C.bank(pb), lhsT=B.wna[:, mm, dsl], rhs=B.oT[k][:, mm, :], start=(mm == 0), stop=(mm == 3)),
                    waits=w0 if mm == 0 else (), sig=False)
            for d in range(8):
                tB = P.op("pe", lambda g, pb=pb, d=d, dsl=dsl, ts=ts: g.matmul(
                    C.bank(pb + 1), lhsT=B.wg[:, d, dsl], rhs=C.hT[:, d, ts], start=(d == 0), stop=(d == 7)), sig=(d == 7))
            for gg in range(4):
                P.op("pe", lambda g, pb=pb, gg=gg, dsl=dsl, k=k: g.matmul(
                    C.bank(pb + 2), lhsT=B.wf[:, gg, dsl], rhs=B.yfT[k][:, gg, :], start=(gg == 0), stop=(gg == 3)), sig=False)
            for d in range(8):
                tD = P.op("pe", lambda g, pb=pb, d=d, dch=dch, ts=ts: g.matmul(
                    C.bank(pb + 3), lhsT=B.wg[:, d, D + dch * 128:D + (dch + 1) * 128], rhs=C.hT[:, d, ts],
                    start=(d == 0), stop=(d == 7)), sig=(d == 7))
            sa = P.op("act", lambda g, pb=pb, k2=k2, dch=dch: g.activation(
                out=B.sga[k2], in_=C.bank(pb + 1), func=AF.Sigmoid, bias=B.gb[:, 0, dch:dch + 1], scale=1.0),
                waits=[tB, s_free[k2]])
            sf = P.op("act", lambda g, pb=pb, k2=k2, dch=dch: g.activation(
                out=B.sgf[k2], in_=C.bank(pb + 3), func=AF.Sigmoid, bias=B.gb[:, 1, dch:dch + 1], scale=1.0),
                waits=[tD])
            d1 = P.op("dve", lambda g, pb=pb, k2=k2: g.tensor_tensor(out=B.sga[k2], in0=C.bank(pb), in1=B.sga[k2], op=ALU.mult),
                      waits=[sa, tD])
            d2 = P.op("dve", lambda g, pb=pb, k2=k2: g.tensor_tensor(out=B.sgf[k2], in0=C.bank(pb + 2), in1=B.sgf[k2], op=ALU.mult),
                      waits=[sf, d1])
            for q in range(4):
                C.bank_free[pb + q] = d2
            d3 = P.op("pool", lambda g, k2=k2, dch=dch: g.tensor_tensor(out=B.mT[:, dch, :], in0=B.sga[k2], in1=B.sgf[k2], op=ALU.add),
                      waits=[d2, m_free] if dch == 0 else [d2])
            s_free[k2] = d3
        in_free[k] = tD
        C.h_free = tD
        for dq in range(2):
            pb = 4 * (unit % 2)
            unit += 1
            last = None
            for di in range(4):
                do = dq * 4 + di
                for dch in range(8):
                    w = [d3, t_wo] + [C.bank_free[pb + q] for q in range(4)] if (di == 0 and dch == 0) else ()
                    last = P.op("pe", lambda g, pb=pb, di=di, dch=dch, do=do: g.matmul(
                        C.bank(pb + di), lhsT=B.wo[:, dch, do * 128:(do + 1) * 128], rhs=B.mT[:, dch, :],
                        start=(dch == 0), stop=(dch == 7)), waits=w, sig=(di == 3 and dch == 7))
            for di in range(4):
                do = dq * 4 + di
                x_tok = P.op("dve", lambda g, pb=pb, di=di, do=do, ts=ts: g.tensor_tensor(
                    out=C.xT[:, do, ts], in0=C.bank(pb + di), in1=C.xT[:, do, ts], op=ALU.add),
                    waits=[last, x_ready[t] if isinstance(x_ready, list) else x_ready])
                C.bank_free[pb + di] = x_tok
            m_free = last
    return x_tok


def emit_final_norm(C, gi, out_d):
    P = C.P
    A = C.arena
    st = [A.take([128, 8, 512], F32) for _ in range(2)]
    sem = [C.fin_sem0, C.fin_sem1]
    free = [None, None]
    toks = []

    def out_fn(c, t, ts, r, t2):
        k = t % 2
        tk = P.op("dve", lambda g, c=c, ts=ts, r=r, k=k: g.scalar_tensor_tensor(
            out=st[k][:, c, :], in0=C.xT[:, c, ts], scalar=C.gains[:, gi, c:c + 1], in1=r[:],
            op0=ALU.mult, op1=ALU.mult), waits=[t2, free[k], C.gains_tok] if c == 0 else (), sig=(c == 7))
        if c == 7:
            free[k] = P.dma("sp", out_d[:, ts].rearrange("(c p) t -> p c t", p=128), st[k], sem[k], waits=[tk])
            toks.append(free[k])
        return tk

    emit_norm(C, gi, None, out_fn=out_fn)
    return toks


def _load_xh(P, C, x_d, h_d, g_d):
    C.gains_tok = P.dma("sp", C.gains[:].rearrange("p a b -> p (a b)"), g_d, P.dsem())
    h_toks = None
    if h_d is not None:
        h_toks = []
        for t in range(4):
            ts = slice(t * 512, (t + 1) * 512)
            h_toks.append(P.dma("sp", C.hT[:, :, ts], h_d[:, ts].rearrange("(c p) t -> p c t", p=128), P.dsem()))
    x_toks = []
    for t in range(4):
        ts = slice(t * 512, (t + 1) * 512)
        x_toks.append(P.dma("sp", C.xT[:, :, ts], x_d[:, ts].rearrange("(c p) t -> p c t", p=128), P.dsem()))
    return x_toks, h_toks


def _dram(nc, name, shape, dt, kind):
    return nc.dram_tensor(name, list(shape), dt, kind=kind).ap()


def _stage_front(nc, P, C, gi0, x_ready, first):
    toks = []
    w1_in = _dram(nc, "w1_in", [D, 2 * DFF], F32, "ExternalInput")
    w1_out = _dram(nc, "w1_out", [DFF, D], F32, "ExternalInput")
    wmix = _dram(nc, "wmixb", [D, 4096], F32, "ExternalInput")
    qT = _dram(nc, "qT", [512, NTOK], BF16, "ExternalOutput")
    kT = _dram(nc, "kT", [512, NTOK], BF16, "ExternalOutput")
    v = _dram(nc, "v", [NTOK, 512], BF16, "ExternalOutput")
    uf = _dram(nc, "uf", [NTOK, 512], BF16, "ExternalOutput")
    xo = _dram(nc, "xT_out", [D, NTOK], F32, "ExternalOutput")
    ho = _dram(nc, "hT_out", [D, NTOK], BF16, "ExternalOutput")
    xs_sem = P.dsem()
    hs_sem = P.dsem()
    if not first:
        P.barrier()
    C.arena.reset()
    h = emit_norm(C, gi0, x_ready)
    C.ffn.carve(C.arena)

    def save_x(dch, tok):
        toks.append(P.dma("sp", xo[dch * 128:(dch + 1) * 128, :], C.xT[:, dch, :], xs_sem, waits=[tok]))

    emit_ffn(C, C.ffn, w1_in, w1_out, h, on_x_final=save_x)
    P.barrier()
    C.arena.reset()

    def save_h(t, ts, tok):
        toks.append(P.dma("sp", ho[:, ts].rearrange("(c p) t -> p c t", p=128), C.hT[:, :, ts], hs_sem, waits=[tok]))

    h = emit_norm(C, gi0 + 1, None, on_chunk=save_h)
    C.inp.carve(C.arena)
    toks += emit_inproj(C, C.inp, wmix, h, qT, kT, v, uf)
    return toks


def build_A():
    nc = bass.Bass("TRN2", target_bir_lowering=False)
    x_d = _dram(nc, "xT_in", [D, NTOK], F32, "ExternalInput")
    g_d = _dram(nc, "gains", [128, 56], F32, "ExternalInput")
    P = Prog(nc)
    C = Ctx(P)
    C.ffn = FFNBufs(P)
    C.inp = InprojBufs(P)
    x_toks, _ = _load_xh(P, C, x_d, None, g_d)
    toks = _stage_front(nc, P, C, 0, x_toks, True)
    P.wait("sp", toks)
    P.emit()
    return nc


def build_NF():
    nc = bass.Bass("TRN2", target_bir_lowering=False)
    qT = _dram(nc, "qT", [512, NTOK], BF16, "ExternalInput")
    kTe = _dram(nc, "kTe", [512, 2560], BF16, "ExternalInput")
    ve = _dram(nc, "ve", [2560, 512], BF16, "ExternalInput")
    bias = _dram(nc, "bias", [8, 128, 3200], F32, "ExternalInput")
    u = _dram(nc, "u", [128, 8192], BF16, "ExternalInput")
    E = _dram(nc, "E", [128, 16384], BF16, "ExternalInput")
    R = _dram(nc, "R", [128, 512], BF16, "ExternalInput")
    CS = _dram(nc, "CS", [128, 128], BF16, "ExternalInput")
    oT = _dram(nc, "oT", [512, NTOK], BF16, "ExternalOutput")
    YT = _dram(nc, "YT", [128, 8192], BF16, "ExternalOutput")
    P = Prog(nc)
    C = Ctx(P, with_x=False, arena=92000)
    na = NABufs(P)
    fn = FNBufs(P)
    import os
    which = os.environ.get("NF_ONLY", "both")
    C.arena.reset()
    if which == "both":
        na.carve(C.arena)
        fn.carve(C.arena)
        gen = emit_fnet_gen(C, fn, u, E, R, CS, YT, fixed_pair=6)
        toks = emit_na(C, na, qT, kTe, ve, bias, oT, filler=gen)
        toks = toks + fn.out_toks
    elif which == "na":
        na.carve(C.arena)
        toks = emit_na(C, na, qT, kTe, ve, bias, oT)
    else:
        fn.carve(C.arena)
        toks = emit_fnet(C, fn, u, E, R, CS, YT)
    P.wait("sp", toks)
    P.emit()
    return nc


def build_E(final, gi0):
    nc = bass.Bass("TRN2", target_bir_lowering=False)
    x_d = _dram(nc, "xT_in", [D, NTOK], F32, "ExternalInput")
    h_d = _dram(nc, "hT_in", [D, NTOK], BF16, "ExternalInput")
    g_d = _dram(nc, "gains", [128, 56], F32, "ExternalInput")
    oT = _dram(nc, "oT", [512, NTOK], BF16, "ExternalInput")
    yfT = _dram(nc, "yfT", [512, NTOK], BF16, "ExternalInput")
    wmixa = _dram(nc, "wmixa", [D, 4096], F32, "ExternalInput")
    w_na = _dram(nc, "w_na", [512, D], F32, "ExternalInput")
    w_f = _dram(nc, "w_f", [512, D], F32, "ExternalInput")
    w_o = _dram(nc, "w_o", [D, D], F32, "ExternalInput")
    gb = _dram(nc, "gb", [128, 16], F32, "ExternalInput")
    w2_in = _dram(nc, "w2_in", [D, 2 * DFF], F32, "ExternalInput")
    w2_out = _dram(nc, "w2_out", [DFF, D], F32, "ExternalInput")
    P = Prog(nc)
    C = Ctx(P)
    C.ffn = FFNBufs(P)
    C.inp = InprojBufs(P)
    ob = OutBufs(P)
    x_toks, h_toks = _load_xh(P, C, x_d, h_d, g_d)
    C.arena.reset()
    ob.carve(C.arena)
    emit_outproj(C, ob, oT, yfT, wmixa, w_na, w_f, w_o, gb, h_ready=h_toks, x_ready=x_toks)
    P.barrier()
    C.arena.reset()
    h = emit_norm(C, gi0 + 2, None)
    C.ffn.carve(C.arena)
    emit_ffn(C, C.ffn, w2_in, w2_out, h)
    if final:
        out = _dram(nc, "outT", [D, NTOK], F32, "ExternalOutput")
        P.barrier()
        C.arena.reset()
        toks = emit_final_norm(C, 6, out)
    else:
        toks = _stage_front(nc, P, C, gi0 + 3, None, False)
    P.wait("sp", toks)
    P.emit()
    return nc


import ml_dtypes
BF = ml_dtypes.bfloat16
NEG = -30000.0


def _tables():
    r = np.arange(128, dtype=np.int64)[:, None, None]
    c = np.arange(64, dtype=np.int64)[None, :, None]
    kr = np.arange(128, dtype=np.int64)[None, None, :]
    th = 2 * np.pi * ((kr * (64 * r + c)) % 8192) / 8192.0
    E = np.concatenate([np.cos(th), -np.sin(th)], axis=2) * 2.0 ** -10
    ch = np.arange(128, dtype=np.int64)
    th2 = 2 * np.pi * ((ch[:, None] * ch[None, :]) % 128) / 128.0
    Cm, Sm = np.cos(th2), np.sin(th2)
    R = np.concatenate([Cm, -Sm, Sm, Cm], axis=1)
    cc = np.arange(64, dtype=np.int64)
    th3 = 2 * np.pi * ((cc[:, None] * cc[None, :]) % 64) / 64.0
    CS64 = np.concatenate([np.cos(th3), np.sin(th3)], axis=1)
    CS = np.concatenate([CS64, CS64], axis=0)
    return (np.ascontiguousarray(E.reshape(128, 64 * 256)).astype(BF), R.astype(BF), CS.astype(BF))


def _ext_chunk(jj, e):
    gc = jj * 16 + e - 2
    if gc < 0:
        return 3 if e == 0 else None
    if gc > 63:
        return 60 if e == 19 else None
    return gc


def _bias_tiles(rpb, jj):
    out = np.full((8, 128, 5, 640), NEG, np.float32)
    qi = np.arange(2)[None, None, :, None]
    qc = np.arange(64)[None, None, None, :]
    ki = np.arange(2)[:, None, None, None]
    kcl = np.arange(64)[None, :, None, None]
    ws = np.clip(qc - 8, 0, 48)
    col_ok = (kcl >= ws) & (kcl < ws + 16)
    dc = np.clip(kcl - qc, -15, 15) + 15
    for typ, ml in ((0, 0), (1, 1), (2, 5), (3, 14), (4, 15)):
        m = jj * 16 + ml
        r = 2 * m + qi
        rs = np.clip(r - 4, 0, 120)
        for kc in range(5):
            gc = _ext_chunk(jj, ml + kc)
            if gc is None:
                continue
            krow = 2 * gc + ki
            ok = (krow >= rs) & (krow < rs + 8) & col_ok
            dr = np.clip(krow - r + 7, 0, 14)
            drb = np.broadcast_to(dr, ok.shape)
            dcb = np.broadcast_to(dc, ok.shape)
            vals = rpb[:, drb, dcb]
            tile = np.where(ok[None], vals, np.float32(NEG)).reshape(8, 128, 128)
            out[:, :, typ, kc * 128:(kc + 1) * 128] = tile
    return np.ascontiguousarray(out.reshape(8, 128, 3200))


def _gain_layout(g):
    n = g.shape[0]
    return np.ascontiguousarray(g.reshape(n, 8, 128).transpose(2, 0, 1).reshape(128, n * 8)).astype(np.float32)


def _run(nc, in_maps):
    res = run_bass_kernel_spmd(nc, in_maps, core_ids=list(range(8)))
    return res.results


def _exchange_front(outs, rpb_l, tabs):
    E, R, CS = tabs
    ims = []
    for core in range(8):
        b, jj = divmod(core, 4)
        kT_all = np.concatenate([outs[b * 4 + q]["kT"] for q in range(4)], axis=1)
        v_all = np.concatenate([outs[b * 4 + q]["v"] for q in range(4)], axis=0)
        uf_all = np.concatenate([outs[b * 4 + q]["uf"] for q in range(4)], axis=0)
        kTe = np.zeros((512, 2560), BF)
        ve = np.zeros((2560, 512), BF)
        for e in range(20):
            gc = _ext_chunk(jj, e)
            if gc is None:
                continue
            kTe[:, e * 128:(e + 1) * 128] = kT_all[:, gc * 128:(gc + 1) * 128]
            ve[e * 128:(e + 1) * 128, :] = v_all[gc * 128:(gc + 1) * 128, :]
        g = jj
        u = np.ascontiguousarray(uf_all[:, g * 128:(g + 1) * 128]).reshape(128, 64 * 128)
        ims.append({"qT": outs[core]["qT"], "kTe": kTe, "ve": ve, "bias": _bias_tiles(rpb_l, jj),
                    "u": u, "E": E, "R": R, "CS": CS})
    return ims


def kernel(x, ffn1_norm, ffn1_w_in, ffn1_w_out, mix_norm, mix_w_in, mix_gate_bias, na_rpb, na_w_out,
           f_w_out, mix_w_o, ffn2_norm, ffn2_w_in, ffn2_w_out, final_norm):
    f32 = lambda a: np.ascontiguousarray(np.asarray(a, dtype=np.float32))
    x = f32(x)
    gains = _gain_layout(np.stack([f32(ffn1_norm)[0], f32(mix_norm)[0], f32(ffn2_norm)[0],
                                   f32(ffn1_norm)[1], f32(mix_norm)[1], f32(ffn2_norm)[1], f32(final_norm)]))
    tabs = _tables()
    rpb = f32(na_rpb)
    ims = []
    for core in range(8):
        b, jj = divmod(core, 4)
        ims.append({"xT_in": np.ascontiguousarray(x[b, jj * NTOK:(jj + 1) * NTOK, :].T), "gains": gains,
                    "w1_in": f32(ffn1_w_in[0]), "w1_out": f32(ffn1_w_out[0]), "wmixb": f32(mix_w_in[0])})
    front = _run(build_A(), ims)
    out = None
    for l in range(L):
        nf = _run(build_NF(), _exchange_front(front, rpb[l], tabs))
        ims = []
        gb = _gain_layout(f32(mix_gate_bias[l]))
        for core in range(8):
            b, jj = divmod(core, 4)
            yfT = np.concatenate([nf[b * 4 + g]["YT"][:, jj * NTOK:(jj + 1) * NTOK] for g in range(4)], axis=0)
            im = {"xT_in": front[core]["xT_out"], "hT_in": front[core]["hT_out"], "gains": gains,
                  "oT": nf[core]["oT"], "yfT": np.ascontiguousarray(yfT), "wmixa": f32(mix_w_in[l]),
                  "w_na": f32(na_w_out[l]), "w_f": f32(f_w_out[l]), "w_o": f32(mix_w_o[l]), "gb": gb,
                  "w2_in": f32(ffn2_w_in[l]), "w2_out": f32(ffn2_w_out[l])}
            if l + 1 < L:
                im.update({"w1_in": f32(ffn1_w_in[l + 1]), "w1_out": f32(ffn1_w_out[l + 1]),
                           "wmixb": f32(mix_w_in[l + 1])})
            ims.append(im)
        res = _run(build_E(final=(l + 1 == L), gi0=3 * l), ims)
        if l + 1 < L:
            front = res
        else:
            out = np.empty((2, 8192, D), np.float32)
            for core in range(8):
                b, jj = divmod(core, 4)
                out[b, jj * NTOK:(jj + 1) * NTOK, :] = res[core]["outT"].T
    return out
```
